# Optimizing a Trainium2 kernel written in Bass

```python
import jax, jax.numpy as jnp
from jax import lax
import numpy as np

D_MODEL = 2048
BATCH = 1
SEQ = 16384
DEPTH = 2
DEC_BATCH = 16
DEC_SEQ = 64
PAST_LEN = 4096

CHUNK = 64
N_HEADS = 16
HEAD_DIM = 128
ROPE_DIM = 64
Q_LORA = 512
KV_LORA = 256
POOL_GROUPS = 4
POOL_GROUP_DIM = 128
POOL_WIDTH = POOL_GROUPS * POOL_GROUP_DIM
POOL_WINDOWS = (2, 4, 8, 16)
POOL_PAD = 15
POOL_OUT = D_MODEL // POOL_GROUPS
N_MEM = 256
MEM_HEADS = 4
MEM_QK_DIM = 128
MEM_V_DIM = D_MODEL // MEM_HEADS
D_FF = 5632
N_BRANCH = 3
Q_BLOCK = 128
ROPE_THETA = 10000.0
EPS = 1e-6
NEG_INF = -1e30
IN_SPLITS = (Q_LORA, Q_LORA + KV_LORA, Q_LORA + KV_LORA + ROPE_DIM,
             Q_LORA + KV_LORA + ROPE_DIM + POOL_WIDTH)
IN_WIDTH = Q_LORA + KV_LORA + ROPE_DIM + POOL_WIDTH + MEM_HEADS * MEM_QK_DIM

kernel_name = "hybrid_streaming_mla_pool_mem_step"


def rmsnorm(x, g):
    x32 = x.astype(jnp.float32)
    y = x32 * lax.rsqrt(jnp.mean(x32 * x32, axis=-1, keepdims=True) + EPS)
    return (y * g.astype(jnp.float32)).astype(x.dtype)


def rope(x, pos):
    half = ROPE_DIM // 2
    inv = jnp.power(ROPE_THETA, -jnp.arange(half, dtype=jnp.float32) / half)
    ang = pos.astype(jnp.float32)[:, None] * inv[None, :]
    shape = (1, pos.shape[0]) + (1,) * (x.ndim - 3) + (half,)
    cos = jnp.cos(ang).reshape(shape)
    sin = jnp.sin(ang).reshape(shape)
    x32 = x.astype(jnp.float32)
    x1, x2 = x32[..., :half], x32[..., half:]
    return jnp.concatenate([x1 * cos - x2 * sin, x1 * sin + x2 * cos], axis=-1).astype(x.dtype)


def swiglu(x, g, wg, wu, wd):
    h = rmsnorm(x, g)
    return (jax.nn.silu(h @ wg) * (h @ wu)) @ wd


def mla_attention(q_nope, q_rope, k_nope, k_rope, v, q_pos, k_pos):
    scale = (HEAD_DIM + ROPE_DIM) ** -0.5
    k_chunk = k_pos // CHUNK

    def block(qn, qr, qp):
        s = (jnp.einsum('bqhd,bkhd->bhqk', qn, k_nope)
             + jnp.einsum('bqhd,bkd->bhqk', qr, k_rope)).astype(jnp.float32) * scale
        mask = k_chunk[None, :] <= (qp // CHUNK)[:, None]
        s = jnp.where(mask[None, None], s, NEG_INF)
        p = jax.nn.softmax(s, axis=-1).astype(v.dtype)
        return jnp.einsum('bhqk,bkhd->bqhd', p, v)

    B, T = q_nope.shape[0], q_nope.shape[1]
    if T <= Q_BLOCK:
        return block(q_nope, q_rope, q_pos)
    nb = T // Q_BLOCK
    qn_b = jnp.moveaxis(q_nope.reshape(B, nb, Q_BLOCK, N_HEADS, HEAD_DIM), 1, 0)
    qr_b = jnp.moveaxis(q_rope.reshape(B, nb, Q_BLOCK, N_HEADS, ROPE_DIM), 1, 0)
    qp_b = q_pos.reshape(nb, Q_BLOCK)
    out = lax.map(lambda a: block(a[0], a[1], a[2]), (qn_b, qr_b, qp_b))
    return jnp.moveaxis(out, 0, 1).reshape(B, T, N_HEADS, HEAD_DIM)


def pool_mix(p_ext, pos, pool_w, pool_scale):
    B, L, _ = p_ext.shape
    T = L - POOL_PAD
    cs = jnp.cumsum(p_ext.astype(jnp.float32), axis=1)
    cs = jnp.concatenate([jnp.zeros((B, 1, POOL_WIDTH), jnp.float32), cs], axis=1)
    cur = p_ext[:, POOL_PAD:].astype(jnp.float32)
    outs = []
    for g, w in enumerate(POOL_WINDOWS):
        lo, hi = g * POOL_GROUP_DIM, (g + 1) * POOL_GROUP_DIM
        upper = cs[:, POOL_PAD + 1:, lo:hi]
        lower = cs[:, POOL_PAD + 1 - w:POOL_PAD + 1 - w + T, lo:hi]
        cnt = jnp.minimum(pos + 1, w).astype(jnp.float32)[None, :, None]
        outs.append((upper - lower) / cnt - cur[..., lo:hi])
    pooled = jnp.stack(outs, axis=2).astype(p_ext.dtype)
    y = jnp.einsum('btgc,gcd->btgd', pooled, pool_w).reshape(B, T, D_MODEL)
    return y * pool_scale


def mem_kv(mem, mem_norm, w_mk, w_mv, mk_norm):
    B, M, _ = mem.shape
    m = rmsnorm(mem, mem_norm)
    k = rmsnorm((m @ w_mk).reshape(B, M, MEM_HEADS, MEM_QK_DIM), mk_norm)
    v = (m @ w_mv).reshape(B, M, MEM_HEADS, MEM_V_DIM)
    return k, v


def layer(x, pos, ckv_past, kr_past, pool_buf, mem_k, mem_v,
          ffn1_norm, ffn1_wg, ffn1_wu, ffn1_wd, mix_norm, w_in, q_lat_norm, kv_lat_norm,
          w_uq, w_uqr, w_uk, w_uv, q_norm, k_norm, qr_norm, kr_norm, pool_w, pool_scale,
          mq_norm, w_gate, b_gate, w_out, ffn2_norm, ffn2_wg, ffn2_wu, ffn2_wd):
    B, T, _ = x.shape
    h = x + 0.5 * swiglu(x, ffn1_norm, ffn1_wg, ffn1_wu, ffn1_wd)
    u = rmsnorm(h, mix_norm)
    q_lat, kv_lat, kr, pin, mq = jnp.split(u @ w_in, IN_SPLITS, axis=-1)

    c_q = rmsnorm(q_lat, q_lat_norm)
    q_nope = rmsnorm((c_q @ w_uq).reshape(B, T, N_HEADS, HEAD_DIM), q_norm)
    q_rope = rope(rmsnorm((c_q @ w_uqr).reshape(B, T, N_HEADS, ROPE_DIM), qr_norm), pos)
    c_kv = rmsnorm(kv_lat, kv_lat_norm)
    k_r = rope(rmsnorm(kr, kr_norm), pos)
    if ckv_past is None:
        ckv_all, kr_all = c_kv, k_r
    else:
        ckv_all = jnp.concatenate([ckv_past, c_kv], axis=1)
        kr_all = jnp.concatenate([kr_past, k_r], axis=1)
    S = ckv_all.shape[1]
    k_nope = rmsnorm((ckv_all @ w_uk).reshape(B, S, N_HEADS, HEAD_DIM), k_norm)
    v = (ckv_all @ w_uv).reshape(B, S, N_HEADS, HEAD_DIM)
    attn = mla_attention(q_nope, q_rope, k_nope, kr_all, v, pos, jnp.arange(S)).reshape(B, T, D_MODEL)

    p_ext = jnp.concatenate([pool_buf.astype(pin.dtype), pin], axis=1)
    pool_out = pool_mix(p_ext, pos, pool_w, pool_scale)
    new_pool = p_ext[:, -POOL_PAD:]

    q_m = rmsnorm(mq.reshape(B, T, MEM_HEADS, MEM_QK_DIM), mq_norm)
    s_m = jnp.einsum('bqhd,bmhd->bhqm', q_m, mem_k).astype(jnp.float32) * MEM_QK_DIM ** -0.5
    p_m = jax.nn.softmax(s_m, axis=-1).astype(mem_v.dtype)
    mem_out = jnp.einsum('bhqm,bmhd->bqhd', p_m, mem_v).reshape(B, T, D_MODEL)

    gates = jax.nn.sigmoid((u @ w_gate).astype(jnp.float32) + b_gate.astype(jnp.float32))
    gates = gates.reshape(B, T, N_BRANCH, D_MODEL)
    merged = (gates[:, :, 0] * attn.astype(jnp.float32)
              + gates[:, :, 1] * pool_out.astype(jnp.float32)
              + gates[:, :, 2] * mem_out.astype(jnp.float32)).astype(x.dtype)
    h = h + merged @ w_out
    out = h + 0.5 * swiglu(h, ffn2_norm, ffn2_wg, ffn2_wu, ffn2_wd)
    return out, c_kv, k_r, new_pool


def setup_inputs(seed: int = 0) -> dict:
    key = jax.random.key(seed)
    ks = iter(jax.random.split(key, 48))

    def nrm(shape, scale):
        return jax.random.normal(next(ks), shape, jnp.float32) * scale

    def gain(dim):
        return 1.0 + 0.02 * jax.random.normal(next(ks), (DEPTH, dim), jnp.float32)

    d = D_MODEL
    inp = {}
    inp["x_prompt"] = nrm((BATCH, SEQ, d), 1.0)
    inp["x_sample"] = nrm((DEC_BATCH, DEC_SEQ, d), 1.0)
    inp["mem_prompt"] = nrm((BATCH, N_MEM, d), 1.0)
    inp["cache_ckv"] = nrm((DEPTH, DEC_BATCH, PAST_LEN, KV_LORA), 1.0)
    inp["cache_krope"] = nrm((DEPTH, DEC_BATCH, PAST_LEN, ROPE_DIM), 1.0)
    inp["state_pool"] = nrm((DEPTH, DEC_BATCH, POOL_PAD, POOL_WIDTH), 1.0)
    inp["cache_mem_k"] = nrm((DEPTH, DEC_BATCH, N_MEM, MEM_HEADS, MEM_QK_DIM), 1.0)
    inp["cache_mem_v"] = nrm((DEPTH, DEC_BATCH, N_MEM, MEM_HEADS, MEM_V_DIM), 1.0)
    inp["ffn1_norm"] = gain(d)
    inp["ffn1_wg"] = nrm((DEPTH, d, D_FF), d ** -0.5)
    inp["ffn1_wu"] = nrm((DEPTH, d, D_FF), d ** -0.5)
    inp["ffn1_wd"] = nrm((DEPTH, D_FF, d), D_FF ** -0.5)
    inp["mix_norm"] = gain(d)
    inp["w_in"] = nrm((DEPTH, d, IN_WIDTH), d ** -0.5)
    inp["q_lat_norm"] = gain(Q_LORA)
    inp["kv_lat_norm"] = gain(KV_LORA)
    inp["w_uq"] = nrm((DEPTH, Q_LORA, N_HEADS * HEAD_DIM), Q_LORA ** -0.5)
    inp["w_uqr"] = nrm((DEPTH, Q_LORA, N_HEADS * ROPE_DIM), Q_LORA ** -0.5)
    inp["w_uk"] = nrm((DEPTH, KV_LORA, N_HEADS * HEAD_DIM), KV_LORA ** -0.5)
    inp["w_uv"] = nrm((DEPTH, KV_LORA, N_HEADS * HEAD_DIM), KV_LORA ** -0.5)
    inp["q_norm"] = gain(HEAD_DIM)
    inp["k_norm"] = gain(HEAD_DIM)
    inp["qr_norm"] = gain(ROPE_DIM)
    inp["kr_norm"] = gain(ROPE_DIM)
    inp["pool_w"] = nrm((DEPTH, POOL_GROUPS, POOL_GROUP_DIM, POOL_OUT), POOL_GROUP_DIM ** -0.5)
    inp["pool_scale"] = 1.0 + 0.1 * nrm((DEPTH, d), 1.0)
    inp["mem_norm"] = gain(d)
    inp["w_mk"] = nrm((DEPTH, d, MEM_HEADS * MEM_QK_DIM), d ** -0.5)
    inp["w_mv"] = nrm((DEPTH, d, MEM_HEADS * MEM_V_DIM), d ** -0.5)
    inp["mq_norm"] = gain(MEM_QK_DIM)
    inp["mk_norm"] = gain(MEM_QK_DIM)
    inp["w_gate"] = nrm((DEPTH, d, N_BRANCH * d), d ** -0.5)
    inp["b_gate"] = nrm((DEPTH, N_BRANCH * d), 0.1)
    inp["w_out"] = nrm((DEPTH, d, d), 0.5 * d ** -0.5)
    inp["ffn2_norm"] = gain(d)
    inp["ffn2_wg"] = nrm((DEPTH, d, D_FF), d ** -0.5)
    inp["ffn2_wu"] = nrm((DEPTH, d, D_FF), d ** -0.5)
    inp["ffn2_wd"] = nrm((DEPTH, D_FF, d), D_FF ** -0.5)
    return inp


def reference(x_prompt, x_sample, mem_prompt, cache_ckv, cache_krope, state_pool, cache_mem_k, cache_mem_v,
              ffn1_norm, ffn1_wg, ffn1_wu, ffn1_wd, mix_norm, w_in, q_lat_norm, kv_lat_norm,
              w_uq, w_uqr, w_uk, w_uv, q_norm, k_norm, qr_norm, kr_norm, pool_w, pool_scale,
              mem_norm, w_mk, w_mv, mq_norm, mk_norm, w_gate, b_gate, w_out,
              ffn2_norm, ffn2_wg, ffn2_wu, ffn2_wd):
    layer_params = (ffn1_norm, ffn1_wg, ffn1_wu, ffn1_wd, mix_norm, w_in, q_lat_norm, kv_lat_norm,
                    w_uq, w_uqr, w_uk, w_uv, q_norm, k_norm, qr_norm, kr_norm, pool_w, pool_scale,
                    mq_norm, w_gate, b_gate, w_out, ffn2_norm, ffn2_wg, ffn2_wu, ffn2_wd)
    Bp, Tp, _ = x_prompt.shape
    Ts = x_sample.shape[1]
    past = cache_ckv.shape[2]
    pos_p = jnp.arange(Tp)
    pos_s = past + jnp.arange(Ts)

    xp, xs = x_prompt, x_sample
    ckv_p, kr_p, pool_p, mk_p, mv_p = [], [], [], [], []
    ckv_s, kr_s, pool_s = [], [], []
    for l in range(DEPTH):
        lp = [w[l] for w in layer_params]
        mk, mv = mem_kv(mem_prompt, mem_norm[l], w_mk[l], w_mv[l], mk_norm[l])
        zero_buf = jnp.zeros((Bp, POOL_PAD, POOL_WIDTH), xp.dtype)
        xp, c1, r1, b1 = layer(xp, pos_p, None, None, zero_buf, mk, mv, *lp)
        ckv_p.append(c1); kr_p.append(r1); pool_p.append(b1); mk_p.append(mk); mv_p.append(mv)
        xs, c2, r2, b2 = layer(xs, pos_s, cache_ckv[l], cache_krope[l], state_pool[l],
                               cache_mem_k[l], cache_mem_v[l], *lp)
        ckv_s.append(c2); kr_s.append(r2); pool_s.append(b2)

    return (xp, xs,
            jnp.stack(ckv_p), jnp.stack(kr_p), jnp.stack(pool_p), jnp.stack(mk_p), jnp.stack(mv_p),
            jnp.stack(ckv_s), jnp.stack(kr_s), jnp.stack(pool_s))
```

```python
import numpy as np
import ml_dtypes
from contextlib import ExitStack
import concourse.bass as bass
import concourse.mybir as mybir
from concourse.bass_utils import run_bass_kernel_spmd

F32 = mybir.dt.float32
BF16 = mybir.dt.bfloat16
AF = mybir.ActivationFunctionType
ALU = mybir.AluOpType

NCORES = 8
D = 2048
DFF = 5632
NFF = DFF // 128
SEQ = 16384
PAST = 4096
DEC_B = 16
DEC_T = 64
NTOK = 2176
TILES = [(0, 512), (512, 512), (1024, 512), (1536, 512), (2048, 128)]
EPS = 1e-6
IN_W = 1856


class Tile:
    __slots__ = ("name", "w", "r")

    def __init__(self, name=""):
        self.name = name
        self.w = None
        self.r = []


class DSem:
    def __init__(self, sem):
        self.sem = sem
        self.count = 0


class Op:
    __slots__ = ("eng", "fn", "deps", "marked", "val", "dsem", "dval")


class Sched:
    ENGS = ("pe", "act", "dve", "pool", "sp")

    def __init__(self, nc, es):
        self.nc = nc
        self.es = es
        self.ops = {e: [] for e in self.ENGS}
        self.sems = {e: es.enter_context(nc.semaphore("s_" + e)) for e in self.ENGS}
        self.n_dsem = 0

    def dsem(self):
        self.n_dsem += 1
        return DSem(self.es.enter_context(self.nc.semaphore("d%d" % self.n_dsem)))

    def _deps(self, reads, writes):
        deps = []
        for t in reads:
            if t.w is not None:
                deps.append(t.w)
        for t in writes:
            if t.w is not None:
                deps.append(t.w)
            deps.extend(t.r)
        return deps

    def _reg(self, o, tok, reads, writes):
        for t in reads:
            t.r.append(tok)
        for t in writes:
            t.w = tok
            t.r = []

    def op(self, eng, fn, reads=(), writes=()):
        o = Op()
        o.eng = eng
        o.fn = fn
        o.deps = self._deps(reads, writes)
        o.marked = False
        o.val = None
        o.dsem = None
        self.ops[eng].append(o)
        self._reg(o, ("e", o), reads, writes)
        return o

    def dma(self, eng, fn, dsem, reads=(), writes=()):
        o = Op()
        o.eng = eng
        o.fn = fn
        o.deps = self._deps(reads, writes)
        o.marked = False
        o.val = None
        o.dsem = dsem
        dsem.count += 16
        o.dval = dsem.count
        self.ops[eng].append(o)
        self._reg(o, ("d", dsem, o.dval), reads, writes)
        return o

    def emit(self, final_dsems=()):
        nc = self.nc
        for e in self.ENGS:
            for o in self.ops[e]:
                for d in o.deps:
                    if d[0] == "e":
                        p = d[1]
                        if p.eng == "pe" and o.eng == "pe" and o.dsem is None:
                            continue
                        p.marked = True
        for e in self.ENGS:
            c = 0
            for o in self.ops[e]:
                if o.marked:
                    assert o.dsem is None
                    c += 1
                    o.val = c
        sems = self.sems
        with nc.Block() as block:
            def run(engname, engine):
                waited = {}
                for o in self.ops[engname]:
                    need = {}
                    for d in o.deps:
                        if d[0] == "e":
                            p = d[1]
                            if p.eng == "pe" and engname == "pe" and o.dsem is None:
                                continue
                            key = ("e", p.eng)
                            v = p.val
                            s = sems[p.eng]
                        else:
                            key = ("d", id(d[1]))
                            v = d[2]
                            s = d[1].sem
                        if waited.get(key, 0) >= v:
                            continue
                        if key not in need or need[key][1] < v:
                            need[key] = (s, v)
                    for key, (s, v) in need.items():
                        engine.wait_ge(s, v)
                        waited[key] = v
                    ins = o.fn(engine)
                    if o.dsem is not None:
                        ins.then_inc(o.dsem.sem, 16)
                    elif o.marked:
                        ins.then_inc(sems[engname], 1)
                if engname == "sp":
                    for ds in final_dsems:
                        engine.wait_ge(ds.sem, ds.count)

            @block.tensor
            def _(e):
                run("pe", e)

            @block.scalar
            def _(e):
                run("act", e)

            @block.vector
            def _(e):
                run("dve", e)

            @block.gpsimd
            def _(e):
                run("pool", e)

            @block.sync
            def _(e):
                run("sp", e)


class Ctx:
    def __init__(self, nc, es):
        self.nc = nc
        self.es = es
        self.S = Sched(nc, es)
        self.n = 0
        self.psb = []
        self.pst = []
        for i in range(8):
            self.psb.append(es.enter_context(nc.psum_tensor("ps%d" % i, [128, 512], F32)))
            self.pst.append(Tile("ps%d" % i))
        self.ps_i = 0
        self.store_ds = {}

    def sb(self, shape, dt, name=None):
        self.n += 1
        return self.es.enter_context(self.nc.sbuf_tensor(name or ("t%d" % self.n), shape, dt))

    def ps(self, lo=0, hi=6):
        k = lo + self.ps_i % (hi - lo)
        self.ps_i += 1
        return self.psb[k], self.pst[k]

    def ring(self, shape, dt, n):
        return Ring(self, shape, dt, n)

    def inp(self, name, shape, dt=F32):
        return self.nc.dram_tensor(name, list(shape), dt, kind="ExternalInput").ap()

    def outp(self, name, shape, dt=F32):
        return self.nc.dram_tensor(name, list(shape), dt, kind="ExternalOutput").ap()

    def store(self, out_ap, in_ap, reads):
        key = id(reads[0])
        if key not in self.store_ds:
            self.store_ds[key] = self.S.dsem()
        self.S.dma("sp", lambda e: e.dma_start(out=out_ap, in_=in_ap), self.store_ds[key], reads=reads)

    def finish(self):
        self.S.emit(final_dsems=list(self.store_ds.values()))

    def load_const(self, dram_ap, shape, dt, eng=None):
        t = self.sb(shape, dt)
        tl = Tile()
        ds = self.S.dsem()
        src_dt_differs = (dt != F32)
        q = "pool" if src_dt_differs else "sp"
        self.S.dma(q, lambda e: e.dma_start(out=t[:], in_=dram_ap), ds, writes=[tl])
        return t, tl


class Ring:
    def __init__(self, cx, shape, dt, n):
        self.bufs = [cx.sb(shape, dt) for _ in range(n)]
        self.tiles = [Tile() for _ in range(n)]
        self.dsems = [cx.S.dsem() for _ in range(n)]
        self.n = n
        self.i = 0

    def next(self):
        k = self.i % self.n
        self.i += 1
        return self.bufs[k], self.tiles[k], self.dsems[k]


def rms_stats(cx, srcs, src_tiles, T, Dn, ones_ap, ones_tile, sq_ring, rstd, rstd_tile, npart=128):
    S = cx.S
    pn, pnt = cx.ps(6, 8)
    n = len(srcs)
    for c, (src, st) in enumerate(zip(srcs, src_tiles)):
        sq, sqt, _ = sq_ring.next()
        S.op("act", lambda e, sq=sq, src=src: e.activation(out=sq[0:npart, 0:T], in_=src, func=AF.Square),
             reads=[st], writes=[sqt])
        S.op("pe", lambda e, sq=sq, c=c: e.matmul(pn[0:npart, 0:T], lhsT=ones_ap, rhs=sq[0:npart, 0:T],
                                                  start=(c == 0), stop=(c == n - 1)),
             reads=[sqt, ones_tile], writes=[pnt])
    S.op("act", lambda e: e.activation(out=rstd[0:npart, 0:T], in_=pn[0:npart, 0:T], func=AF.Sqrt, bias=EPS, scale=1.0 / Dn),
         reads=[pnt], writes=[rstd_tile])
    S.op("dve", lambda e: e.reciprocal(out=rstd[0:npart, 0:T], in_=rstd[0:npart, 0:T]),
         reads=[rstd_tile], writes=[rstd_tile])


def stt(e, out, in0, scalar, in1, op0=ALU.mult, op1=ALU.mult):
    return e.scalar_tensor_tensor(out=out, in0=in0, scalar=scalar, in1=in1, op0=op0, op1=op1)


def build_A():
    nc = bass.Bass("TRN2", target_bir_lowering=False)
    with ExitStack() as es:
        cx = Ctx(nc, es)
        S = cx.S
        xT = cx.inp("xT", [D, NTOK])
        g1 = cx.inp("g1", [128, 16])
        wg = cx.inp("wg", [NFF, 128, 16, 128])
        wu = cx.inp("wu", [NFF, 128, 16, 128])
        wd = cx.inp("wd", [16, 128, NFF, 128])
        gmix = cx.inp("gmix", [128, 16])
        win = cx.inp("win", [15, 128, 16, 128])
        gql = cx.inp("gql", [128, 4])
        gkvl = cx.inp("gkvl", [128, 2])
        gq = cx.inp("gq", [128, 1])
        gqr = cx.inp("gqr", [128, 1])
        gkr = cx.inp("gkr", [128, 1])
        gmq = cx.inp("gmq", [128, 1])
        wuq = cx.inp("wuq", [16, 128, 4, 128])
        wuqr = cx.inp("wuqr", [8, 128, 4, 128])
        cosT = cx.inp("cosT", [128, NTOK])
        sinT = cx.inp("sinT", [128, NTOK])
        rotT = cx.inp("rotT", [128, 128])
        memT = cx.inp("memT", [D, 256])
        gmem = cx.inp("gmem", [128, 16])
        wmk = cx.inp("wmk", [4, 128, 16, 128])
        gmk = cx.inp("gmk", [128, 1])
        wmv = cx.inp("wmv", [4, 128, 16, 512])
        o_h = cx.outp("o_h", [D, NTOK])
        o_qn = cx.outp("o_qn", [D, NTOK], BF16)
        o_qr = cx.outp("o_qr", [1024, NTOK], BF16)
        o_ckv = cx.outp("o_ckv", [256, NTOK])
        o_kr = cx.outp("o_kr", [64, NTOK])
        o_pin = cx.outp("o_pin", [512, NTOK])
        o_qm = cx.outp("o_qm", [512, NTOK], BF16)
        o_mk = cx.outp("o_mk", [512, 256])
        o_mv = cx.outp("o_mv", [256, D])

        g1_sb, g1_t = cx.load_const(g1, [128, 16], F32)
        gmix_sb, gmix_t = cx.load_const(gmix, [128, 16], F32)
        gql_sb, gql_t = cx.load_const(gql, [128, 4], F32)
        gkvl_sb, gkvl_t = cx.load_const(gkvl, [128, 2], F32)
        gq_sb, gq_t = cx.load_const(gq, [128, 1], F32)
        gqr_sb, gqr_t = cx.load_const(gqr, [128, 1], F32)
        gkr_sb, gkr_t = cx.load_const(gkr, [128, 1], F32)
        gmq_sb, gmq_t = cx.load_const(gmq, [128, 1], F32)
        gmem_sb, gmem_t = cx.load_const(gmem, [128, 16], F32)
        gmk_sb, gmk_t = cx.load_const(gmk, [128, 1], F32)
        rot_bf, rot_bf_t = cx.load_const(rotT, [128, 128], BF16)
        rot_f, rot_f_t = cx.load_const(rotT, [128, 128], F32)
        ones = cx.sb([128, 128], BF16)
        ones_t = Tile()
        S.op("dve", lambda e: e.memset(ones[:], 1.0), writes=[ones_t])
        ones2 = cx.sb([128, 128], BF16)
        ones2_t = Tile()
        S.op("dve", lambda e: e.memset(ones2[:], 0.0), writes=[ones2_t])
        S.op("dve", lambda e: e.memset(ones2[0:64, 0:64], 1.0), reads=[ones2_t], writes=[ones2_t])
        S.op("dve", lambda e: e.memset(ones2[64:128, 64:128], 1.0), reads=[ones2_t], writes=[ones2_t])

        onesf = cx.sb([64, 64], F32)
        onesf_t = Tile()
        S.op("dve", lambda e: e.memset(onesf[:], 1.0), writes=[onesf_t])
        x_sb = cx.sb([128, 16, 512], F32); x_t = Tile()
        xn = cx.sb([128, 16, 512], BF16); xn_t = Tile()
        act = cx.sb([128, NFF, 512], BF16); act_t = Tile()
        proj = cx.sb([128, 15, 512], F32); proj_t = Tile()
        cq = cx.sb([128, 4, 512], BF16); cq_t = Tile()
        ckv_sb = cx.sb([128, 2, 512], F32); ckv_t = Tile()
        qm_sb = cx.sb([128, 4, 512], BF16); qm_t = Tile()
        cos_sb = cx.sb([128, 512], F32); cos_t = Tile()
        sin_sb = cx.sb([128, 512], F32); sin_t = Tile()
        krn = cx.sb([64, 512], F32); krn_t = Tile()
        kro = cx.sb([64, 512], F32); kro_t = Tile()
        tmpf = cx.sb([128, 512], F32); tmpf_t = Tile()
        rstd = cx.sb([128, 512], F32); rstd_t = Tile()
        silu_ring = cx.ring([128, 512], BF16, 2)
        sq_ring = cx.ring([128, 512], BF16, 3)
        sqf_ring = cx.ring([128, 512], F32, 1)
        qo_ring = cx.ring([128, 512], BF16, 2)
        qrn_ring = cx.ring([128, 512], BF16, 2)
        w_ring = cx.ring([128, 16, 128], BF16, 5)
        wd_ring = cx.ring([128, NFF, 128], BF16, 2)
        wq_ring = cx.ring([128, 4, 128], BF16, 4)
        cs_ds = S.dsem()
        cos_ds = S.dsem()
        sin_ds = S.dsem()

        def cast_load(ring, src_ap, shape_slice=None):
            buf, tl, ds = ring.next()
            dst = buf[:] if shape_slice is None else shape_slice(buf)
            S.dma("pool", lambda e: e.dma_start(out=dst, in_=src_ap), ds, writes=[tl])
            return buf, tl

        def rmsnorm_chunks(src_sb, src_t, C, T, g_sb, g_t, out_sb, out_t, c0=0):
            rms_stats(cx, [src_sb[:, c0 + c, 0:T] for c in range(C)], [src_t] * C, T, 128.0 * C,
                      ones[:], ones_t, sq_ring, rstd, rstd_t)
            for c in range(C):
                S.op("dve", lambda e, c=c: stt(e, out_sb[:, c, 0:T], src_sb[:, c0 + c, 0:T], g_sb[:, c:c + 1], rstd[:, 0:T]),
                     reads=[src_t, g_t, rstd_t], writes=[out_t])

        m_sb = x_sb; m_t = x_t
        mn = xn; mn_t = xn_t
        S.dma("sp", lambda e: e.dma_start(out=m_sb[:, :, 0:256], in_=memT.rearrange("(c p) t -> p c t", p=128)), cs_ds, writes=[m_t])
        rmsnorm_chunks(m_sb, m_t, 16, 256, gmem_sb, gmem_t, mn, mn_t)
        mk_sb = proj; mk_t = proj_t
        for h in range(4):
            wb, wt = cast_load(w_ring, wmk[h])
            pk, pkt = cx.ps()
            for c in range(16):
                S.op("pe", lambda e, c=c, wb=wb, pk=pk: e.matmul(pk[:, 0:256], lhsT=wb[:, c, :], rhs=mn[:, c, 0:256], start=(c == 0), stop=(c == 15)),
                     reads=[wt, mn_t], writes=[pkt])
            rms_stats(cx, [pk[:, 0:256]], [pkt], 256, 128.0, ones[:], ones_t, sq_ring, rstd, rstd_t)
            S.op("dve", lambda e, h=h, pk=pk: stt(e, mk_sb[:, h, 0:256], pk[:, 0:256], gmk_sb[:, 0:1], rstd[:, 0:256]),
                 reads=[pkt, gmk_t, rstd_t], writes=[mk_t])
        cx.store(o_mk.rearrange("(c p) t -> p c t", p=128), mk_sb[:, 0:4, 0:256], [mk_t])
        mv_t = proj_t
        wmv_ds = S.dsem()
        for gcol in range(4):
            wb, wt = act, act_t
            S.dma("pool", lambda e, gcol=gcol: e.dma_start(out=act[:, 0:16, :], in_=wmv[gcol]), wmv_ds, writes=[act_t])
            for mc in range(2):
                pv, pvt = cx.ps()
                for c in range(16):
                    S.op("pe", lambda e, c=c, wb=wb, pv=pv, mc=mc: e.matmul(pv[:, :], lhsT=mn[:, c, mc * 128:(mc + 1) * 128], rhs=wb[:, c, :],
                                                                            start=(c == 0), stop=(c == 15)),
                         reads=[wt, mn_t], writes=[pvt])
                S.op("act", lambda e, pv=pv, mc=mc, gcol=gcol: e.activation(out=proj[:, 4 + mc * 4 + gcol, :], in_=pv[:, :], func=AF.Copy),
                     reads=[pvt], writes=[mv_t])
        for mc in range(2):
            cx.store(o_mv[mc * 128:(mc + 1) * 128, :].rearrange("p (g f) -> p g f", f=512), proj[:, 4 + mc * 4:8 + mc * 4, :], [mv_t])

        for (t0, T) in TILES:
            S.dma("sp", lambda e, t0=t0, T=T: e.dma_start(out=x_sb[:, :, 0:T], in_=xT[:, t0:t0 + T].rearrange("(c p) t -> p c t", p=128)),
                  cs_ds, writes=[x_t])
            S.dma("sp", lambda e, t0=t0, T=T: e.dma_start(out=cos_sb[:, 0:T], in_=cosT[:, t0:t0 + T]), cos_ds, writes=[cos_t])
            S.dma("sp", lambda e, t0=t0, T=T: e.dma_start(out=sin_sb[:, 0:T], in_=sinT[:, t0:t0 + T]), sin_ds, writes=[sin_t])
            rmsnorm_chunks(x_sb, x_t, 16, T, g1_sb, g1_t, xn, xn_t)
            for j in range(NFF):
                wgb, wgt = cast_load(w_ring, wg[j])
                wub, wut = cast_load(w_ring, wu[j])
                pg, pgt = cx.ps()
                pu, put = cx.ps()
                for c in range(16):
                    S.op("pe", lambda e, c=c, wgb=wgb, pg=pg, T=T: e.matmul(pg[:, 0:T], lhsT=wgb[:, c, :], rhs=xn[:, c, 0:T], start=(c == 0), stop=(c == 15)),
                         reads=[wgt, xn_t], writes=[pgt])
                for c in range(16):
                    S.op("pe", lambda e, c=c, wub=wub, pu=pu, T=T: e.matmul(pu[:, 0:T], lhsT=wub[:, c, :], rhs=xn[:, c, 0:T], start=(c == 0), stop=(c == 15)),
                         reads=[wut, xn_t], writes=[put])
                sl, slt, _ = silu_ring.next()
                S.op("act", lambda e, sl=sl, pg=pg, T=T: e.activation(out=sl[:, 0:T], in_=pg[:, 0:T], func=AF.Silu), reads=[pgt], writes=[slt])
                S.op("dve", lambda e, sl=sl, pu=pu, j=j, T=T: e.tensor_tensor(out=act[:, j, 0:T], in0=sl[:, 0:T], in1=pu[:, 0:T], op=ALU.mult),
                     reads=[slt, put], writes=[act_t])
            for c in range(16):
                wdb, wdt = cast_load(wd_ring, wd[c])
                po, pot = cx.ps()
                for j in range(NFF):
                    S.op("pe", lambda e, j=j, wdb=wdb, po=po, T=T: e.matmul(po[:, 0:T], lhsT=wdb[:, j, :], rhs=act[:, j, 0:T], start=(j == 0), stop=(j == NFF - 1)),
                         reads=[wdt, act_t], writes=[pot])
                S.op("dve", lambda e, c=c, po=po, T=T: stt(e, x_sb[:, c, 0:T], po[:, 0:T], 0.5, x_sb[:, c, 0:T], ALU.mult, ALU.add),
                     reads=[pot, x_t], writes=[x_t])
            cx.store(o_h[:, t0:t0 + T].rearrange("(c p) t -> p c t", p=128), x_sb[:, :, 0:T], [x_t])
            rmsnorm_chunks(x_sb, x_t, 16, T, gmix_sb, gmix_t, xn, xn_t)
            for j in range(15):
                wb, wt = cast_load(w_ring, win[j])
                pp, ppt = cx.ps()
                for c in range(16):
                    S.op("pe", lambda e, c=c, wb=wb, pp=pp, T=T: e.matmul(pp[:, 0:T], lhsT=wb[:, c, :], rhs=xn[:, c, 0:T], start=(c == 0), stop=(c == 15)),
                         reads=[wt, xn_t], writes=[ppt])
                S.op("act", lambda e, j=j, pp=pp, T=T: e.activation(out=proj[:, j, 0:T], in_=pp[:, 0:T], func=AF.Copy), reads=[ppt], writes=[proj_t])
            cx.store(o_pin[:, t0:t0 + T].rearrange("(c p) t -> p c t", p=128), proj[:, 7:11, 0:T], [proj_t])
            rmsnorm_chunks(proj, proj_t, 4, T, gql_sb, gql_t, cq, cq_t, c0=0)
            rmsnorm_chunks(proj, proj_t, 2, T, gkvl_sb, gkvl_t, ckv_sb, ckv_t, c0=4)
            cx.store(o_ckv[:, t0:t0 + T].rearrange("(c p) t -> p c t", p=128), ckv_sb[:, :, 0:T], [ckv_t])
            sqf, sqft, _ = sqf_ring.next()
            pn, pnt = cx.ps(6, 8)
            S.op("act", lambda e, T=T: e.activation(out=sqf[0:64, 0:T], in_=proj[0:64, 6, 0:T], func=AF.Square), reads=[proj_t], writes=[sqft])
            S.op("pe", lambda e, T=T, pn=pn: e.matmul(pn[0:64, 0:T], lhsT=onesf[0:64, 0:64], rhs=sqf[0:64, 0:T], start=True, stop=True),
                 reads=[sqft, onesf_t], writes=[pnt])
            S.op("act", lambda e, T=T, pn=pn: e.activation(out=rstd[0:64, 0:T], in_=pn[0:64, 0:T], func=AF.Sqrt, bias=EPS, scale=1.0 / 64),
                 reads=[pnt], writes=[rstd_t])
            S.op("dve", lambda e, T=T: e.reciprocal(out=rstd[0:64, 0:T], in_=rstd[0:64, 0:T]), reads=[rstd_t], writes=[rstd_t])
            S.op("dve", lambda e, T=T: stt(e, krn[:, 0:T], proj[0:64, 6, 0:T], gkr_sb[0:64, 0:1], rstd[0:64, 0:T]),
                 reads=[proj_t, gkr_t, rstd_t], writes=[krn_t])
            pr, prt = cx.ps()
            S.op("pe", lambda e, T=T, pr=pr: e.matmul(pr[0:64, 0:T], lhsT=rot_f[0:64, 0:64], rhs=krn[:, 0:T], start=True, stop=True),
                 reads=[krn_t, rot_f_t], writes=[prt])
            S.op("dve", lambda e, T=T, pr=pr: e.tensor_tensor(out=tmpf[0:64, 0:T], in0=pr[0:64, 0:T], in1=sin_sb[0:64, 0:T], op=ALU.mult),
                 reads=[prt, sin_t], writes=[tmpf_t])
            S.op("dve", lambda e, T=T: e.tensor_tensor(out=kro[:, 0:T], in0=krn[:, 0:T], in1=cos_sb[0:64, 0:T], op=ALU.mult),
                 reads=[krn_t, cos_t], writes=[kro_t])
            S.op("dve", lambda e, T=T: e.tensor_tensor(out=kro[:, 0:T], in0=kro[:, 0:T], in1=tmpf[0:64, 0:T], op=ALU.add),
                 reads=[kro_t, tmpf_t], writes=[kro_t])
            cx.store(o_kr[:, t0:t0 + T], kro[:, 0:T], [kro_t])
            for h in range(4):
                rms_stats(cx, [proj[:, 11 + h, 0:T]], [proj_t], T, 128.0, ones[:], ones_t, sq_ring, rstd, rstd_t)
                S.op("dve", lambda e, h=h, T=T: stt(e, qm_sb[:, h, 0:T], proj[:, 11 + h, 0:T], gmq_sb[:, 0:1], rstd[:, 0:T]),
                     reads=[proj_t, gmq_t, rstd_t], writes=[qm_t])
            cx.store(o_qm[:, t0:t0 + T].rearrange("(c p) t -> p c t", p=128), qm_sb[:, :, 0:T], [qm_t])
            for h in range(16):
                wb, wt = cast_load(wq_ring, wuq[h])
                pq, pqt = cx.ps()
                for c in range(4):
                    S.op("pe", lambda e, c=c, wb=wb, pq=pq, T=T: e.matmul(pq[:, 0:T], lhsT=wb[:, c, :], rhs=cq[:, c, 0:T], start=(c == 0), stop=(c == 3)),
                         reads=[wt, cq_t], writes=[pqt])
                rms_stats(cx, [pq[:, 0:T]], [pqt], T, 128.0, ones[:], ones_t, sq_ring, rstd, rstd_t)
                qo, qot, _ = qo_ring.next()
                S.op("dve", lambda e, pq=pq, qo=qo, T=T: stt(e, qo[:, 0:T], pq[:, 0:T], gq_sb[:, 0:1], rstd[:, 0:T]),
                     reads=[pqt, gq_t, rstd_t], writes=[qot])
                cx.store(o_qn[h * 128:(h + 1) * 128, t0:t0 + T], qo[:, 0:T], [qot])
            for hp in range(8):
                wb, wt = cast_load(wq_ring, wuqr[hp])
                pq, pqt = cx.ps()
                for c in range(4):
                    S.op("pe", lambda e, c=c, wb=wb, pq=pq, T=T: e.matmul(pq[:, 0:T], lhsT=wb[:, c, :], rhs=cq[:, c, 0:T], start=(c == 0), stop=(c == 3)),
                         reads=[wt, cq_t], writes=[pqt])
                rms_stats(cx, [pq[:, 0:T]], [pqt], T, 64.0, ones2[:], ones2_t, sq_ring, rstd, rstd_t)
                qrn, qrnt, _ = qrn_ring.next()
                S.op("dve", lambda e, pq=pq, qrn=qrn, T=T: stt(e, qrn[:, 0:T], pq[:, 0:T], gqr_sb[:, 0:1], rstd[:, 0:T]),
                     reads=[pqt, gqr_t, rstd_t], writes=[qrnt])
                pr, prt = cx.ps()
                S.op("pe", lambda e, pr=pr, qrn=qrn, T=T: e.matmul(pr[:, 0:T], lhsT=rot_bf[:], rhs=qrn[:, 0:T], start=True, stop=True),
                     reads=[qrnt, rot_bf_t], writes=[prt])
                S.op("dve", lambda e, pr=pr, T=T: e.tensor_tensor(out=tmpf[:, 0:T], in0=pr[:, 0:T], in1=sin_sb[:, 0:T], op=ALU.mult),
                     reads=[prt, sin_t], writes=[tmpf_t])
                qo, qot, _ = qo_ring.next()
                S.op("dve", lambda e, qrn=qrn, qo=qo, T=T: e.tensor_tensor(out=qo[:, 0:T], in0=qrn[:, 0:T], in1=cos_sb[:, 0:T], op=ALU.mult),
                     reads=[qrnt, cos_t], writes=[qot])
                S.op("dve", lambda e, qo=qo, T=T: e.tensor_tensor(out=qo[:, 0:T], in0=qo[:, 0:T], in1=tmpf[:, 0:T], op=ALU.add),
                     reads=[qot, tmpf_t], writes=[qot])
                cx.store(o_qr[hp * 128:(hp + 1) * 128, t0:t0 + T], qo[:, 0:T], [qot])
        print("A sbuf remaining", nc.sbuf_bytes_remaining)
        cx.finish()
    return nc


def _tok_index(i):
    return np.concatenate([np.arange(512 * (8 * s + i), 512 * (8 * s + i) + 512) for s in range(4)])


def _positions(i):
    return np.concatenate([_tok_index(i), PAST + np.arange(DEC_T), PAST + np.arange(DEC_T)]).astype(np.int64)


def _rope_tables(i):
    half = 32
    inv = np.power(np.float32(10000.0), -np.arange(half, dtype=np.float32) / np.float32(half)).astype(np.float32)
    pos = _positions(i).astype(np.float32)
    ang = (pos[:, None] * inv[None, :]).astype(np.float32)
    cos = np.cos(ang).astype(np.float32).T
    sin = np.sin(ang).astype(np.float32).T
    return np.ascontiguousarray(np.tile(cos, (4, 1))), np.ascontiguousarray(np.tile(sin, (4, 1)))


def _rot_matrix():
    r = np.zeros((128, 128), np.float32)
    for b in range(2):
        for m in range(32):
            r[b * 64 + m + 32, b * 64 + m] = -1.0
        for m in range(32, 64):
            r[b * 64 + m - 32, b * 64 + m] = 1.0
    return r


def _slab(w, kc, nj, f=128):
    return np.ascontiguousarray(w.reshape(kc, 128, nj, f).transpose(2, 1, 0, 3))


def _gain(g, c):
    return np.ascontiguousarray(g.reshape(c, 128).T)


WIN_STARTS = [0, 128, 256, 384, 512, 640, 768, 832, 960, 1088, 1216, 1344, 1472, 1600, 1728]
WIN_WIDTHS = [128] * 6 + [64] + [128] * 8


def _prep_A_weights(inp, l):
    w = {}
    w["g1"] = _gain(inp["ffn1_norm"][l], 16)
    w["wg"] = _slab(inp["ffn1_wg"][l], 16, NFF)
    w["wu"] = _slab(inp["ffn1_wu"][l], 16, NFF)
    w["wd"] = _slab(inp["ffn1_wd"][l], NFF, 16)
    w["gmix"] = _gain(inp["mix_norm"][l], 16)
    win = inp["w_in"][l]
    wp = np.zeros((D, 15 * 128), np.float32)
    for j, (s0, wd_) in enumerate(zip(WIN_STARTS, WIN_WIDTHS)):
        wp[:, j * 128:j * 128 + wd_] = win[:, s0:s0 + wd_]
    w["win"] = _slab(wp, 16, 15)
    w["gql"] = _gain(inp["q_lat_norm"][l], 4)
    w["gkvl"] = _gain(inp["kv_lat_norm"][l], 2)
    w["gq"] = np.ascontiguousarray(inp["q_norm"][l][:, None])
    w["gqr"] = np.ascontiguousarray(np.concatenate([inp["qr_norm"][l]] * 2)[:, None])
    w["gkr"] = np.ascontiguousarray(np.concatenate([inp["kr_norm"][l]] * 2)[:, None])
    w["gmq"] = np.ascontiguousarray(inp["mq_norm"][l][:, None])
    w["wuq"] = _slab(inp["w_uq"][l], 4, 16)
    w["wuqr"] = _slab(inp["w_uqr"][l], 4, 8)
    w["rotT"] = _rot_matrix()
    w["memT"] = np.ascontiguousarray(inp["mem_prompt"][0].T)
    w["gmem"] = _gain(inp["mem_norm"][l], 16)
    w["wmk"] = _slab(inp["w_mk"][l], 16, 4)
    w["gmk"] = np.ascontiguousarray(inp["mk_norm"][l][:, None])
    w["wmv"] = _slab(inp["w_mv"][l], 16, 4, 512)
    return w


def _initial_xT(inp, i):
    xp = inp["x_prompt"][0][_tok_index(i)]
    xs = inp["x_sample"][2 * i:2 * i + 2].reshape(2 * DEC_T, D)
    return np.ascontiguousarray(np.concatenate([xp, xs], 0).T)


SCALE = float((128 + 64) ** -0.5)
S_KEYS = PAST + DEC_T


def build_B1():
    nc = bass.Bass("TRN2", target_bir_lowering=False)
    with ExitStack() as es:
        cx = Ctx(nc, es)
        S = cx.S
        qnT = cx.inp("qnT", [D, NTOK], BF16)
        qrT = cx.inp("qrT", [1024, NTOK], BF16)
        ckvA = cx.inp("ckvA", [256, SEQ])
        krA = cx.inp("krA", [64, SEQ])
        ckvS = cx.inp("ckvS", [2, 256, S_KEYS])
        krS = cx.inp("krS", [2, 64, S_KEYS])
        maskd = cx.inp("mask", [128, 32, 512])
        wuk = cx.inp("wuk", [16, 128, 2, 128])
        wuv = cx.inp("wuv", [16, 128, 2, 128])
        gk = cx.inp("gk", [128, 1])
        o_at = cx.outp("o_at", [D, NTOK], BF16)

        gk_sb, gk_t = cx.load_const(gk, [128, 1], F32)
        mask_sb, mask_t = cx.load_const(maskd, [128, 32, 512], BF16)
        krA_sb, krA_t = cx.load_const(krA, [64, SEQ], BF16)
        krS_sb = []
        for b in range(2):
            krS_sb.append(cx.load_const(krS[b], [64, S_KEYS], BF16))
        ones = cx.sb([128, 128], BF16); ones_t = Tile()
        S.op("dve", lambda e: e.memset(ones[:], 1.0), writes=[ones_t])
        onesf = cx.sb([128, 128], F32); onesf_t = Tile()
        S.op("dve", lambda e: e.memset(onesf[:], 1.0), writes=[onesf_t])
        acc_ring = cx.ring([128, 512], F32, 4)
        KT = cx.sb([128, SEQ], BF16); KT_t = Tile()
        V = cx.sb([128, 128, 128], BF16); V_t = Tile()
        rstd = cx.sb([128, 512], F32); rstd_t = Tile()
        rinv = cx.sb([128, 512], F32); rinv_t = Tile()
        sq_ring = cx.ring([128, 512], BF16, 2)
        ck_ring = cx.ring([128, 2, 512], BF16, 3)
        p_ring = cx.ring([128, 512], BF16, 4)
        rstd_ring = cx.ring([128, 512], F32, 3)
        ao_ring = cx.ring([128, 512], BF16, 2)
        qn_ring = cx.ring([128, NTOK], BF16, 2)
        qr_ring = cx.ring([64, NTOK], BF16, 2)
        wk_ring = cx.ring([128, 2, 128], BF16, 2)
        wv_ring = cx.ring([128, 2, 128], BF16, 2)
        po, pot = cx.psb[4], cx.pst[4]
        psm, psmt = cx.psb[5], cx.pst[5]

        def build_kv(h, src_ap, nkeys, wkb, wkt, wvb, wvt):
            k0 = 0
            while k0 < nkeys:
                kl = min(512, nkeys - k0)
                ck, ckt, ds = ck_ring.next()
                S.dma("pool", lambda e, ck=ck, k0=k0, kl=kl: e.dma_start(out=ck[:, :, 0:kl], in_=src_ap[:, k0:k0 + kl].rearrange("(c p) k -> p c k", p=128)),
                      ds, writes=[ckt])
                pk, pkt = cx.ps(0, 4)
                rs, rst, _ = rstd_ring.next()
                for c in range(2):
                    S.op("pe", lambda e, c=c, ck=ck, pk=pk, kl=kl: e.matmul(pk[:, 0:kl], lhsT=wkb[:, c, :], rhs=ck[:, c, 0:kl], start=(c == 0), stop=(c == 1)),
                         reads=[wkt, ckt], writes=[pkt])
                pv, pvt = cx.ps(0, 4)
                nsub = (kl + 127) // 128
                kkmax = min(128, kl)
                for sub in range(nsub):
                    kk = min(128, kl - sub * 128)
                    for c in range(2):
                        S.op("pe", lambda e, c=c, ck=ck, pv=pv, sub=sub, kk=kk: e.matmul(pv[0:kk, sub * 128:(sub + 1) * 128], lhsT=ck[:, c, sub * 128:sub * 128 + kk],
                                                                                        rhs=wvb[:, c, :], start=(c == 0), stop=(c == 1)),
                             reads=[wvt, ckt], writes=[pvt])
                rms_stats(cx, [pk[:, 0:kl]], [pkt], kl, 128.0, ones[:], ones_t, sq_ring, rs, rst)
                kt0 = k0 // 128
                S.op("act", lambda e, pv=pv, kt0=kt0, nsub=nsub, kkmax=kkmax: e.activation(
                    out=V[0:kkmax, kt0:kt0 + nsub, :], in_=pv[0:kkmax, 0:nsub * 128].rearrange("p (a b) -> p a b", b=128), func=AF.Copy),
                     reads=[pvt], writes=[V_t])
                S.op("dve", lambda e, pk=pk, k0=k0, kl=kl, rs=rs: stt(e, KT[:, k0:k0 + kl], pk[:, 0:kl], gk_sb[:, 0:1], rs[:, 0:kl]),
                     reads=[pkt, gk_t, rst], writes=[KT_t])
                k0 += kl

        def attend(h, qn, qnt, qr, qrt, kr_sb, kr_t, q_off, Tq, tiles, mask_lo):
            n = len(tiles)
            LOOK = 3
            staged = []
            accs = [acc_ring.next()[0:2], acc_ring.next()[0:2]]

            def stage1(idx):
                kt, kk = tiles[idx]
                pss, psst = cx.ps(0, 4)
                S.op("pe", lambda e, pss=pss, kt=kt, kk=kk: e.matmul(pss[0:kk, 0:Tq], lhsT=KT[:, kt * 128:kt * 128 + kk], rhs=qn[:, q_off:q_off + Tq], start=True, stop=False),
                     reads=[KT_t, qnt], writes=[psst])
                S.op("pe", lambda e, pss=pss, kt=kt, kk=kk: e.matmul(pss[0:kk, 0:Tq], lhsT=kr_sb[0:64, kt * 128:kt * 128 + kk], rhs=qr[0:64, q_off:q_off + Tq], start=False, stop=True),
                     reads=[kr_t, qrt], writes=[psst])
                pT, pTt, _ = p_ring.next()
                S.op("act", lambda e, pss=pss, pT=pT, kk=kk: e.activation(out=pT[0:kk, 0:Tq], in_=pss[0:kk, 0:Tq], func=AF.Exp, scale=SCALE),
                     reads=[psst], writes=[pTt])
                if mask_lo is not None and kt >= mask_lo:
                    m = kt - mask_lo
                    S.op("pool", lambda e, pT=pT, m=m: e.tensor_tensor(out=pT[:, 0:Tq], in0=pT[:, 0:Tq], in1=mask_sb[:, m, 0:Tq], op=ALU.mult),
                         reads=[pTt, mask_t], writes=[pTt])
                staged.append((pT, pTt))

            def stage2(idx):
                kt, kk = tiles[idx]
                pT, pTt = staged[idx]
                S.op("pe", lambda e, pT=pT, kt=kt, kk=kk, idx=idx: e.matmul(po[:, 0:Tq], lhsT=V[0:kk, kt, :], rhs=pT[0:kk, 0:Tq], start=(idx == 0), stop=(idx == n - 1)),
                     reads=[V_t, pTt], writes=[pot])
                acc, acct = accs[idx % 2]
                if idx < 2:
                    S.op("dve", lambda e, pT=pT, kk=kk, acc=acc: e.tensor_copy(out=acc[0:kk, 0:Tq], in_=pT[0:kk, 0:Tq]),
                         reads=[pTt], writes=[acct])
                else:
                    S.op("dve", lambda e, pT=pT, kk=kk, acc=acc: e.tensor_tensor(out=acc[0:kk, 0:Tq], in0=acc[0:kk, 0:Tq], in1=pT[0:kk, 0:Tq], op=ALU.add),
                         reads=[pTt, acct], writes=[acct])

            for idx in range(n):
                stage1(idx)
                if idx >= LOOK:
                    stage2(idx - LOOK)
            for idx in range(max(0, n - LOOK), n):
                stage2(idx)
            for j in range(2):
                S.op("pe", lambda e, j=j: e.matmul(psm[:, 0:Tq], lhsT=onesf[:], rhs=accs[j][0][:, 0:Tq], start=(j == 0), stop=(j == 1)),
                     reads=[onesf_t, accs[j][1]], writes=[psmt])
            S.op("dve", lambda e: e.reciprocal(out=rinv[:, 0:Tq], in_=psm[:, 0:Tq]), reads=[psmt], writes=[rinv_t])
            ao, aot, _ = ao_ring.next()
            S.op("dve", lambda e, ao=ao: e.tensor_tensor(out=ao[:, 0:Tq], in0=po[:, 0:Tq], in1=rinv[:, 0:Tq], op=ALU.mult),
                 reads=[pot, rinv_t], writes=[aot])
            cx.store(o_at[h * 128:(h + 1) * 128, q_off:q_off + Tq], ao[:, 0:Tq], [aot])

        for h in range(16):
            wkb, wkt, ds = wk_ring.next()
            S.dma("pool", lambda e, wkb=wkb, h=h: e.dma_start(out=wkb[:], in_=wuk[h]), ds, writes=[wkt])
            wvb, wvt, ds = wv_ring.next()
            S.dma("pool", lambda e, wvb=wvb, h=h: e.dma_start(out=wvb[:], in_=wuv[h]), ds, writes=[wvt])
            qn, qnt, ds = qn_ring.next()
            S.dma("sp", lambda e, qn=qn, h=h: e.dma_start(out=qn[:], in_=qnT[h * 128:(h + 1) * 128, :]), ds, writes=[qnt])
            qr, qrt, ds = qr_ring.next()
            S.dma("sp", lambda e, qr=qr, h=h: e.dma_start(out=qr[:], in_=qrT[h * 64:(h + 1) * 64, :]), ds, writes=[qrt])
            build_kv(h, ckvA, SEQ, wkb, wkt, wvb, wvt)
            for s in range(4):
                attend(h, qn, qnt, qr, qrt, krA_sb, krA_t, 512 * s, 512, [(kt, 128) for kt in range(32 * (s + 1))], 32 * s)
            for b in range(2):
                build_kv(h, ckvS[b], S_KEYS, wkb, wkt, wvb, wvt)
                tiles = [(kt, 128) for kt in range(32)] + [(32, 64)]
                attend(h, qn, qnt, qr, qrt, krS_sb[b][0], krS_sb[b][1], 2048 + 64 * b, 64, tiles, None)
        print("B1 sbuf remaining", nc.sbuf_bytes_remaining)
        cx.finish()
    return nc


def _mask_for_core(i):
    p = np.arange(128)[:, None, None]
    m = np.arange(32)[None, :, None]
    f = np.arange(512)[None, None, :]
    return ((128 * m + p) // 64 <= (512 * i + f) // 64).astype(np.float32)


TILES_B = [(0, 512, 0, 0), (512, 512, 527, 0), (1024, 512, 1054, 0), (1536, 512, 1581, 0), (2048, 64, 2108, 1), (2112, 64, 2187, 2)]
PINE_W = 4 * 527 + 2 * 79
SCALE_M = float(128 ** -0.5)


def build_B2():
    nc = bass.Bass("TRN2", target_bir_lowering=False)
    with ExitStack() as es:
        cx = Ctx(nc, es)
        S = cx.S
        hT = cx.inp("hT", [D, NTOK])
        atT = cx.inp("atT", [D, NTOK], BF16)
        qmT = cx.inp("qmT", [512, NTOK], BF16)
        pinE = cx.inp("pinE", [512, PINE_W])
        icnt = cx.inp("icnt", [128, 4, NTOK])
        mk3 = cx.inp("mk3", [3, 512, 256])
        mv3 = cx.inp("mv3", [3, 256, D])
        gmix = cx.inp("gmix", [128, 16])
        poolw = cx.inp("poolw", [4, 128, 512])
        pscale = cx.inp("pscale", [128, 16])
        wgate = cx.inp("wgate", [48, 128, 16, 128])
        bgate = cx.inp("bgate", [128, 48])
        wout = cx.inp("wout", [16, 128, 16, 128])
        g2 = cx.inp("g2", [128, 16])
        wg = cx.inp("wg", [NFF, 128, 16, 128])
        wu = cx.inp("wu", [NFF, 128, 16, 128])
        wd = cx.inp("wd", [16, 128, NFF, 128])
        o_x = cx.outp("o_x", [D, NTOK])

        gmix_sb, gmix_t = cx.load_const(gmix, [128, 16], F32)
        g2_sb, g2_t = cx.load_const(g2, [128, 16], F32)
        pscale_sb, pscale_t = cx.load_const(pscale, [128, 16], F32)
        bgate_sb, bgate_t = cx.load_const(bgate, [128, 48], F32)
        poolw_sb, poolw_t = cx.load_const(poolw.rearrange("g p f -> p g f"), [128, 4, 512], BF16)
        ones = cx.sb([128, 128], BF16); ones_t = Tile()
        S.op("dve", lambda e: e.memset(ones[:], 1.0), writes=[ones_t])

        x_sb = cx.sb([128, 16, 512], F32); x_t = Tile()
        xn = cx.sb([128, 16, 512], BF16); xn_t = Tile()
        act = cx.sb([128, NFF, 512], BF16); act_t = Tile()
        at_sb = cx.sb([128, 16, 512], BF16); at_t = Tile()
        pe_sb = cx.sb([128, 4, 527], F32); pe_t = Tile()
        lvA = cx.sb([128, 527], F32); lvA_t = Tile()
        lvB = cx.sb([128, 527], F32); lvB_t = Tile()
        pooled = cx.sb([128, 4, 512], BF16); pooled_t = Tile()
        PmT = cx.sb([128, 4, 2, 512], BF16); PmT_t = Tile()
        mk_sb = cx.sb([128, 4, 256], BF16); mk_t = Tile()
        mv_sb = cx.sb([128, 2, D], BF16); mv_t = Tile()
        qm_sb = cx.sb([128, 4, 512], BF16); qm_t = Tile()
        rstd = cx.sb([128, 512], F32); rstd_t = Tile()
        rinv = cx.sb([128, 512], F32); rinv_t = Tile()
        mrg = cx.sb([128, 512], F32); mrg_t = Tile()
        tmp1 = cx.sb([128, 512], F32); tmp1_t = Tile()
        icnt_ring = cx.ring([128, 512], F32, 2)
        gate_ring = cx.ring([128, 512], F32, 3)
        sq_ring = cx.ring([128, 512], BF16, 2)
        silu_ring = cx.ring([128, 512], BF16, 2)
        w_ring = cx.ring([128, 16, 128], BF16, 3)
        wd_ring = cx.ring([128, NFF, 128], BF16, 2)
        ld_ds = S.dsem()
        at_ds = S.dsem()
        qm_ds = S.dsem()
        pe_ds = S.dsem()
        mem_ds = S.dsem()
        mv_ds = S.dsem()

        def cast_load(ring, src_ap):
            buf, tl, ds = ring.next()
            S.dma("pool", lambda e: e.dma_start(out=buf[:], in_=src_ap), ds, writes=[tl])
            return buf, tl

        def rmsnorm16(T, g_sb, g_t):
            rms_stats(cx, [x_sb[:, c, 0:T] for c in range(16)], [x_t] * 16, T, float(D), ones[:], ones_t, sq_ring, rstd, rstd_t)
            for c in range(16):
                S.op("dve", lambda e, c=c: stt(e, xn[:, c, 0:T], x_sb[:, c, 0:T], g_sb[:, c:c + 1], rstd[:, 0:T]),
                     reads=[x_t, g_t, rstd_t], writes=[xn_t])

        for (t0, T, seg, ms) in TILES_B:
            L = 15 + T
            S.dma("sp", lambda e, t0=t0, T=T: e.dma_start(out=x_sb[:, :, 0:T], in_=hT[:, t0:t0 + T].rearrange("(c p) t -> p c t", p=128)), ld_ds, writes=[x_t])
            S.dma("sp", lambda e, t0=t0, T=T: e.dma_start(out=at_sb[:, :, 0:T], in_=atT[:, t0:t0 + T].rearrange("(c p) t -> p c t", p=128)), at_ds, writes=[at_t])
            S.dma("sp", lambda e, t0=t0, T=T: e.dma_start(out=qm_sb[:, :, 0:T], in_=qmT[:, t0:t0 + T].rearrange("(c p) t -> p c t", p=128)), qm_ds, writes=[qm_t])
            S.dma("sp", lambda e, seg=seg, L=L: e.dma_start(out=pe_sb[:, :, 0:L], in_=pinE[:, seg:seg + L].rearrange("(g p) t -> p g t", p=128)), pe_ds, writes=[pe_t])
            S.dma("pool", lambda e, ms=ms: e.dma_start(out=mk_sb[:], in_=mk3[ms].rearrange("(h p) m -> p h m", p=128)), mem_ds, writes=[mk_t])
            S.dma("pool", lambda e, ms=ms: e.dma_start(out=mv_sb[:], in_=mv3[ms].rearrange("(c p) f -> p c f", p=128)), mv_ds, writes=[mv_t])
            rmsnorm16(T, gmix_sb, gmix_t)
            for g in range(4):
                e_ = lambda a, b, g=g: pe_sb[:, g, a:b]
                S.op("pool", lambda e, g=g, L=L: e.tensor_tensor(out=lvA[:, 1:L], in0=pe_sb[:, g, 1:L], in1=pe_sb[:, g, 0:L - 1], op=ALU.add),
                     reads=[pe_t], writes=[lvA_t])
                cur, cur_t = lvA, lvA_t
                if g >= 1:
                    S.op("pool", lambda e, L=L: e.tensor_tensor(out=lvB[:, 3:L], in0=lvA[:, 3:L], in1=lvA[:, 1:L - 2], op=ALU.add),
                         reads=[lvA_t], writes=[lvB_t])
                    cur, cur_t = lvB, lvB_t
                if g >= 2:
                    S.op("pool", lambda e, L=L: e.tensor_tensor(out=lvA[:, 7:L], in0=lvB[:, 7:L], in1=lvB[:, 3:L - 4], op=ALU.add),
                         reads=[lvB_t, lvA_t], writes=[lvA_t])
                    cur, cur_t = lvA, lvA_t
                if g >= 3:
                    S.op("pool", lambda e, L=L: e.tensor_tensor(out=lvB[:, 15:L], in0=lvA[:, 15:L], in1=lvA[:, 7:L - 8], op=ALU.add),
                         reads=[lvA_t, lvB_t], writes=[lvB_t])
                    cur, cur_t = lvB, lvB_t
                ic, ict, ds = icnt_ring.next()
                S.dma("sp", lambda e, ic=ic, g=g, t0=t0, T=T: e.dma_start(out=ic[:, 0:T], in_=icnt[:, g, t0:t0 + T]), ds, writes=[ict])
                S.op("dve", lambda e, cur=cur, ic=ic, L=L, T=T: e.tensor_tensor(out=tmp1[:, 0:T], in0=cur[:, 15:L], in1=ic[:, 0:T], op=ALU.mult),
                     reads=[cur_t, ict], writes=[tmp1_t])
                S.op("dve", lambda e, g=g, L=L, T=T: e.tensor_tensor(out=pooled[:, g, 0:T], in0=tmp1[:, 0:T], in1=pe_sb[:, g, 15:L], op=ALU.subtract),
                     reads=[tmp1_t, pe_t], writes=[pooled_t])
            for hm in range(4):
                psm, psmt = cx.ps(6, 8)
                for mc in range(2):
                    pss, psst = cx.ps()
                    S.op("pe", lambda e, pss=pss, hm=hm, mc=mc, T=T: e.matmul(pss[:, 0:T], lhsT=mk_sb[:, hm, mc * 128:(mc + 1) * 128], rhs=qm_sb[:, hm, 0:T], start=True, stop=True),
                         reads=[mk_t, qm_t], writes=[psst])
                    S.op("act", lambda e, pss=pss, hm=hm, mc=mc, T=T: e.activation(out=PmT[:, hm, mc, 0:T], in_=pss[:, 0:T], func=AF.Exp, scale=SCALE_M),
                         reads=[psst], writes=[PmT_t])
                for mc in range(2):
                    S.op("pe", lambda e, psm=psm, hm=hm, mc=mc, T=T: e.matmul(psm[:, 0:T], lhsT=ones[:], rhs=PmT[:, hm, mc, 0:T], start=(mc == 0), stop=(mc == 1)),
                         reads=[ones_t, PmT_t], writes=[psmt])
                S.op("dve", lambda e, psm=psm, T=T: e.reciprocal(out=rinv[:, 0:T], in_=psm[:, 0:T]), reads=[psmt], writes=[rinv_t])
                for mc in range(2):
                    S.op("dve", lambda e, hm=hm, mc=mc, T=T: e.tensor_tensor(out=PmT[:, hm, mc, 0:T], in0=PmT[:, hm, mc, 0:T], in1=rinv[:, 0:T], op=ALU.mult),
                         reads=[PmT_t, rinv_t], writes=[PmT_t])
            for dc in range(16):
                g = dc // 4
                cc = dc % 4
                pp, ppt = cx.ps()
                S.op("pe", lambda e, pp=pp, g=g, cc=cc, T=T: e.matmul(pp[:, 0:T], lhsT=poolw_sb[:, g, cc * 128:(cc + 1) * 128], rhs=pooled[:, g, 0:T], start=True, stop=True),
                     reads=[poolw_t, pooled_t], writes=[ppt])
                pm, pmt = cx.ps()
                for mc in range(2):
                    S.op("pe", lambda e, pm=pm, dc=dc, g=g, mc=mc, T=T: e.matmul(pm[:, 0:T], lhsT=mv_sb[:, mc, dc * 128:(dc + 1) * 128], rhs=PmT[:, g, mc, 0:T], start=(mc == 0), stop=(mc == 1)),
                         reads=[mv_t, PmT_t], writes=[pmt])
                gts = []
                for br in range(3):
                    wb, wt = cast_load(w_ring, wgate[br * 16 + dc])
                    pg, pgt = cx.ps()
                    for c in range(16):
                        S.op("pe", lambda e, c=c, wb=wb, pg=pg, T=T: e.matmul(pg[:, 0:T], lhsT=wb[:, c, :], rhs=xn[:, c, 0:T], start=(c == 0), stop=(c == 15)),
                             reads=[wt, xn_t], writes=[pgt])
                    gt, gtt, _ = gate_ring.next()
                    S.op("act", lambda e, pg=pg, gt=gt, br=br, dc=dc, T=T: e.activation(out=gt[:, 0:T], in_=pg[:, 0:T], func=AF.Sigmoid,
                                                                                       bias=bgate_sb[:, br * 16 + dc:br * 16 + dc + 1]),
                         reads=[pgt, bgate_t], writes=[gtt])
                    gts.append((gt, gtt))
                S.op("dve", lambda e, dc=dc, gt=gts[0][0], T=T: e.tensor_tensor(out=mrg[:, 0:T], in0=at_sb[:, dc, 0:T], in1=gt[:, 0:T], op=ALU.mult),
                     reads=[at_t, gts[0][1]], writes=[mrg_t])
                S.op("dve", lambda e, dc=dc, pp=pp, gt=gts[1][0], T=T: stt(e, tmp1[:, 0:T], pp[:, 0:T], pscale_sb[:, dc:dc + 1], gt[:, 0:T]),
                     reads=[ppt, pscale_t, gts[1][1]], writes=[tmp1_t])
                S.op("dve", lambda e, T=T: e.tensor_tensor(out=mrg[:, 0:T], in0=mrg[:, 0:T], in1=tmp1[:, 0:T], op=ALU.add),
                     reads=[mrg_t, tmp1_t], writes=[mrg_t])
                S.op("dve", lambda e, pm=pm, gt=gts[2][0], T=T: e.tensor_tensor(out=tmp1[:, 0:T], in0=pm[:, 0:T], in1=gt[:, 0:T], op=ALU.mult),
                     reads=[pmt, gts[2][1]], writes=[tmp1_t])
                S.op("dve", lambda e, dc=dc, T=T: e.tensor_tensor(out=at_sb[:, dc, 0:T], in0=mrg[:, 0:T], in1=tmp1[:, 0:T], op=ALU.add),
                     reads=[mrg_t, tmp1_t], writes=[at_t])
            for dco in range(16):
                wb, wt = cast_load(w_ring, wout[dco])
                pw, pwt = cx.ps()
                for c in range(16):
                    S.op("pe", lambda e, c=c, wb=wb, pw=pw, T=T: e.matmul(pw[:, 0:T], lhsT=wb[:, c, :], rhs=at_sb[:, c, 0:T], start=(c == 0), stop=(c == 15)),
                         reads=[wt, at_t], writes=[pwt])
                S.op("dve", lambda e, dco=dco, pw=pw, T=T: e.tensor_tensor(out=x_sb[:, dco, 0:T], in0=x_sb[:, dco, 0:T], in1=pw[:, 0:T], op=ALU.add),
                     reads=[pwt, x_t], writes=[x_t])
            rmsnorm16(T, g2_sb, g2_t)
            for j in range(NFF):
                wgb, wgt = cast_load(w_ring, wg[j])
                wub, wut = cast_load(w_ring, wu[j])
                pg, pgt = cx.ps()
                pu, put = cx.ps()
                for c in range(16):
                    S.op("pe", lambda e, c=c, wgb=wgb, pg=pg, T=T: e.matmul(pg[:, 0:T], lhsT=wgb[:, c, :], rhs=xn[:, c, 0:T], start=(c == 0), stop=(c == 15)),
                         reads=[wgt, xn_t], writes=[pgt])
                for c in range(16):
                    S.op("pe", lambda e, c=c, wub=wub, pu=pu, T=T: e.matmul(pu[:, 0:T], lhsT=wub[:, c, :], rhs=xn[:, c, 0:T], start=(c == 0), stop=(c == 15)),
                         reads=[wut, xn_t], writes=[put])
                sl, slt, _ = silu_ring.next()
                S.op("act", lambda e, sl=sl, pg=pg, T=T: e.activation(out=sl[:, 0:T], in_=pg[:, 0:T], func=AF.Silu), reads=[pgt], writes=[slt])
                S.op("dve", lambda e, sl=sl, pu=pu, j=j, T=T: e.tensor_tensor(out=act[:, j, 0:T], in0=sl[:, 0:T], in1=pu[:, 0:T], op=ALU.mult),
                     reads=[slt, put], writes=[act_t])
            for c in range(16):
                wdb, wdt = cast_load(wd_ring, wd[c])
                po, pot = cx.ps()
                for j in range(NFF):
                    S.op("pe", lambda e, j=j, wdb=wdb, po=po, T=T: e.matmul(po[:, 0:T], lhsT=wdb[:, j, :], rhs=act[:, j, 0:T], start=(j == 0), stop=(j == NFF - 1)),
                         reads=[wdt, act_t], writes=[pot])
                S.op("dve", lambda e, c=c, po=po, T=T: stt(e, x_sb[:, c, 0:T], po[:, 0:T], 0.5, x_sb[:, c, 0:T], ALU.mult, ALU.add),
                     reads=[pot, x_t], writes=[x_t])
            cx.store(o_x[:, t0:t0 + T].rearrange("(c p) t -> p c t", p=128), x_sb[:, :, 0:T], [x_t])
        print("B2 sbuf remaining", nc.sbuf_bytes_remaining)
        cx.finish()
    return nc


_PROGS = {}


def _prog(name, fn):
    if name not in _PROGS:
        _PROGS[name] = fn()
    return _PROGS[name]


def _run(nc, in_maps):
    res = run_bass_kernel_spmd(nc, in_maps, core_ids=list(range(NCORES)))
    return [{k: np.asarray(v) for k, v in r.items()} for r in res.results]


def _icnt(i):
    pos = _positions(i)
    out = np.empty((4, NTOK), np.float32)
    for g, w in enumerate((2, 4, 8, 16)):
        out[g] = np.float32(1.0) / np.minimum(pos + 1, w).astype(np.float32)
    return np.ascontiguousarray(np.broadcast_to(out[None], (128, 4, NTOK)))


def kernel(**inp):
    inp = {k: np.asarray(v) for k, v in inp.items()}
    ncA = _prog("A", build_A)
    ncB1 = _prog("B1", build_B1)
    ncB2 = _prog("B2", build_B2)
    xT = [_initial_xT(inp, i) for i in range(NCORES)]
    tabs = [_rope_tables(i) for i in range(NCORES)]
    masks = [_mask_for_core(i) for i in range(NCORES)]
    icnts = [_icnt(i) for i in range(NCORES)]
    L = 2
    o_ckv_p = np.zeros((L, 1, SEQ, 256), np.float32)
    o_kr_p = np.zeros((L, 1, SEQ, 64), np.float32)
    o_pool_p = np.zeros((L, 1, 15, 512), np.float32)
    o_mk_p = np.zeros((L, 1, 256, 4, 128), np.float32)
    o_mv_p = np.zeros((L, 1, 256, 4, 512), np.float32)
    o_ckv_s = np.zeros((L, DEC_B, DEC_T, 256), np.float32)
    o_kr_s = np.zeros((L, DEC_B, DEC_T, 64), np.float32)
    o_pool_s = np.zeros((L, DEC_B, 15, 512), np.float32)
    for l in range(L):
        wA = _prep_A_weights(inp, l)
        maps = []
        for i in range(NCORES):
            m = dict(wA)
            m["xT"] = xT[i]
            m["cosT"], m["sinT"] = tabs[i]
            maps.append(m)
        rA = _run(ncA, maps)
        del maps, wA
        ckv_all = np.zeros((256, SEQ), np.float32)
        kr_all = np.zeros((64, SEQ), np.float32)
        pin_all = np.zeros((512, SEQ), np.float32)
        for i in range(NCORES):
            for s in range(4):
                g0 = 512 * (8 * s + i)
                ckv_all[:, g0:g0 + 512] = rA[i]["o_ckv"][:, 512 * s:512 * s + 512]
                kr_all[:, g0:g0 + 512] = rA[i]["o_kr"][:, 512 * s:512 * s + 512]
                pin_all[:, g0:g0 + 512] = rA[i]["o_pin"][:, 512 * s:512 * s + 512]
        o_ckv_p[l, 0] = ckv_all.T
        o_kr_p[l, 0] = kr_all.T
        o_pool_p[l, 0] = pin_all[:, SEQ - 15:].T
        o_mk_p[l, 0] = rA[0]["o_mk"].T.reshape(256, 4, 128)
        o_mv_p[l, 0] = rA[0]["o_mv"].reshape(256, 4, 512)
        for b in range(DEC_B):
            i, bb = b // 2, b % 2
            c0 = 2048 + 64 * bb
            o_ckv_s[l, b] = rA[i]["o_ckv"][:, c0:c0 + 64].T
            o_kr_s[l, b] = rA[i]["o_kr"][:, c0:c0 + 64].T
            o_pool_s[l, b] = rA[i]["o_pin"][:, c0 + 64 - 15:c0 + 64].T
        wuk = _slab(inp["w_uk"][l], 2, 16)
        wuv = _slab(inp["w_uv"][l], 2, 16)
        gk = np.ascontiguousarray(inp["k_norm"][l][:, None])
        maps = []
        for i in range(NCORES):
            m = {"qnT": rA[i]["o_qn"], "qrT": rA[i]["o_qr"], "ckvA": ckv_all, "krA": kr_all, "mask": masks[i],
                 "wuk": wuk, "wuv": wuv, "gk": gk}
            cs, ks = [], []
            for bb in range(2):
                b = 2 * i + bb
                c0 = 2048 + 64 * bb
                cs.append(np.concatenate([inp["cache_ckv"][l, b].T, rA[i]["o_ckv"][:, c0:c0 + 64]], axis=1))
                ks.append(np.concatenate([inp["cache_krope"][l, b].T, rA[i]["o_kr"][:, c0:c0 + 64]], axis=1))
            m["ckvS"] = np.ascontiguousarray(np.stack(cs))
            m["krS"] = np.ascontiguousarray(np.stack(ks))
            maps.append(m)
        rB1 = _run(ncB1, maps)
        del maps
        wB = {
            "gmix": _gain(inp["mix_norm"][l], 16),
            "poolw": np.ascontiguousarray(inp["pool_w"][l]),
            "pscale": _gain(inp["pool_scale"][l], 16),
            "wgate": _slab(inp["w_gate"][l], 16, 48),
            "bgate": _gain(inp["b_gate"][l], 48),
            "wout": _slab(inp["w_out"][l], 16, 16),
            "g2": _gain(inp["ffn2_norm"][l], 16),
            "wg": _slab(inp["ffn2_wg"][l], 16, NFF),
            "wu": _slab(inp["ffn2_wu"][l], 16, NFF),
            "wd": _slab(inp["ffn2_wd"][l], NFF, 16),
        }
        maps = []
        for i in range(NCORES):
            m = dict(wB)
            m["hT"] = rA[i]["o_h"]
            m["atT"] = rB1[i]["o_at"]
            m["qmT"] = rA[i]["o_qm"]
            segs = []
            for s in range(4):
                g0 = 512 * (8 * s + i)
                halo = pin_all[:, g0 - 15:g0] if g0 > 0 else np.zeros((512, 15), np.float32)
                segs.append(np.concatenate([halo, pin_all[:, g0:g0 + 512]], axis=1))
            mks = [rA[0]["o_mk"]]
            mvs = [rA[0]["o_mv"]]
            for bb in range(2):
                b = 2 * i + bb
                c0 = 2048 + 64 * bb
                segs.append(np.concatenate([inp["state_pool"][l, b].T, rA[i]["o_pin"][:, c0:c0 + 64]], axis=1))
                mks.append(inp["cache_mem_k"][l, b].reshape(256, 512).T)
                mvs.append(inp["cache_mem_v"][l, b].reshape(256, D))
            m["pinE"] = np.ascontiguousarray(np.concatenate(segs, axis=1))
            m["icnt"] = icnts[i]
            m["mk3"] = np.ascontiguousarray(np.stack(mks))
            m["mv3"] = np.ascontiguousarray(np.stack(mvs))
            maps.append(m)
        rB2 = _run(ncB2, maps)
        del maps, wB
        xT = [rB2[i]["o_x"] for i in range(NCORES)]
    y_p = np.zeros((1, SEQ, D), np.float32)
    y_s = np.zeros((DEC_B, DEC_T, D), np.float32)
    for i in range(NCORES):
        y_p[0, _tok_index(i)] = xT[i][:, 0:2048].T
        for bb in range(2):
            y_s[2 * i + bb] = xT[i][:, 2048 + 64 * bb:2048 + 64 * bb + 64].T
    return (y_p, y_s, o_ckv_p, o_kr_p, o_pool_p, o_mk_p, o_mv_p, o_ckv_s, o_kr_s, o_pool_s)
```

```python
import numpy as np
import ml_dtypes
from contextlib import ExitStack
import concourse.bass as bass
import concourse.mybir as mybir
from concourse.bass_utils import run_bass_kernel_spmd

F32 = mybir.dt.float32
BF16 = mybir.dt.bfloat16
AF = mybir.ActivationFunctionType
ALU = mybir.AluOpType

NCORES = 8
D = 2048
DFF = 5632
NFF = DFF // 128
SEQ = 16384
PAST = 4096
DEC_B = 16
DEC_T = 64
NTOK = 2176
TILES = [(0, 512), (512, 512), (1024, 512), (1536, 512), (2048, 128)]
EPS = 1e-6
IN_W = 1856


class Tile:
    __slots__ = ("name", "w", "r")

    def __init__(self, name=""):
        self.name = name
        self.w = None
        self.r = []


class DSem:
    def __init__(self, sem):
        self.sem = sem
        self.count = 0


class Op:
    __slots__ = ("eng", "fn", "deps", "marked", "val", "dsem", "dval")


class Sched:
    ENGS = ("pe", "act", "dve", "pool", "sp")

    def __init__(self, nc, es):
        self.nc = nc
        self.es = es
        self.ops = {e: [] for e in self.ENGS}
        self.sems = {e: es.enter_context(nc.semaphore("s_" + e)) for e in self.ENGS}
        self.n_dsem = 0

    def dsem(self):
        self.n_dsem += 1
        return DSem(self.es.enter_context(self.nc.semaphore("d%d" % self.n_dsem)))

    def _deps(self, reads, writes):
        deps = []
        for t in reads:
            if t.w is not None:
                deps.append(t.w)
        for t in writes:
            if t.w is not None:
                deps.append(t.w)
            deps.extend(t.r)
        return deps

    def _reg(self, o, tok, reads, writes):
        for t in reads:
            t.r.append(tok)
        for t in writes:
            t.w = tok
            t.r = []

    def op(self, eng, fn, reads=(), writes=()):
        o = Op()
        o.eng = eng
        o.fn = fn
        o.deps = self._deps(reads, writes)
        o.marked = False
        o.val = None
        o.dsem = None
        self.ops[eng].append(o)
        self._reg(o, ("e", o), reads, writes)
        return o

    def dma(self, eng, fn, dsem, reads=(), writes=()):
        o = Op()
        o.eng = eng
        o.fn = fn
        o.deps = self._deps(reads, writes)
        o.marked = False
        o.val = None
        o.dsem = dsem
        dsem.count += 16
        o.dval = dsem.count
        self.ops[eng].append(o)
        self._reg(o, ("d", dsem, o.dval), reads, writes)
        return o

    def emit(self, final_dsems=()):
        nc = self.nc
        for e in self.ENGS:
            for o in self.ops[e]:
                for d in o.deps:
                    if d[0] == "e":
                        p = d[1]
                        if p.eng == "pe" and o.eng == "pe" and o.dsem is None:
                            continue
                        p.marked = True
        for e in self.ENGS:
            c = 0
            for o in self.ops[e]:
                if o.marked:
                    assert o.dsem is None
                    c += 1
                    o.val = c
        sems = self.sems
        with nc.Block() as block:
            def run(engname, engine):
                waited = {}
                for o in self.ops[engname]:
                    need = {}
                    for d in o.deps:
                        if d[0] == "e":
                            p = d[1]
                            if p.eng == "pe" and engname == "pe" and o.dsem is None:
                                continue
                            key = ("e", p.eng)
                            v = p.val
                            s = sems[p.eng]
                        else:
                            key = ("d", id(d[1]))
                            v = d[2]
                            s = d[1].sem
                        if waited.get(key, 0) >= v:
                            continue
                        if key not in need or need[key][1] < v:
                            need[key] = (s, v)
                    for key, (s, v) in need.items():
                        engine.wait_ge(s, v)
                        waited[key] = v
                    ins = o.fn(engine)
                    if o.dsem is not None:
                        ins.then_inc(o.dsem.sem, 16)
                    elif o.marked:
                        ins.then_inc(sems[engname], 1)
                if engname == "sp":
                    for ds in final_dsems:
                        engine.wait_ge(ds.sem, ds.count)

            @block.tensor
            def _(e):
                run("pe", e)

            @block.scalar
            def _(e):
                run("act", e)

            @block.vector
            def _(e):
                run("dve", e)

            @block.gpsimd
            def _(e):
                run("pool", e)

            @block.sync
            def _(e):
                run("sp", e)


class Ctx:
    def __init__(self, nc, es):
        self.nc = nc
        self.es = es
        self.S = Sched(nc, es)
        self.n = 0
        self.psb = []
        self.pst = []
        for i in range(8):
            self.psb.append(es.enter_context(nc.psum_tensor("ps%d" % i, [128, 512], F32)))
            self.pst.append(Tile("ps%d" % i))
        self.ps_i = 0
        self.store_ds = {}

    def sb(self, shape, dt, name=None):
        self.n += 1
        return self.es.enter_context(self.nc.sbuf_tensor(name or ("t%d" % self.n), shape, dt))

    def ps(self, lo=0, hi=6):
        k = lo + self.ps_i % (hi - lo)
        self.ps_i += 1
        return self.psb[k], self.pst[k]

    def ring(self, shape, dt, n):
        return Ring(self, shape, dt, n)

    def inp(self, name, shape, dt=F32):
        return self.nc.dram_tensor(name, list(shape), dt, kind="ExternalInput").ap()

    def outp(self, name, shape, dt=F32):
        return self.nc.dram_tensor(name, list(shape), dt, kind="ExternalOutput").ap()

    def store(self, out_ap, in_ap, reads):
        key = id(reads[0])
        if key not in self.store_ds:
            self.store_ds[key] = self.S.dsem()
        self.S.dma("sp", lambda e: e.dma_start(out=out_ap, in_=in_ap), self.store_ds[key], reads=reads)

    def finish(self):
        self.S.emit(final_dsems=list(self.store_ds.values()))

    def load_const(self, dram_ap, shape, dt, eng=None):
        t = self.sb(shape, dt)
        tl = Tile()
        ds = self.S.dsem()
        src_dt_differs = (dt != F32)
        q = "pool" if src_dt_differs else "sp"
        self.S.dma(q, lambda e: e.dma_start(out=t[:], in_=dram_ap), ds, writes=[tl])
        return t, tl


class Ring:
    def __init__(self, cx, shape, dt, n):
        self.bufs = [cx.sb(shape, dt) for _ in range(n)]
        self.tiles = [Tile() for _ in range(n)]
        self.dsems = [cx.S.dsem() for _ in range(n)]
        self.n = n
        self.i = 0

    def next(self):
        k = self.i % self.n
        self.i += 1
        return self.bufs[k], self.tiles[k], self.dsems[k]


def rms_stats(cx, srcs, src_tiles, T, Dn, ones_ap, ones_tile, sq_ring, rstd, rstd_tile, npart=128):
    S = cx.S
    pn, pnt = cx.ps(6, 8)
    n = len(srcs)
    for c, (src, st) in enumerate(zip(srcs, src_tiles)):
        sq, sqt, _ = sq_ring.next()
        S.op("act", lambda e, sq=sq, src=src: e.activation(out=sq[0:npart, 0:T], in_=src, func=AF.Square),
             reads=[st], writes=[sqt])
        S.op("pe", lambda e, sq=sq, c=c: e.matmul(pn[0:npart, 0:T], lhsT=ones_ap, rhs=sq[0:npart, 0:T],
                                                  start=(c == 0), stop=(c == n - 1)),
             reads=[sqt, ones_tile], writes=[pnt])
    S.op("act", lambda e: e.activation(out=rstd[0:npart, 0:T], in_=pn[0:npart, 0:T], func=AF.Sqrt, bias=EPS, scale=1.0 / Dn),
         reads=[pnt], writes=[rstd_tile])
    S.op("dve", lambda e: e.reciprocal(out=rstd[0:npart, 0:T], in_=rstd[0:npart, 0:T]),
         reads=[rstd_tile], writes=[rstd_tile])


def stt(e, out, in0, scalar, in1, op0=ALU.mult, op1=ALU.mult):
    return e.scalar_tensor_tensor(out=out, in0=in0, scalar=scalar, in1=in1, op0=op0, op1=op1)


def build_A():
    nc = bass.Bass("TRN2", target_bir_lowering=False)
    with ExitStack() as es:
        cx = Ctx(nc, es)
        S = cx.S
        xT = cx.inp("xT", [D, NTOK])
        g1 = cx.inp("g1", [128, 16])
        wg = cx.inp("wg", [NFF, 128, 16, 128])
        wu = cx.inp("wu", [NFF, 128, 16, 128])
        wd = cx.inp("wd", [16, 128, NFF, 128])
        gmix = cx.inp("gmix", [128, 16])
        win = cx.inp("win", [15, 128, 16, 128])
        gql = cx.inp("gql", [128, 4])
        gkvl = cx.inp("gkvl", [128, 2])
        gq = cx.inp("gq", [128, 1])
        gqr = cx.inp("gqr", [128, 1])
        gkr = cx.inp("gkr", [128, 1])
        gmq = cx.inp("gmq", [128, 1])
        wuq = cx.inp("wuq", [16, 128, 4, 128])
        wuqr = cx.inp("wuqr", [8, 128, 4, 128])
        cosT = cx.inp("cosT", [128, NTOK])
        sinT = cx.inp("sinT", [128, NTOK])
        rotT = cx.inp("rotT", [128, 128])
        memT = cx.inp("memT", [D, 256])
        gmem = cx.inp("gmem", [128, 16])
        wmk = cx.inp("wmk", [4, 128, 16, 128])
        gmk = cx.inp("gmk", [128, 1])
        wmv = cx.inp("wmv", [4, 128, 16, 512])
        o_h = cx.outp("o_h", [D, NTOK])
        o_qn = cx.outp("o_qn", [D, NTOK], BF16)
        o_qr = cx.outp("o_qr", [1024, NTOK], BF16)
        o_ckv = cx.outp("o_ckv", [256, NTOK])
        o_kr = cx.outp("o_kr", [64, NTOK])
        o_pin = cx.outp("o_pin", [512, NTOK])
        o_qm = cx.outp("o_qm", [512, NTOK], BF16)
        o_mk = cx.outp("o_mk", [512, 256])
        o_mv = cx.outp("o_mv", [256, D])

        g1_sb, g1_t = cx.load_const(g1, [128, 16], F32)
        gmix_sb, gmix_t = cx.load_const(gmix, [128, 16], F32)
        gql_sb, gql_t = cx.load_const(gql, [128, 4], F32)
        gkvl_sb, gkvl_t = cx.load_const(gkvl, [128, 2], F32)
        gq_sb, gq_t = cx.load_const(gq, [128, 1], F32)
        gqr_sb, gqr_t = cx.load_const(gqr, [128, 1], F32)
        gkr_sb, gkr_t = cx.load_const(gkr, [128, 1], F32)
        gmq_sb, gmq_t = cx.load_const(gmq, [128, 1], F32)
        gmem_sb, gmem_t = cx.load_const(gmem, [128, 16], F32)
        gmk_sb, gmk_t = cx.load_const(gmk, [128, 1], F32)
        rot_bf, rot_bf_t = cx.load_const(rotT, [128, 128], BF16)
        rot_f, rot_f_t = cx.load_const(rotT, [128, 128], F32)
        ones = cx.sb([128, 128], BF16)
        ones_t = Tile()
        S.op("dve", lambda e: e.memset(ones[:], 1.0), writes=[ones_t])
        ones2 = cx.sb([128, 128], BF16)
        ones2_t = Tile()
        S.op("dve", lambda e: e.memset(ones2[:], 0.0), writes=[ones2_t])
        S.op("dve", lambda e: e.memset(ones2[0:64, 0:64], 1.0), reads=[ones2_t], writes=[ones2_t])
        S.op("dve", lambda e: e.memset(ones2[64:128, 64:128], 1.0), reads=[ones2_t], writes=[ones2_t])

        onesf = cx.sb([64, 64], F32)
        onesf_t = Tile()
        S.op("dve", lambda e: e.memset(onesf[:], 1.0), writes=[onesf_t])
        x_sb = cx.sb([128, 16, 512], F32); x_t = Tile()
        xn = cx.sb([128, 16, 512], BF16); xn_t = Tile()
        act = cx.sb([128, NFF, 512], BF16); act_t = Tile()
        proj = cx.sb([128, 15, 512], F32); proj_t = Tile()
        cq = cx.sb([128, 4, 512], BF16); cq_t = Tile()
        ckv_sb = cx.sb([128, 2, 512], F32); ckv_t = Tile()
        qm_sb = cx.sb([128, 4, 512], BF16); qm_t = Tile()
        cos_sb = cx.sb([128, 512], F32); cos_t = Tile()
        sin_sb = cx.sb([128, 512], F32); sin_t = Tile()
        krn = cx.sb([64, 512], F32); krn_t = Tile()
        kro = cx.sb([64, 512], F32); kro_t = Tile()
        tmpf = cx.sb([128, 512], F32); tmpf_t = Tile()
        rstd = cx.sb([128, 512], F32); rstd_t = Tile()
        silu_ring = cx.ring([128, 512], BF16, 2)
        sq_ring = cx.ring([128, 512], BF16, 3)
        sqf_ring = cx.ring([128, 512], F32, 1)
        qo_ring = cx.ring([128, 512], BF16, 2)
        qrn_ring = cx.ring([128, 512], BF16, 2)
        w_ring = cx.ring([128, 16, 128], BF16, 5)
        wd_ring = cx.ring([128, NFF, 128], BF16, 2)
        wq_ring = cx.ring([128, 4, 128], BF16, 4)
        cs_ds = S.dsem()
        cos_ds = S.dsem()
        sin_ds = S.dsem()

        def cast_load(ring, src_ap, shape_slice=None):
            buf, tl, ds = ring.next()
            dst = buf[:] if shape_slice is None else shape_slice(buf)
            S.dma("pool", lambda e: e.dma_start(out=dst, in_=src_ap), ds, writes=[tl])
            return buf, tl

        def rmsnorm_chunks(src_sb, src_t, C, T, g_sb, g_t, out_sb, out_t, c0=0):
            rms_stats(cx, [src_sb[:, c0 + c, 0:T] for c in range(C)], [src_t] * C, T, 128.0 * C,
                      ones[:], ones_t, sq_ring, rstd, rstd_t)
            for c in range(C):
                S.op("dve", lambda e, c=c: stt(e, out_sb[:, c, 0:T], src_sb[:, c0 + c, 0:T], g_sb[:, c:c + 1], rstd[:, 0:T]),
                     reads=[src_t, g_t, rstd_t], writes=[out_t])

        m_sb = x_sb; m_t = x_t
        mn = xn; mn_t = xn_t
        S.dma("sp", lambda e: e.dma_start(out=m_sb[:, :, 0:256], in_=memT.rearrange("(c p) t -> p c t", p=128)), cs_ds, writes=[m_t])
        rmsnorm_chunks(m_sb, m_t, 16, 256, gmem_sb, gmem_t, mn, mn_t)
        mk_sb = proj; mk_t = proj_t
        for h in range(4):
            wb, wt = cast_load(w_ring, wmk[h])
            pk, pkt = cx.ps()
            for c in range(16):
                S.op("pe", lambda e, c=c, wb=wb, pk=pk: e.matmul(pk[:, 0:256], lhsT=wb[:, c, :], rhs=mn[:, c, 0:256], start=(c == 0), stop=(c == 15)),
                     reads=[wt, mn_t], writes=[pkt])
            rms_stats(cx, [pk[:, 0:256]], [pkt], 256, 128.0, ones[:], ones_t, sq_ring, rstd, rstd_t)
            S.op("dve", lambda e, h=h, pk=pk: stt(e, mk_sb[:, h, 0:256], pk[:, 0:256], gmk_sb[:, 0:1], rstd[:, 0:256]),
                 reads=[pkt, gmk_t, rstd_t], writes=[mk_t])
        cx.store(o_mk.rearrange("(c p) t -> p c t", p=128), mk_sb[:, 0:4, 0:256], [mk_t])
        mv_t = proj_t
        wmv_ds = S.dsem()
        for gcol in range(4):
            wb, wt = act, act_t
            S.dma("pool", lambda e, gcol=gcol: e.dma_start(out=act[:, 0:16, :], in_=wmv[gcol]), wmv_ds, writes=[act_t])
            for mc in range(2):
                pv, pvt = cx.ps()
                for c in range(16):
                    S.op("pe", lambda e, c=c, wb=wb, pv=pv, mc=mc: e.matmul(pv[:, :], lhsT=mn[:, c, mc * 128:(mc + 1) * 128], rhs=wb[:, c, :],
                                                                            start=(c == 0), stop=(c == 15)),
                         reads=[wt, mn_t], writes=[pvt])
                S.op("act", lambda e, pv=pv, mc=mc, gcol=gcol: e.activation(out=proj[:, 4 + mc * 4 + gcol, :], in_=pv[:, :], func=AF.Copy),
                     reads=[pvt], writes=[mv_t])
        for mc in range(2):
            cx.store(o_mv[mc * 128:(mc + 1) * 128, :].rearrange("p (g f) -> p g f", f=512), proj[:, 4 + mc * 4:8 + mc * 4, :], [mv_t])

        for (t0, T) in TILES:
            S.dma("sp", lambda e, t0=t0, T=T: e.dma_start(out=x_sb[:, :, 0:T], in_=xT[:, t0:t0 + T].rearrange("(c p) t -> p c t", p=128)),
                  cs_ds, writes=[x_t])
            S.dma("sp", lambda e, t0=t0, T=T: e.dma_start(out=cos_sb[:, 0:T], in_=cosT[:, t0:t0 + T]), cos_ds, writes=[cos_t])
            S.dma("sp", lambda e, t0=t0, T=T: e.dma_start(out=sin_sb[:, 0:T], in_=sinT[:, t0:t0 + T]), sin_ds, writes=[sin_t])
            rmsnorm_chunks(x_sb, x_t, 16, T, g1_sb, g1_t, xn, xn_t)
            for j in range(NFF):
                wgb, wgt = cast_load(w_ring, wg[j])
                wub, wut = cast_load(w_ring, wu[j])
                pg, pgt = cx.ps()
                pu, put = cx.ps()
                for c in range(16):
                    S.op("pe", lambda e, c=c, wgb=wgb, pg=pg, T=T: e.matmul(pg[:, 0:T], lhsT=wgb[:, c, :], rhs=xn[:, c, 0:T], start=(c == 0), stop=(c == 15)),
                         reads=[wgt, xn_t], writes=[pgt])
                for c in range(16):
                    S.op("pe", lambda e, c=c, wub=wub, pu=pu, T=T: e.matmul(pu[:, 0:T], lhsT=wub[:, c, :], rhs=xn[:, c, 0:T], start=(c == 0), stop=(c == 15)),
                         reads=[wut, xn_t], writes=[put])
                sl, slt, _ = silu_ring.next()
                S.op("act", lambda e, sl=sl, pg=pg, T=T: e.activation(out=sl[:, 0:T], in_=pg[:, 0:T], func=AF.Silu), reads=[pgt], writes=[slt])
                S.op("dve", lambda e, sl=sl, pu=pu, j=j, T=T: e.tensor_tensor(out=act[:, j, 0:T], in0=sl[:, 0:T], in1=pu[:, 0:T], op=ALU.mult),
                     reads=[slt, put], writes=[act_t])
            for c in range(16):
                wdb, wdt = cast_load(wd_ring, wd[c])
                po, pot = cx.ps()
                for j in range(NFF):
                    S.op("pe", lambda e, j=j, wdb=wdb, po=po, T=T: e.matmul(po[:, 0:T], lhsT=wdb[:, j, :], rhs=act[:, j, 0:T], start=(j == 0), stop=(j == NFF - 1)),
                         reads=[wdt, act_t], writes=[pot])
                S.op("dve", lambda e, c=c, po=po, T=T: stt(e, x_sb[:, c, 0:T], po[:, 0:T], 0.5, x_sb[:, c, 0:T], ALU.mult, ALU.add),
                     reads=[pot, x_t], writes=[x_t])
            cx.store(o_h[:, t0:t0 + T].rearrange("(c p) t -> p c t", p=128), x_sb[:, :, 0:T], [x_t])
            rmsnorm_chunks(x_sb, x_t, 16, T, gmix_sb, gmix_t, xn, xn_t)
            for j in range(15):
                wb, wt = cast_load(w_ring, win[j])
                pp, ppt = cx.ps()
                for c in range(16):
                    S.op("pe", lambda e, c=c, wb=wb, pp=pp, T=T: e.matmul(pp[:, 0:T], lhsT=wb[:, c, :], rhs=xn[:, c, 0:T], start=(c == 0), stop=(c == 15)),
                         reads=[wt, xn_t], writes=[ppt])
                S.op("act", lambda e, j=j, pp=pp, T=T: e.activation(out=proj[:, j, 0:T], in_=pp[:, 0:T], func=AF.Copy), reads=[ppt], writes=[proj_t])
            cx.store(o_pin[:, t0:t0 + T].rearrange("(c p) t -> p c t", p=128), proj[:, 7:11, 0:T], [proj_t])
            rmsnorm_chunks(proj, proj_t, 4, T, gql_sb, gql_t, cq, cq_t, c0=0)
            rmsnorm_chunks(proj, proj_t, 2, T, gkvl_sb, gkvl_t, ckv_sb, ckv_t, c0=4)
            cx.store(o_ckv[:, t0:t0 + T].rearrange("(c p) t -> p c t", p=128), ckv_sb[:, :, 0:T], [ckv_t])
            sqf, sqft, _ = sqf_ring.next()
            pn, pnt = cx.ps(6, 8)
            S.op("act", lambda e, T=T: e.activation(out=sqf[0:64, 0:T], in_=proj[0:64, 6, 0:T], func=AF.Square), reads=[proj_t], writes=[sqft])
            S.op("pe", lambda e, T=T, pn=pn: e.matmul(pn[0:64, 0:T], lhsT=onesf[0:64, 0:64], rhs=sqf[0:64, 0:T], start=True, stop=True),
                 reads=[sqft, onesf_t], writes=[pnt])
            S.op("act", lambda e, T=T, pn=pn: e.activation(out=rstd[0:64, 0:T], in_=pn[0:64, 0:T], func=AF.Sqrt, bias=EPS, scale=1.0 / 64),
                 reads=[pnt], writes=[rstd_t])
            S.op("dve", lambda e, T=T: e.reciprocal(out=rstd[0:64, 0:T], in_=rstd[0:64, 0:T]), reads=[rstd_t], writes=[rstd_t])
            S.op("dve", lambda e, T=T: stt(e, krn[:, 0:T], proj[0:64, 6, 0:T], gkr_sb[0:64, 0:1], rstd[0:64, 0:T]),
                 reads=[proj_t, gkr_t, rstd_t], writes=[krn_t])
            pr, prt = cx.ps()
            S.op("pe", lambda e, T=T, pr=pr: e.matmul(pr[0:64, 0:T], lhsT=rot_f[0:64, 0:64], rhs=krn[:, 0:T], start=True, stop=True),
                 reads=[krn_t, rot_f_t], writes=[prt])
            S.op("dve", lambda e, T=T, pr=pr: e.tensor_tensor(out=tmpf[0:64, 0:T], in0=pr[0:64, 0:T], in1=sin_sb[0:64, 0:T], op=ALU.mult),
                 reads=[prt, sin_t], writes=[tmpf_t])
            S.op("dve", lambda e, T=T: e.tensor_tensor(out=kro[:, 0:T], in0=krn[:, 0:T], in1=cos_sb[0:64, 0:T], op=ALU.mult),
                 reads=[krn_t, cos_t], writes=[kro_t])
            S.op("dve", lambda e, T=T: e.tensor_tensor(out=kro[:, 0:T], in0=kro[:, 0:T], in1=tmpf[0:64, 0:T], op=ALU.add),
                 reads=[kro_t, tmpf_t], writes=[kro_t])
            cx.store(o_kr[:, t0:t0 + T], kro[:, 0:T], [kro_t])
            for h in range(4):
                rms_stats(cx, [proj[:, 11 + h, 0:T]], [proj_t], T, 128.0, ones[:], ones_t, sq_ring, rstd, rstd_t)
                S.op("dve", lambda e, h=h, T=T: stt(e, qm_sb[:, h, 0:T], proj[:, 11 + h, 0:T], gmq_sb[:, 0:1], rstd[:, 0:T]),
                     reads=[proj_t, gmq_t, rstd_t], writes=[qm_t])
            cx.store(o_qm[:, t0:t0 + T].rearrange("(c p) t -> p c t", p=128), qm_sb[:, :, 0:T], [qm_t])
            for h in range(16):
                wb, wt = cast_load(wq_ring, wuq[h])
                pq, pqt = cx.ps()
                for c in range(4):
                    S.op("pe", lambda e, c=c, wb=wb, pq=pq, T=T: e.matmul(pq[:, 0:T], lhsT=wb[:, c, :], rhs=cq[:, c, 0:T], start=(c == 0), stop=(c == 3)),
                         reads=[wt, cq_t], writes=[pqt])
                rms_stats(cx, [pq[:, 0:T]], [pqt], T, 128.0, ones[:], ones_t, sq_ring, rstd, rstd_t)
                qo, qot, _ = qo_ring.next()
                S.op("dve", lambda e, pq=pq, qo=qo, T=T: stt(e, qo[:, 0:T], pq[:, 0:T], gq_sb[:, 0:1], rstd[:, 0:T]),
                     reads=[pqt, gq_t, rstd_t], writes=[qot])
                cx.store(o_qn[h * 128:(h + 1) * 128, t0:t0 + T], qo[:, 0:T], [qot])
            for hp in range(8):
                wb, wt = cast_load(wq_ring, wuqr[hp])
                pq, pqt = cx.ps()
                for c in range(4):
                    S.op("pe", lambda e, c=c, wb=wb, pq=pq, T=T: e.matmul(pq[:, 0:T], lhsT=wb[:, c, :], rhs=cq[:, c, 0:T], start=(c == 0), stop=(c == 3)),
                         reads=[wt, cq_t], writes=[pqt])
                rms_stats(cx, [pq[:, 0:T]], [pqt], T, 64.0, ones2[:], ones2_t, sq_ring, rstd, rstd_t)
                qrn, qrnt, _ = qrn_ring.next()
                S.op("dve", lambda e, pq=pq, qrn=qrn, T=T: stt(e, qrn[:, 0:T], pq[:, 0:T], gqr_sb[:, 0:1], rstd[:, 0:T]),
                     reads=[pqt, gqr_t, rstd_t], writes=[qrnt])
                pr, prt = cx.ps()
                S.op("pe", lambda e, pr=pr, qrn=qrn, T=T: e.matmul(pr[:, 0:T], lhsT=rot_bf[:], rhs=qrn[:, 0:T], start=True, stop=True),
                     reads=[qrnt, rot_bf_t], writes=[prt])
                S.op("dve", lambda e, pr=pr, T=T: e.tensor_tensor(out=tmpf[:, 0:T], in0=pr[:, 0:T], in1=sin_sb[:, 0:T], op=ALU.mult),
                     reads=[prt, sin_t], writes=[tmpf_t])
                qo, qot, _ = qo_ring.next()
                S.op("dve", lambda e, qrn=qrn, qo=qo, T=T: e.tensor_tensor(out=qo[:, 0:T], in0=qrn[:, 0:T], in1=cos_sb[:, 0:T], op=ALU.mult),
                     reads=[qrnt, cos_t], writes=[qot])
                S.op("dve", lambda e, qo=qo, T=T: e.tensor_tensor(out=qo[:, 0:T], in0=qo[:, 0:T], in1=tmpf[:, 0:T], op=ALU.add),
                     reads=[qot, tmpf_t], writes=[qot])
                cx.store(o_qr[hp * 128:(hp + 1) * 128, t0:t0 + T], qo[:, 0:T], [qot])
        print("A sbuf remaining", nc.sbuf_bytes_remaining)
        cx.finish()
    return nc


def _tok_index(i):
    return np.concatenate([np.arange(512 * (8 * s + i), 512 * (8 * s + i) + 512) for s in range(4)])


def _positions(i):
    return np.concatenate([_tok_index(i), PAST + np.arange(DEC_T), PAST + np.arange(DEC_T)]).astype(np.int64)


def _rope_tables(i):
    half = 32
    inv = np.power(np.float32(10000.0), -np.arange(half, dtype=np.float32) / np.float32(half)).astype(np.float32)
    pos = _positions(i).astype(np.float32)
    ang = (pos[:, None] * inv[None, :]).astype(np.float32)
    cos = np.cos(ang).astype(np.float32).T
    sin = np.sin(ang).astype(np.float32).T
    return np.ascontiguousarray(np.tile(cos, (4, 1))), np.ascontiguousarray(np.tile(sin, (4, 1)))


def _rot_matrix():
    r = np.zeros((128, 128), np.float32)
    for b in range(2):
        for m in range(32):
            r[b * 64 + m + 32, b * 64 + m] = -1.0
        for m in range(32, 64):
            r[b * 64 + m - 32, b * 64 + m] = 1.0
    return r


def _slab(w, kc, nj, f=128):
    return np.ascontiguousarray(w.reshape(kc, 128, nj, f).transpose(2, 1, 0, 3))


def _gain(g, c):
    return np.ascontiguousarray(g.reshape(c, 128).T)


WIN_STARTS = [0, 128, 256, 384, 512, 640, 768, 832, 960, 1088, 1216, 1344, 1472, 1600, 1728]
WIN_WIDTHS = [128] * 6 + [64] + [128] * 8


def _prep_A_weights(inp, l):
    w = {}
    w["g1"] = _gain(inp["ffn1_norm"][l], 16)
    w["wg"] = _slab(inp["ffn1_wg"][l], 16, NFF)
    w["wu"] = _slab(inp["ffn1_wu"][l], 16, NFF)
    w["wd"] = _slab(inp["ffn1_wd"][l], NFF, 16)
    w["gmix"] = _gain(inp["mix_norm"][l], 16)
    win = inp["w_in"][l]
    wp = np.zeros((D, 15 * 128), np.float32)
    for j, (s0, wd_) in enumerate(zip(WIN_STARTS, WIN_WIDTHS)):
        wp[:, j * 128:j * 128 + wd_] = win[:, s0:s0 + wd_]
    w["win"] = _slab(wp, 16, 15)
    w["gql"] = _gain(inp["q_lat_norm"][l], 4)
    w["gkvl"] = _gain(inp["kv_lat_norm"][l], 2)
    w["gq"] = np.ascontiguousarray(inp["q_norm"][l][:, None])
    w["gqr"] = np.ascontiguousarray(np.concatenate([inp["qr_norm"][l]] * 2)[:, None])
    w["gkr"] = np.ascontiguousarray(np.concatenate([inp["kr_norm"][l]] * 2)[:, None])
    w["gmq"] = np.ascontiguousarray(inp["mq_norm"][l][:, None])
    w["wuq"] = _slab(inp["w_uq"][l], 4, 16)
    w["wuqr"] = _slab(inp["w_uqr"][l], 4, 8)
    w["rotT"] = _rot_matrix()
    w["memT"] = np.ascontiguousarray(inp["mem_prompt"][0].T)
    w["gmem"] = _gain(inp["mem_norm"][l], 16)
    w["wmk"] = _slab(inp["w_mk"][l], 16, 4)
    w["gmk"] = np.ascontiguousarray(inp["mk_norm"][l][:, None])
    w["wmv"] = _slab(inp["w_mv"][l], 16, 4, 512)
    return w


def _initial_xT(inp, i):
    xp = inp["x_prompt"][0][_tok_index(i)]
    xs = inp["x_sample"][2 * i:2 * i + 2].reshape(2 * DEC_T, D)
    return np.ascontiguousarray(np.concatenate([xp, xs], 0).T)


SCALE = float((128 + 64) ** -0.5)
S_KEYS = PAST + DEC_T


def build_B1():
    nc = bass.Bass("TRN2", target_bir_lowering=False)
    with ExitStack() as es:
        cx = Ctx(nc, es)
        S = cx.S
        qnT = cx.inp("qnT", [D, NTOK], BF16)
        qrT = cx.inp("qrT", [1024, NTOK], BF16)
        ckvA = cx.inp("ckvA", [256, SEQ])
        krA = cx.inp("krA", [64, SEQ])
        ckvS = cx.inp("ckvS", [2, 256, S_KEYS])
        krS = cx.inp("krS", [2, 64, S_KEYS])
        maskd = cx.inp("mask", [128, 32, 512])
        wuk = cx.inp("wuk", [16, 128, 2, 128])
        wuv = cx.inp("wuv", [16, 128, 2, 128])
        gk = cx.inp("gk", [128, 1])
        o_at = cx.outp("o_at", [D, NTOK], BF16)

        gk_sb, gk_t = cx.load_const(gk, [128, 1], F32)
        mask_sb, mask_t = cx.load_const(maskd, [128, 32, 512], BF16)
        krA_sb, krA_t = cx.load_const(krA, [64, SEQ], BF16)
        krS_sb = []
        for b in range(2):
            krS_sb.append(cx.load_const(krS[b], [64, S_KEYS], BF16))
        ones = cx.sb([128, 128], BF16); ones_t = Tile()
        S.op("dve", lambda e: e.memset(ones[:], 1.0), writes=[ones_t])
        KT = cx.sb([128, SEQ], BF16); KT_t = Tile()
        V = cx.sb([128, 128, 128], BF16); V_t = Tile()
        rstd = cx.sb([128, 512], F32); rstd_t = Tile()
        rinv = cx.sb([128, 512], F32); rinv_t = Tile()
        sq_ring = cx.ring([128, 512], BF16, 2)
        ck_ring = cx.ring([128, 2, 512], BF16, 3)
        p_ring = cx.ring([128, 512], BF16, 4)
        rstd_ring = cx.ring([128, 512], F32, 3)
        ao_ring = cx.ring([128, 512], BF16, 2)
        qn_ring = cx.ring([128, NTOK], BF16, 2)
        qr_ring = cx.ring([64, NTOK], BF16, 2)
        wk_ring = cx.ring([128, 2, 128], BF16, 2)
        wv_ring = cx.ring([128, 2, 128], BF16, 2)
        po, pot = cx.psb[4], cx.pst[4]
        psm, psmt = cx.psb[5], cx.pst[5]

        def build_kv(h, src_ap, nkeys, wkb, wkt, wvb, wvt):
            k0 = 0
            while k0 < nkeys:
                kl = min(512, nkeys - k0)
                ck, ckt, ds = ck_ring.next()
                S.dma("pool", lambda e, ck=ck, k0=k0, kl=kl: e.dma_start(out=ck[:, :, 0:kl], in_=src_ap[:, k0:k0 + kl].rearrange("(c p) k -> p c k", p=128)),
                      ds, writes=[ckt])
                pk, pkt = cx.ps(0, 4)
                rs, rst, _ = rstd_ring.next()
                for c in range(2):
                    S.op("pe", lambda e, c=c, ck=ck, pk=pk, kl=kl: e.matmul(pk[:, 0:kl], lhsT=wkb[:, c, :], rhs=ck[:, c, 0:kl], start=(c == 0), stop=(c == 1)),
                         reads=[wkt, ckt], writes=[pkt])
                pv, pvt = cx.ps(0, 4)
                nsub = (kl + 127) // 128
                kkmax = min(128, kl)
                for sub in range(nsub):
                    kk = min(128, kl - sub * 128)
                    for c in range(2):
                        S.op("pe", lambda e, c=c, ck=ck, pv=pv, sub=sub, kk=kk: e.matmul(pv[0:kk, sub * 128:(sub + 1) * 128], lhsT=ck[:, c, sub * 128:sub * 128 + kk],
                                                                                        rhs=wvb[:, c, :], start=(c == 0), stop=(c == 1)),
                             reads=[wvt, ckt], writes=[pvt])
                rms_stats(cx, [pk[:, 0:kl]], [pkt], kl, 128.0, ones[:], ones_t, sq_ring, rs, rst)
                kt0 = k0 // 128
                S.op("act", lambda e, pv=pv, kt0=kt0, nsub=nsub, kkmax=kkmax: e.activation(
                    out=V[0:kkmax, kt0:kt0 + nsub, :], in_=pv[0:kkmax, 0:nsub * 128].rearrange("p (a b) -> p a b", b=128), func=AF.Copy),
                     reads=[pvt], writes=[V_t])
                S.op("dve", lambda e, pk=pk, k0=k0, kl=kl, rs=rs: stt(e, KT[:, k0:k0 + kl], pk[:, 0:kl], gk_sb[:, 0:1], rs[:, 0:kl]),
                     reads=[pkt, gk_t, rst], writes=[KT_t])
                k0 += kl

        def attend(h, qn, qnt, qr, qrt, kr_sb, kr_t, q_off, Tq, tiles, mask_lo):
            n = len(tiles)
            LOOK = 3
            staged = []

            def stage1(idx):
                kt, kk = tiles[idx]
                pss, psst = cx.ps(0, 4)
                S.op("pe", lambda e, pss=pss, kt=kt, kk=kk: e.matmul(pss[0:kk, 0:Tq], lhsT=KT[:, kt * 128:kt * 128 + kk], rhs=qn[:, q_off:q_off + Tq], start=True, stop=False),
                     reads=[KT_t, qnt], writes=[psst])
                S.op("pe", lambda e, pss=pss, kt=kt, kk=kk: e.matmul(pss[0:kk, 0:Tq], lhsT=kr_sb[0:64, kt * 128:kt * 128 + kk], rhs=qr[0:64, q_off:q_off + Tq], start=False, stop=True),
                     reads=[kr_t, qrt], writes=[psst])
                pT, pTt, _ = p_ring.next()
                S.op("act", lambda e, pss=pss, pT=pT, kk=kk: e.activation(out=pT[0:kk, 0:Tq], in_=pss[0:kk, 0:Tq], func=AF.Exp, scale=SCALE),
                     reads=[psst], writes=[pTt])
                if mask_lo is not None and kt >= mask_lo:
                    m = kt - mask_lo
                    S.op("dve", lambda e, pT=pT, m=m: e.tensor_tensor(out=pT[:, 0:Tq], in0=pT[:, 0:Tq], in1=mask_sb[:, m, 0:Tq], op=ALU.mult),
                         reads=[pTt, mask_t], writes=[pTt])
                staged.append((pT, pTt))

            def stage2(idx):
                kt, kk = tiles[idx]
                pT, pTt = staged[idx]
                S.op("pe", lambda e, pT=pT, kt=kt, kk=kk, idx=idx: e.matmul(po[:, 0:Tq], lhsT=V[0:kk, kt, :], rhs=pT[0:kk, 0:Tq], start=(idx == 0), stop=(idx == n - 1)),
                     reads=[V_t, pTt], writes=[pot])
                S.op("pe", lambda e, pT=pT, kk=kk, idx=idx: e.matmul(psm[:, 0:Tq], lhsT=ones[0:kk, :], rhs=pT[0:kk, 0:Tq], start=(idx == 0), stop=(idx == n - 1)),
                     reads=[ones_t, pTt], writes=[psmt])

            for idx in range(n):
                stage1(idx)
                if idx >= LOOK:
                    stage2(idx - LOOK)
            for idx in range(max(0, n - LOOK), n):
                stage2(idx)
            S.op("dve", lambda e: e.reciprocal(out=rinv[:, 0:Tq], in_=psm[:, 0:Tq]), reads=[psmt], writes=[rinv_t])
            ao, aot, _ = ao_ring.next()
            S.op("dve", lambda e, ao=ao: e.tensor_tensor(out=ao[:, 0:Tq], in0=po[:, 0:Tq], in1=rinv[:, 0:Tq], op=ALU.mult),
                 reads=[pot, rinv_t], writes=[aot])
            cx.store(o_at[h * 128:(h + 1) * 128, q_off:q_off + Tq], ao[:, 0:Tq], [aot])

        for h in range(16):
            wkb, wkt, ds = wk_ring.next()
            S.dma("pool", lambda e, wkb=wkb, h=h: e.dma_start(out=wkb[:], in_=wuk[h]), ds, writes=[wkt])
            wvb, wvt, ds = wv_ring.next()
            S.dma("pool", lambda e, wvb=wvb, h=h: e.dma_start(out=wvb[:], in_=wuv[h]), ds, writes=[wvt])
            qn, qnt, ds = qn_ring.next()
            S.dma("sp", lambda e, qn=qn, h=h: e.dma_start(out=qn[:], in_=qnT[h * 128:(h + 1) * 128, :]), ds, writes=[qnt])
            qr, qrt, ds = qr_ring.next()
            S.dma("sp", lambda e, qr=qr, h=h: e.dma_start(out=qr[:], in_=qrT[h * 64:(h + 1) * 64, :]), ds, writes=[qrt])
            build_kv(h, ckvA, SEQ, wkb, wkt, wvb, wvt)
            for s in range(4):
                attend(h, qn, qnt, qr, qrt, krA_sb, krA_t, 512 * s, 512, [(kt, 128) for kt in range(32 * (s + 1))], 32 * s)
            for b in range(2):
                build_kv(h, ckvS[b], S_KEYS, wkb, wkt, wvb, wvt)
                tiles = [(kt, 128) for kt in range(32)] + [(32, 64)]
                attend(h, qn, qnt, qr, qrt, krS_sb[b][0], krS_sb[b][1], 2048 + 64 * b, 64, tiles, None)
        print("B1 sbuf remaining", nc.sbuf_bytes_remaining)
        cx.finish()
    return nc


def _mask_for_core(i):
    p = np.arange(128)[:, None, None]
    m = np.arange(32)[None, :, None]
    f = np.arange(512)[None, None, :]
    return ((128 * m + p) // 64 <= (512 * i + f) // 64).astype(np.float32)


TILES_B = [(0, 512, 0, 0), (512, 512, 527, 0), (1024, 512, 1054, 0), (1536, 512, 1581, 0), (2048, 64, 2108, 1), (2112, 64, 2187, 2)]
PINE_W = 4 * 527 + 2 * 79
SCALE_M = float(128 ** -0.5)


def build_B2():
    nc = bass.Bass("TRN2", target_bir_lowering=False)
    with ExitStack() as es:
        cx = Ctx(nc, es)
        S = cx.S
        hT = cx.inp("hT", [D, NTOK])
        atT = cx.inp("atT", [D, NTOK], BF16)
        qmT = cx.inp("qmT", [512, NTOK], BF16)
        pinE = cx.inp("pinE", [512, PINE_W])
        icnt = cx.inp("icnt", [128, 4, NTOK])
        mk3 = cx.inp("mk3", [3, 512, 256])
        mv3 = cx.inp("mv3", [3, 256, D])
        gmix = cx.inp("gmix", [128, 16])
        poolw = cx.inp("poolw", [4, 128, 512])
        pscale = cx.inp("pscale", [128, 16])
        wgate = cx.inp("wgate", [48, 128, 16, 128])
        bgate = cx.inp("bgate", [128, 48])
        wout = cx.inp("wout", [16, 128, 16, 128])
        g2 = cx.inp("g2", [128, 16])
        wg = cx.inp("wg", [NFF, 128, 16, 128])
        wu = cx.inp("wu", [NFF, 128, 16, 128])
        wd = cx.inp("wd", [16, 128, NFF, 128])
        o_x = cx.outp("o_x", [D, NTOK])

        gmix_sb, gmix_t = cx.load_const(gmix, [128, 16], F32)
        g2_sb, g2_t = cx.load_const(g2, [128, 16], F32)
        pscale_sb, pscale_t = cx.load_const(pscale, [128, 16], F32)
        bgate_sb, bgate_t = cx.load_const(bgate, [128, 48], F32)
        poolw_sb, poolw_t = cx.load_const(poolw.rearrange("g p f -> p g f"), [128, 4, 512], BF16)
        ones = cx.sb([128, 128], BF16); ones_t = Tile()
        S.op("dve", lambda e: e.memset(ones[:], 1.0), writes=[ones_t])

        x_sb = cx.sb([128, 16, 512], F32); x_t = Tile()
        xn = cx.sb([128, 16, 512], BF16); xn_t = Tile()
        act = cx.sb([128, NFF, 512], BF16); act_t = Tile()
        at_sb = cx.sb([128, 16, 512], BF16); at_t = Tile()
        pe_sb = cx.sb([128, 4, 527], F32); pe_t = Tile()
        lvA = cx.sb([128, 527], F32); lvA_t = Tile()
        lvB = cx.sb([128, 527], F32); lvB_t = Tile()
        pooled = cx.sb([128, 4, 512], BF16); pooled_t = Tile()
        PmT = cx.sb([128, 4, 2, 512], BF16); PmT_t = Tile()
        mk_sb = cx.sb([128, 4, 256], BF16); mk_t = Tile()
        mv_sb = cx.sb([128, 2, D], BF16); mv_t = Tile()
        qm_sb = cx.sb([128, 4, 512], BF16); qm_t = Tile()
        rstd = cx.sb([128, 512], F32); rstd_t = Tile()
        rinv = cx.sb([128, 512], F32); rinv_t = Tile()
        mrg = cx.sb([128, 512], F32); mrg_t = Tile()
        tmp1 = cx.sb([128, 512], F32); tmp1_t = Tile()
        icnt_ring = cx.ring([128, 512], F32, 2)
        gate_ring = cx.ring([128, 512], F32, 3)
        sq_ring = cx.ring([128, 512], BF16, 2)
        silu_ring = cx.ring([128, 512], BF16, 2)
        w_ring = cx.ring([128, 16, 128], BF16, 3)
        wd_ring = cx.ring([128, NFF, 128], BF16, 2)
        ld_ds = S.dsem()
        at_ds = S.dsem()
        qm_ds = S.dsem()
        pe_ds = S.dsem()
        mem_ds = S.dsem()
        mv_ds = S.dsem()

        def cast_load(ring, src_ap):
            buf, tl, ds = ring.next()
            S.dma("pool", lambda e: e.dma_start(out=buf[:], in_=src_ap), ds, writes=[tl])
            return buf, tl

        def rmsnorm16(T, g_sb, g_t):
            rms_stats(cx, [x_sb[:, c, 0:T] for c in range(16)], [x_t] * 16, T, float(D), ones[:], ones_t, sq_ring, rstd, rstd_t)
            for c in range(16):
                S.op("dve", lambda e, c=c: stt(e, xn[:, c, 0:T], x_sb[:, c, 0:T], g_sb[:, c:c + 1], rstd[:, 0:T]),
                     reads=[x_t, g_t, rstd_t], writes=[xn_t])

        for (t0, T, seg, ms) in TILES_B:
            L = 15 + T
            S.dma("sp", lambda e, t0=t0, T=T: e.dma_start(out=x_sb[:, :, 0:T], in_=hT[:, t0:t0 + T].rearrange("(c p) t -> p c t", p=128)), ld_ds, writes=[x_t])
            S.dma("sp", lambda e, t0=t0, T=T: e.dma_start(out=at_sb[:, :, 0:T], in_=atT[:, t0:t0 + T].rearrange("(c p) t -> p c t", p=128)), at_ds, writes=[at_t])
            S.dma("sp", lambda e, t0=t0, T=T: e.dma_start(out=qm_sb[:, :, 0:T], in_=qmT[:, t0:t0 + T].rearrange("(c p) t -> p c t", p=128)), qm_ds, writes=[qm_t])
            S.dma("sp", lambda e, seg=seg, L=L: e.dma_start(out=pe_sb[:, :, 0:L], in_=pinE[:, seg:seg + L].rearrange("(g p) t -> p g t", p=128)), pe_ds, writes=[pe_t])
            S.dma("pool", lambda e, ms=ms: e.dma_start(out=mk_sb[:], in_=mk3[ms].rearrange("(h p) m -> p h m", p=128)), mem_ds, writes=[mk_t])
            S.dma("pool", lambda e, ms=ms: e.dma_start(out=mv_sb[:], in_=mv3[ms].rearrange("(c p) f -> p c f", p=128)), mv_ds, writes=[mv_t])
            rmsnorm16(T, gmix_sb, gmix_t)
            for g in range(4):
                e_ = lambda a, b, g=g: pe_sb[:, g, a:b]
                S.op("pool", lambda e, g=g, L=L: e.tensor_tensor(out=lvA[:, 1:L], in0=pe_sb[:, g, 1:L], in1=pe_sb[:, g, 0:L - 1], op=ALU.add),
                     reads=[pe_t], writes=[lvA_t])
                cur, cur_t = lvA, lvA_t
                if g >= 1:
                    S.op("pool", lambda e, L=L: e.tensor_tensor(out=lvB[:, 3:L], in0=lvA[:, 3:L], in1=lvA[:, 1:L - 2], op=ALU.add),
                         reads=[lvA_t], writes=[lvB_t])
                    cur, cur_t = lvB, lvB_t
                if g >= 2:
                    S.op("pool", lambda e, L=L: e.tensor_tensor(out=lvA[:, 7:L], in0=lvB[:, 7:L], in1=lvB[:, 3:L - 4], op=ALU.add),
                         reads=[lvB_t, lvA_t], writes=[lvA_t])
                    cur, cur_t = lvA, lvA_t
                if g >= 3:
                    S.op("pool", lambda e, L=L: e.tensor_tensor(out=lvB[:, 15:L], in0=lvA[:, 15:L], in1=lvA[:, 7:L - 8], op=ALU.add),
                         reads=[lvA_t, lvB_t], writes=[lvB_t])
                    cur, cur_t = lvB, lvB_t
                ic, ict, ds = icnt_ring.next()
                S.dma("sp", lambda e, ic=ic, g=g, t0=t0, T=T: e.dma_start(out=ic[:, 0:T], in_=icnt[:, g, t0:t0 + T]), ds, writes=[ict])
                S.op("dve", lambda e, cur=cur, ic=ic, L=L, T=T: e.tensor_tensor(out=tmp1[:, 0:T], in0=cur[:, 15:L], in1=ic[:, 0:T], op=ALU.mult),
                     reads=[cur_t, ict], writes=[tmp1_t])
                S.op("dve", lambda e, g=g, L=L, T=T: e.tensor_tensor(out=pooled[:, g, 0:T], in0=tmp1[:, 0:T], in1=pe_sb[:, g, 15:L], op=ALU.subtract),
                     reads=[tmp1_t, pe_t], writes=[pooled_t])
            for hm in range(4):
                psm, psmt = cx.ps(6, 8)
                for mc in range(2):
                    pss, psst = cx.ps()
                    S.op("pe", lambda e, pss=pss, hm=hm, mc=mc, T=T: e.matmul(pss[:, 0:T], lhsT=mk_sb[:, hm, mc * 128:(mc + 1) * 128], rhs=qm_sb[:, hm, 0:T], start=True, stop=True),
                         reads=[mk_t, qm_t], writes=[psst])
                    S.op("act", lambda e, pss=pss, hm=hm, mc=mc, T=T: e.activation(out=PmT[:, hm, mc, 0:T], in_=pss[:, 0:T], func=AF.Exp, scale=SCALE_M),
                         reads=[psst], writes=[PmT_t])
                for mc in range(2):
                    S.op("pe", lambda e, psm=psm, hm=hm, mc=mc, T=T: e.matmul(psm[:, 0:T], lhsT=ones[:], rhs=PmT[:, hm, mc, 0:T], start=(mc == 0), stop=(mc == 1)),
                         reads=[ones_t, PmT_t], writes=[psmt])
                S.op("dve", lambda e, psm=psm, T=T: e.reciprocal(out=rinv[:, 0:T], in_=psm[:, 0:T]), reads=[psmt], writes=[rinv_t])
                for mc in range(2):
                    S.op("dve", lambda e, hm=hm, mc=mc, T=T: e.tensor_tensor(out=PmT[:, hm, mc, 0:T], in0=PmT[:, hm, mc, 0:T], in1=rinv[:, 0:T], op=ALU.mult),
                         reads=[PmT_t, rinv_t], writes=[PmT_t])
            for dc in range(16):
                g = dc // 4
                cc = dc % 4
                pp, ppt = cx.ps()
                S.op("pe", lambda e, pp=pp, g=g, cc=cc, T=T: e.matmul(pp[:, 0:T], lhsT=poolw_sb[:, g, cc * 128:(cc + 1) * 128], rhs=pooled[:, g, 0:T], start=True, stop=True),
                     reads=[poolw_t, pooled_t], writes=[ppt])
                pm, pmt = cx.ps()
                for mc in range(2):
                    S.op("pe", lambda e, pm=pm, dc=dc, g=g, mc=mc, T=T: e.matmul(pm[:, 0:T], lhsT=mv_sb[:, mc, dc * 128:(dc + 1) * 128], rhs=PmT[:, g, mc, 0:T], start=(mc == 0), stop=(mc == 1)),
                         reads=[mv_t, PmT_t], writes=[pmt])
                gts = []
                for br in range(3):
                    wb, wt = cast_load(w_ring, wgate[br * 16 + dc])
                    pg, pgt = cx.ps()
                    for c in range(16):
                        S.op("pe", lambda e, c=c, wb=wb, pg=pg, T=T: e.matmul(pg[:, 0:T], lhsT=wb[:, c, :], rhs=xn[:, c, 0:T], start=(c == 0), stop=(c == 15)),
                             reads=[wt, xn_t], writes=[pgt])
                    gt, gtt, _ = gate_ring.next()
                    S.op("act", lambda e, pg=pg, gt=gt, br=br, dc=dc, T=T: e.activation(out=gt[:, 0:T], in_=pg[:, 0:T], func=AF.Sigmoid,
                                                                                       bias=bgate_sb[:, br * 16 + dc:br * 16 + dc + 1]),
                         reads=[pgt, bgate_t], writes=[gtt])
                    gts.append((gt, gtt))
                S.op("dve", lambda e, dc=dc, gt=gts[0][0], T=T: e.tensor_tensor(out=mrg[:, 0:T], in0=at_sb[:, dc, 0:T], in1=gt[:, 0:T], op=ALU.mult),
                     reads=[at_t, gts[0][1]], writes=[mrg_t])
                S.op("dve", lambda e, dc=dc, pp=pp, gt=gts[1][0], T=T: stt(e, tmp1[:, 0:T], pp[:, 0:T], pscale_sb[:, dc:dc + 1], gt[:, 0:T]),
                     reads=[ppt, pscale_t, gts[1][1]], writes=[tmp1_t])
                S.op("dve", lambda e, T=T: e.tensor_tensor(out=mrg[:, 0:T], in0=mrg[:, 0:T], in1=tmp1[:, 0:T], op=ALU.add),
                     reads=[mrg_t, tmp1_t], writes=[mrg_t])
                S.op("dve", lambda e, pm=pm, gt=gts[2][0], T=T: e.tensor_tensor(out=tmp1[:, 0:T], in0=pm[:, 0:T], in1=gt[:, 0:T], op=ALU.mult),
                     reads=[pmt, gts[2][1]], writes=[tmp1_t])
                S.op("dve", lambda e, dc=dc, T=T: e.tensor_tensor(out=at_sb[:, dc, 0:T], in0=mrg[:, 0:T], in1=tmp1[:, 0:T], op=ALU.add),
                     reads=[mrg_t, tmp1_t], writes=[at_t])
            for dco in range(16):
                wb, wt = cast_load(w_ring, wout[dco])
                pw, pwt = cx.ps()
                for c in range(16):
                    S.op("pe", lambda e, c=c, wb=wb, pw=pw, T=T: e.matmul(pw[:, 0:T], lhsT=wb[:, c, :], rhs=at_sb[:, c, 0:T], start=(c == 0), stop=(c == 15)),
                         reads=[wt, at_t], writes=[pwt])
                S.op("dve", lambda e, dco=dco, pw=pw, T=T: e.tensor_tensor(out=x_sb[:, dco, 0:T], in0=x_sb[:, dco, 0:T], in1=pw[:, 0:T], op=ALU.add),
                     reads=[pwt, x_t], writes=[x_t])
            rmsnorm16(T, g2_sb, g2_t)
            for j in range(NFF):
                wgb, wgt = cast_load(w_ring, wg[j])
                wub, wut = cast_load(w_ring, wu[j])
                pg, pgt = cx.ps()
                pu, put = cx.ps()
                for c in range(16):
                    S.op("pe", lambda e, c=c, wgb=wgb, pg=pg, T=T: e.matmul(pg[:, 0:T], lhsT=wgb[:, c, :], rhs=xn[:, c, 0:T], start=(c == 0), stop=(c == 15)),
                         reads=[wgt, xn_t], writes=[pgt])
                for c in range(16):
                    S.op("pe", lambda e, c=c, wub=wub, pu=pu, T=T: e.matmul(pu[:, 0:T], lhsT=wub[:, c, :], rhs=xn[:, c, 0:T], start=(c == 0), stop=(c == 15)),
                         reads=[wut, xn_t], writes=[put])
                sl, slt, _ = silu_ring.next()
                S.op("act", lambda e, sl=sl, pg=pg, T=T: e.activation(out=sl[:, 0:T], in_=pg[:, 0:T], func=AF.Silu), reads=[pgt], writes=[slt])
                S.op("dve", lambda e, sl=sl, pu=pu, j=j, T=T: e.tensor_tensor(out=act[:, j, 0:T], in0=sl[:, 0:T], in1=pu[:, 0:T], op=ALU.mult),
                     reads=[slt, put], writes=[act_t])
            for c in range(16):
                wdb, wdt = cast_load(wd_ring, wd[c])
                po, pot = cx.ps()
                for j in range(NFF):
                    S.op("pe", lambda e, j=j, wdb=wdb, po=po, T=T: e.matmul(po[:, 0:T], lhsT=wdb[:, j, :], rhs=act[:, j, 0:T], start=(j == 0), stop=(j == NFF - 1)),
                         reads=[wdt, act_t], writes=[pot])
                S.op("dve", lambda e, c=c, po=po, T=T: stt(e, x_sb[:, c, 0:T], po[:, 0:T], 0.5, x_sb[:, c, 0:T], ALU.mult, ALU.add),
                     reads=[pot, x_t], writes=[x_t])
            cx.store(o_x[:, t0:t0 + T].rearrange("(c p) t -> p c t", p=128), x_sb[:, :, 0:T], [x_t])
        print("B2 sbuf remaining", nc.sbuf_bytes_remaining)
        cx.finish()
    return nc


_PROGS = {}


def _prog(name, fn):
    if name not in _PROGS:
        _PROGS[name] = fn()
    return _PROGS[name]


def _run(nc, in_maps):
    res = run_bass_kernel_spmd(nc, in_maps, core_ids=list(range(NCORES)))
    return [{k: np.asarray(v) for k, v in r.items()} for r in res.results]


def _icnt(i):
    pos = _positions(i)
    out = np.empty((4, NTOK), np.float32)
    for g, w in enumerate((2, 4, 8, 16)):
        out[g] = np.float32(1.0) / np.minimum(pos + 1, w).astype(np.float32)
    return np.ascontiguousarray(np.broadcast_to(out[None], (128, 4, NTOK)))


def kernel(**inp):
    inp = {k: np.asarray(v) for k, v in inp.items()}
    ncA = _prog("A", build_A)
    ncB1 = _prog("B1", build_B1)
    ncB2 = _prog("B2", build_B2)
    xT = [_initial_xT(inp, i) for i in range(NCORES)]
    tabs = [_rope_tables(i) for i in range(NCORES)]
    masks = [_mask_for_core(i) for i in range(NCORES)]
    icnts = [_icnt(i) for i in range(NCORES)]
    L = 2
    o_ckv_p = np.zeros((L, 1, SEQ, 256), np.float32)
    o_kr_p = np.zeros((L, 1, SEQ, 64), np.float32)
    o_pool_p = np.zeros((L, 1, 15, 512), np.float32)
    o_mk_p = np.zeros((L, 1, 256, 4, 128), np.float32)
    o_mv_p = np.zeros((L, 1, 256, 4, 512), np.float32)
    o_ckv_s = np.zeros((L, DEC_B, DEC_T, 256), np.float32)
    o_kr_s = np.zeros((L, DEC_B, DEC_T, 64), np.float32)
    o_pool_s = np.zeros((L, DEC_B, 15, 512), np.float32)
    for l in range(L):
        wA = _prep_A_weights(inp, l)
        maps = []
        for i in range(NCORES):
            m = dict(wA)
            m["xT"] = xT[i]
            m["cosT"], m["sinT"] = tabs[i]
            maps.append(m)
        rA = _run(ncA, maps)
        del maps, wA
        ckv_all = np.zeros((256, SEQ), np.float32)
        kr_all = np.zeros((64, SEQ), np.float32)
        pin_all = np.zeros((512, SEQ), np.float32)
        for i in range(NCORES):
            for s in range(4):
                g0 = 512 * (8 * s + i)
                ckv_all[:, g0:g0 + 512] = rA[i]["o_ckv"][:, 512 * s:512 * s + 512]
                kr_all[:, g0:g0 + 512] = rA[i]["o_kr"][:, 512 * s:512 * s + 512]
                pin_all[:, g0:g0 + 512] = rA[i]["o_pin"][:, 512 * s:512 * s + 512]
        o_ckv_p[l, 0] = ckv_all.T
        o_kr_p[l, 0] = kr_all.T
        o_pool_p[l, 0] = pin_all[:, SEQ - 15:].T
        o_mk_p[l, 0] = rA[0]["o_mk"].T.reshape(256, 4, 128)
        o_mv_p[l, 0] = rA[0]["o_mv"].reshape(256, 4, 512)
        for b in range(DEC_B):
            i, bb = b // 2, b % 2
            c0 = 2048 + 64 * bb
            o_ckv_s[l, b] = rA[i]["o_ckv"][:, c0:c0 + 64].T
            o_kr_s[l, b] = rA[i]["o_kr"][:, c0:c0 + 64].T
            o_pool_s[l, b] = rA[i]["o_pin"][:, c0 + 64 - 15:c0 + 64].T
        wuk = _slab(inp["w_uk"][l], 2, 16)
        wuv = _slab(inp["w_uv"][l], 2, 16)
        gk = np.ascontiguousarray(inp["k_norm"][l][:, None])
        maps = []
        for i in range(NCORES):
            m = {"qnT": rA[i]["o_qn"], "qrT": rA[i]["o_qr"], "ckvA": ckv_all, "krA": kr_all, "mask": masks[i],
                 "wuk": wuk, "wuv": wuv, "gk": gk}
            cs, ks = [], []
            for bb in range(2):
                b = 2 * i + bb
                c0 = 2048 + 64 * bb
                cs.append(np.concatenate([inp["cache_ckv"][l, b].T, rA[i]["o_ckv"][:, c0:c0 + 64]], axis=1))
                ks.append(np.concatenate([inp["cache_krope"][l, b].T, rA[i]["o_kr"][:, c0:c0 + 64]], axis=1))
            m["ckvS"] = np.ascontiguousarray(np.stack(cs))
            m["krS"] = np.ascontiguousarray(np.stack(ks))
            maps.append(m)
        rB1 = _run(ncB1, maps)
        del maps
        wB = {
            "gmix": _gain(inp["mix_norm"][l], 16),
            "poolw": np.ascontiguousarray(inp["pool_w"][l]),
            "pscale": _gain(inp["pool_scale"][l], 16),
            "wgate": _slab(inp["w_gate"][l], 16, 48),
            "bgate": _gain(inp["b_gate"][l], 48),
            "wout": _slab(inp["w_out"][l], 16, 16),
            "g2": _gain(inp["ffn2_norm"][l], 16),
            "wg": _slab(inp["ffn2_wg"][l], 16, NFF),
            "wu": _slab(inp["ffn2_wu"][l], 16, NFF),
            "wd": _slab(inp["ffn2_wd"][l], NFF, 16),
        }
        maps = []
        for i in range(NCORES):
            m = dict(wB)
            m["hT"] = rA[i]["o_h"]
            m["atT"] = rB1[i]["o_at"]
            m["qmT"] = rA[i]["o_qm"]
            segs = []
            for s in range(4):
                g0 = 512 * (8 * s + i)
                halo = pin_all[:, g0 - 15:g0] if g0 > 0 else np.zeros((512, 15), np.float32)
                segs.append(np.concatenate([halo, pin_all[:, g0:g0 + 512]], axis=1))
            mks = [rA[0]["o_mk"]]
            mvs = [rA[0]["o_mv"]]
            for bb in range(2):
                b = 2 * i + bb
                c0 = 2048 + 64 * bb
                segs.append(np.concatenate([inp["state_pool"][l, b].T, rA[i]["o_pin"][:, c0:c0 + 64]], axis=1))
                mks.append(inp["cache_mem_k"][l, b].reshape(256, 512).T)
                mvs.append(inp["cache_mem_v"][l, b].reshape(256, D))
            m["pinE"] = np.ascontiguousarray(np.concatenate(segs, axis=1))
            m["icnt"] = icnts[i]
            m["mk3"] = np.ascontiguousarray(np.stack(mks))
            m["mv3"] = np.ascontiguousarray(np.stack(mvs))
            maps.append(m)
        rB2 = _run(ncB2, maps)
        del maps, wB
        xT = [rB2[i]["o_x"] for i in range(NCORES)]
    y_p = np.zeros((1, SEQ, D), np.float32)
    y_s = np.zeros((DEC_B, DEC_T, D), np.float32)
    for i in range(NCORES):
        y_p[0, _tok_index(i)] = xT[i][:, 0:2048].T
        for bb in range(2):
            y_s[2 * i + bb] = xT[i][:, 2048 + 64 * bb:2048 + 64 * bb + 64].T
    return (y_p, y_s, o_ckv_p, o_kr_p, o_pool_p, o_mk_p, o_mv_p, o_ckv_s, o_kr_s, o_pool_s)
```

```python
import numpy as np
import ml_dtypes
from contextlib import ExitStack
import concourse.bass as bass
import concourse.mybir as mybir
from concourse.bass_utils import run_bass_kernel_spmd

F32 = mybir.dt.float32
BF16 = mybir.dt.bfloat16
AF = mybir.ActivationFunctionType
ALU = mybir.AluOpType

NCORES = 8
D = 2048
DFF = 5632
NFF = DFF // 128
SEQ = 16384
PAST = 4096
DEC_B = 16
DEC_T = 64
NTOK = 2176
TILES = [(0, 512), (512, 512), (1024, 512), (1536, 512), (2048, 128)]
EPS = 1e-6
IN_W = 1856


class Tile:
    __slots__ = ("name", "w", "r")

    def __init__(self, name=""):
        self.name = name
        self.w = None
        self.r = []


class DSem:
    def __init__(self, sem):
        self.sem = sem
        self.count = 0


class Op:
    __slots__ = ("eng", "fn", "deps", "marked", "val", "dsem", "dval")


class Sched:
    ENGS = ("pe", "act", "dve", "pool", "sp")

    def __init__(self, nc, es):
        self.nc = nc
        self.es = es
        self.ops = {e: [] for e in self.ENGS}
        self.sems = {e: es.enter_context(nc.semaphore("s_" + e)) for e in self.ENGS}
        self.n_dsem = 0

    def dsem(self):
        self.n_dsem += 1
        return DSem(self.es.enter_context(self.nc.semaphore("d%d" % self.n_dsem)))

    def _deps(self, reads, writes):
        deps = []
        for t in reads:
            if t.w is not None:
                deps.append(t.w)
        for t in writes:
            if t.w is not None:
                deps.append(t.w)
            deps.extend(t.r)
        return deps

    def _reg(self, o, tok, reads, writes):
        for t in reads:
            t.r.append(tok)
        for t in writes:
            t.w = tok
            t.r = []

    def op(self, eng, fn, reads=(), writes=()):
        o = Op()
        o.eng = eng
        o.fn = fn
        o.deps = self._deps(reads, writes)
        o.marked = False
        o.val = None
        o.dsem = None
        self.ops[eng].append(o)
        self._reg(o, ("e", o), reads, writes)
        return o

    def dma(self, eng, fn, dsem, reads=(), writes=()):
        o = Op()
        o.eng = eng
        o.fn = fn
        o.deps = self._deps(reads, writes)
        o.marked = False
        o.val = None
        o.dsem = dsem
        dsem.count += 16
        o.dval = dsem.count
        self.ops[eng].append(o)
        self._reg(o, ("d", dsem, o.dval), reads, writes)
        return o

    def emit(self, final_dsems=()):
        nc = self.nc
        for e in self.ENGS:
            for o in self.ops[e]:
                for d in o.deps:
                    if d[0] == "e":
                        p = d[1]
                        if p.eng == "pe" and o.eng == "pe" and o.dsem is None:
                            continue
                        p.marked = True
        for e in self.ENGS:
            c = 0
            for o in self.ops[e]:
                if o.marked:
                    assert o.dsem is None
                    c += 1
                    o.val = c
        sems = self.sems
        with nc.Block() as block:
            def run(engname, engine):
                waited = {}
                for o in self.ops[engname]:
                    need = {}
                    for d in o.deps:
                        if d[0] == "e":
                            p = d[1]
                            if p.eng == "pe" and engname == "pe" and o.dsem is None:
                                continue
                            key = ("e", p.eng)
                            v = p.val
                            s = sems[p.eng]
                        else:
                            key = ("d", id(d[1]))
                            v = d[2]
                            s = d[1].sem
                        if waited.get(key, 0) >= v:
                            continue
                        if key not in need or need[key][1] < v:
                            need[key] = (s, v)
                    for key, (s, v) in need.items():
                        engine.wait_ge(s, v)
                        waited[key] = v
                    ins = o.fn(engine)
                    if o.dsem is not None:
                        ins.then_inc(o.dsem.sem, 16)
                    elif o.marked:
                        ins.then_inc(sems[engname], 1)
                if engname == "sp":
                    for ds in final_dsems:
                        engine.wait_ge(ds.sem, ds.count)

            @block.tensor
            def _(e):
                run("pe", e)

            @block.scalar
            def _(e):
                run("act", e)

            @block.vector
            def _(e):
                run("dve", e)

            @block.gpsimd
            def _(e):
                run("pool", e)

            @block.sync
            def _(e):
                run("sp", e)


class Ctx:
    def __init__(self, nc, es):
        self.nc = nc
        self.es = es
        self.S = Sched(nc, es)
        self.n = 0
        self.psb = []
        self.pst = []
        for i in range(8):
            self.psb.append(es.enter_context(nc.psum_tensor("ps%d" % i, [128, 512], F32)))
            self.pst.append(Tile("ps%d" % i))
        self.ps_i = 0
        self.store_ds = {}

    def sb(self, shape, dt, name=None):
        self.n += 1
        return self.es.enter_context(self.nc.sbuf_tensor(name or ("t%d" % self.n), shape, dt))

    def ps(self, lo=0, hi=6):
        k = lo + self.ps_i % (hi - lo)
        self.ps_i += 1
        return self.psb[k], self.pst[k]

    def ring(self, shape, dt, n):
        return Ring(self, shape, dt, n)

    def inp(self, name, shape, dt=F32):
        return self.nc.dram_tensor(name, list(shape), dt, kind="ExternalInput").ap()

    def outp(self, name, shape, dt=F32):
        return self.nc.dram_tensor(name, list(shape), dt, kind="ExternalOutput").ap()

    def store(self, out_ap, in_ap, reads):
        key = id(reads[0])
        if key not in self.store_ds:
            self.store_ds[key] = self.S.dsem()
        self.S.dma("sp", lambda e: e.dma_start(out=out_ap, in_=in_ap), self.store_ds[key], reads=reads)

    def finish(self):
        self.S.emit(final_dsems=list(self.store_ds.values()))

    def load_const(self, dram_ap, shape, dt, eng=None):
        t = self.sb(shape, dt)
        tl = Tile()
        ds = self.S.dsem()
        src_dt_differs = (dt != F32)
        q = "pool" if src_dt_differs else "sp"
        self.S.dma(q, lambda e: e.dma_start(out=t[:], in_=dram_ap), ds, writes=[tl])
        return t, tl


class Ring:
    def __init__(self, cx, shape, dt, n):
        self.bufs = [cx.sb(shape, dt) for _ in range(n)]
        self.tiles = [Tile() for _ in range(n)]
        self.dsems = [cx.S.dsem() for _ in range(n)]
        self.n = n
        self.i = 0

    def next(self):
        k = self.i % self.n
        self.i += 1
        return self.bufs[k], self.tiles[k], self.dsems[k]


def rms_stats(cx, srcs, src_tiles, T, Dn, ones_ap, ones_tile, sq_ring, rstd, rstd_tile, npart=128):
    S = cx.S
    pn, pnt = cx.ps(6, 8)
    n = len(srcs)
    for c, (src, st) in enumerate(zip(srcs, src_tiles)):
        sq, sqt, _ = sq_ring.next()
        S.op("act", lambda e, sq=sq, src=src: e.activation(out=sq[0:npart, 0:T], in_=src, func=AF.Square),
             reads=[st], writes=[sqt])
        S.op("pe", lambda e, sq=sq, c=c: e.matmul(pn[0:npart, 0:T], lhsT=ones_ap, rhs=sq[0:npart, 0:T],
                                                  start=(c == 0), stop=(c == n - 1)),
             reads=[sqt, ones_tile], writes=[pnt])
    S.op("act", lambda e: e.activation(out=rstd[0:npart, 0:T], in_=pn[0:npart, 0:T], func=AF.Sqrt, bias=EPS, scale=1.0 / Dn),
         reads=[pnt], writes=[rstd_tile])
    S.op("dve", lambda e: e.reciprocal(out=rstd[0:npart, 0:T], in_=rstd[0:npart, 0:T]),
         reads=[rstd_tile], writes=[rstd_tile])


def stt(e, out, in0, scalar, in1, op0=ALU.mult, op1=ALU.mult):
    return e.scalar_tensor_tensor(out=out, in0=in0, scalar=scalar, in1=in1, op0=op0, op1=op1)


def build_A():
    nc = bass.Bass("TRN2", target_bir_lowering=False)
    with ExitStack() as es:
        cx = Ctx(nc, es)
        S = cx.S
        xT = cx.inp("xT", [D, NTOK])
        g1 = cx.inp("g1", [128, 16])
        wg = cx.inp("wg", [NFF, 128, 16, 128])
        wu = cx.inp("wu", [NFF, 128, 16, 128])
        wd = cx.inp("wd", [16, 128, NFF, 128])
        gmix = cx.inp("gmix", [128, 16])
        win = cx.inp("win", [15, 128, 16, 128])
        gql = cx.inp("gql", [128, 4])
        gkvl = cx.inp("gkvl", [128, 2])
        gq = cx.inp("gq", [128, 1])
        gqr = cx.inp("gqr", [128, 1])
        gkr = cx.inp("gkr", [128, 1])
        gmq = cx.inp("gmq", [128, 1])
        wuq = cx.inp("wuq", [16, 128, 4, 128])
        wuqr = cx.inp("wuqr", [8, 128, 4, 128])
        cosT = cx.inp("cosT", [128, NTOK])
        sinT = cx.inp("sinT", [128, NTOK])
        rotT = cx.inp("rotT", [128, 128])
        memT = cx.inp("memT", [D, 256])
        gmem = cx.inp("gmem", [128, 16])
        wmk = cx.inp("wmk", [4, 128, 16, 128])
        gmk = cx.inp("gmk", [128, 1])
        wmv = cx.inp("wmv", [4, 128, 16, 512])
        o_h = cx.outp("o_h", [D, NTOK])
        o_qn = cx.outp("o_qn", [D, NTOK], BF16)
        o_qr = cx.outp("o_qr", [1024, NTOK], BF16)
        o_ckv = cx.outp("o_ckv", [256, NTOK])
        o_kr = cx.outp("o_kr", [64, NTOK])
        o_pin = cx.outp("o_pin", [512, NTOK])
        o_qm = cx.outp("o_qm", [512, NTOK], BF16)
        o_mk = cx.outp("o_mk", [512, 256])
        o_mv = cx.outp("o_mv", [256, D])

        g1_sb, g1_t = cx.load_const(g1, [128, 16], F32)
        gmix_sb, gmix_t = cx.load_const(gmix, [128, 16], F32)
        gql_sb, gql_t = cx.load_const(gql, [128, 4], F32)
        gkvl_sb, gkvl_t = cx.load_const(gkvl, [128, 2], F32)
        gq_sb, gq_t = cx.load_const(gq, [128, 1], F32)
        gqr_sb, gqr_t = cx.load_const(gqr, [128, 1], F32)
        gkr_sb, gkr_t = cx.load_const(gkr, [128, 1], F32)
        gmq_sb, gmq_t = cx.load_const(gmq, [128, 1], F32)
        gmem_sb, gmem_t = cx.load_const(gmem, [128, 16], F32)
        gmk_sb, gmk_t = cx.load_const(gmk, [128, 1], F32)
        rot_bf, rot_bf_t = cx.load_const(rotT, [128, 128], BF16)
        rot_f, rot_f_t = cx.load_const(rotT, [128, 128], F32)
        ones = cx.sb([128, 128], BF16)
        ones_t = Tile()
        S.op("dve", lambda e: e.memset(ones[:], 1.0), writes=[ones_t])
        ones2 = cx.sb([128, 128], BF16)
        ones2_t = Tile()
        S.op("dve", lambda e: e.memset(ones2[:], 0.0), writes=[ones2_t])
        S.op("dve", lambda e: e.memset(ones2[0:64, 0:64], 1.0), reads=[ones2_t], writes=[ones2_t])
        S.op("dve", lambda e: e.memset(ones2[64:128, 64:128], 1.0), reads=[ones2_t], writes=[ones2_t])

        onesf = cx.sb([64, 64], F32)
        onesf_t = Tile()
        S.op("dve", lambda e: e.memset(onesf[:], 1.0), writes=[onesf_t])
        x_sb = cx.sb([128, 16, 512], F32); x_t = Tile()
        xn = cx.sb([128, 16, 512], BF16); xn_t = Tile()
        act = cx.sb([128, NFF, 512], BF16); act_t = Tile()
        proj = cx.sb([128, 15, 512], F32); proj_t = Tile()
        cq = cx.sb([128, 4, 512], BF16); cq_t = Tile()
        ckv_sb = cx.sb([128, 2, 512], F32); ckv_t = Tile()
        qm_sb = cx.sb([128, 4, 512], BF16); qm_t = Tile()
        cos_sb = cx.sb([128, 512], F32); cos_t = Tile()
        sin_sb = cx.sb([128, 512], F32); sin_t = Tile()
        krn = cx.sb([64, 512], F32); krn_t = Tile()
        kro = cx.sb([64, 512], F32); kro_t = Tile()
        tmpf = cx.sb([128, 512], F32); tmpf_t = Tile()
        rstd = cx.sb([128, 512], F32); rstd_t = Tile()
        silu_ring = cx.ring([128, 512], BF16, 2)
        sq_ring = cx.ring([128, 512], BF16, 3)
        sqf_ring = cx.ring([128, 512], F32, 1)
        qo_ring = cx.ring([128, 512], BF16, 2)
        qrn_ring = cx.ring([128, 512], BF16, 2)
        w_ring = cx.ring([128, 16, 128], BF16, 5)
        wd_ring = cx.ring([128, NFF, 128], BF16, 2)
        wq_ring = cx.ring([128, 4, 128], BF16, 4)
        cs_ds = S.dsem()
        cos_ds = S.dsem()
        sin_ds = S.dsem()

        def cast_load(ring, src_ap, shape_slice=None):
            buf, tl, ds = ring.next()
            dst = buf[:] if shape_slice is None else shape_slice(buf)
            S.dma("pool", lambda e: e.dma_start(out=dst, in_=src_ap), ds, writes=[tl])
            return buf, tl

        def rmsnorm_chunks(src_sb, src_t, C, T, g_sb, g_t, out_sb, out_t, c0=0):
            rms_stats(cx, [src_sb[:, c0 + c, 0:T] for c in range(C)], [src_t] * C, T, 128.0 * C,
                      ones[:], ones_t, sq_ring, rstd, rstd_t)
            for c in range(C):
                S.op("dve", lambda e, c=c: stt(e, out_sb[:, c, 0:T], src_sb[:, c0 + c, 0:T], g_sb[:, c:c + 1], rstd[:, 0:T]),
                     reads=[src_t, g_t, rstd_t], writes=[out_t])

        m_sb = x_sb; m_t = x_t
        mn = xn; mn_t = xn_t
        S.dma("sp", lambda e: e.dma_start(out=m_sb[:, :, 0:256], in_=memT.rearrange("(c p) t -> p c t", p=128)), cs_ds, writes=[m_t])
        rmsnorm_chunks(m_sb, m_t, 16, 256, gmem_sb, gmem_t, mn, mn_t)
        mk_sb = proj; mk_t = proj_t
        for h in range(4):
            wb, wt = cast_load(w_ring, wmk[h])
            pk, pkt = cx.ps()
            for c in range(16):
                S.op("pe", lambda e, c=c, wb=wb, pk=pk: e.matmul(pk[:, 0:256], lhsT=wb[:, c, :], rhs=mn[:, c, 0:256], start=(c == 0), stop=(c == 15)),
                     reads=[wt, mn_t], writes=[pkt])
            rms_stats(cx, [pk[:, 0:256]], [pkt], 256, 128.0, ones[:], ones_t, sq_ring, rstd, rstd_t)
            S.op("dve", lambda e, h=h, pk=pk: stt(e, mk_sb[:, h, 0:256], pk[:, 0:256], gmk_sb[:, 0:1], rstd[:, 0:256]),
                 reads=[pkt, gmk_t, rstd_t], writes=[mk_t])
        cx.store(o_mk.rearrange("(c p) t -> p c t", p=128), mk_sb[:, 0:4, 0:256], [mk_t])
        mv_t = proj_t
        wmv_ds = S.dsem()
        for gcol in range(4):
            wb, wt = act, act_t
            S.dma("pool", lambda e, gcol=gcol: e.dma_start(out=act[:, 0:16, :], in_=wmv[gcol]), wmv_ds, writes=[act_t])
            for mc in range(2):
                pv, pvt = cx.ps()
                for c in range(16):
                    S.op("pe", lambda e, c=c, wb=wb, pv=pv, mc=mc: e.matmul(pv[:, :], lhsT=mn[:, c, mc * 128:(mc + 1) * 128], rhs=wb[:, c, :],
                                                                            start=(c == 0), stop=(c == 15)),
                         reads=[wt, mn_t], writes=[pvt])
                S.op("act", lambda e, pv=pv, mc=mc, gcol=gcol: e.activation(out=proj[:, 4 + mc * 4 + gcol, :], in_=pv[:, :], func=AF.Copy),
                     reads=[pvt], writes=[mv_t])
        for mc in range(2):
            cx.store(o_mv[mc * 128:(mc + 1) * 128, :].rearrange("p (g f) -> p g f", f=512), proj[:, 4 + mc * 4:8 + mc * 4, :], [mv_t])

        for (t0, T) in TILES:
            S.dma("sp", lambda e, t0=t0, T=T: e.dma_start(out=x_sb[:, :, 0:T], in_=xT[:, t0:t0 + T].rearrange("(c p) t -> p c t", p=128)),
                  cs_ds, writes=[x_t])
            S.dma("sp", lambda e, t0=t0, T=T: e.dma_start(out=cos_sb[:, 0:T], in_=cosT[:, t0:t0 + T]), cos_ds, writes=[cos_t])
            S.dma("sp", lambda e, t0=t0, T=T: e.dma_start(out=sin_sb[:, 0:T], in_=sinT[:, t0:t0 + T]), sin_ds, writes=[sin_t])
            rmsnorm_chunks(x_sb, x_t, 16, T, g1_sb, g1_t, xn, xn_t)
            for j in range(NFF):
                wgb, wgt = cast_load(w_ring, wg[j])
                wub, wut = cast_load(w_ring, wu[j])
                pg, pgt = cx.ps()
                pu, put = cx.ps()
                for c in range(16):
                    S.op("pe", lambda e, c=c, wgb=wgb, pg=pg, T=T: e.matmul(pg[:, 0:T], lhsT=wgb[:, c, :], rhs=xn[:, c, 0:T], start=(c == 0), stop=(c == 15)),
                         reads=[wgt, xn_t], writes=[pgt])
                for c in range(16):
                    S.op("pe", lambda e, c=c, wub=wub, pu=pu, T=T: e.matmul(pu[:, 0:T], lhsT=wub[:, c, :], rhs=xn[:, c, 0:T], start=(c == 0), stop=(c == 15)),
                         reads=[wut, xn_t], writes=[put])
                sl, slt, _ = silu_ring.next()
                S.op("act", lambda e, sl=sl, pg=pg, T=T: e.activation(out=sl[:, 0:T], in_=pg[:, 0:T], func=AF.Silu), reads=[pgt], writes=[slt])
                S.op("dve", lambda e, sl=sl, pu=pu, j=j, T=T: e.tensor_tensor(out=act[:, j, 0:T], in0=sl[:, 0:T], in1=pu[:, 0:T], op=ALU.mult),
                     reads=[slt, put], writes=[act_t])
            for c in range(16):
                wdb, wdt = cast_load(wd_ring, wd[c])
                po, pot = cx.ps()
                for j in range(NFF):
                    S.op("pe", lambda e, j=j, wdb=wdb, po=po, T=T: e.matmul(po[:, 0:T], lhsT=wdb[:, j, :], rhs=act[:, j, 0:T], start=(j == 0), stop=(j == NFF - 1)),
                         reads=[wdt, act_t], writes=[pot])
                S.op("dve", lambda e, c=c, po=po, T=T: stt(e, x_sb[:, c, 0:T], po[:, 0:T], 0.5, x_sb[:, c, 0:T], ALU.mult, ALU.add),
                     reads=[pot, x_t], writes=[x_t])
            cx.store(o_h[:, t0:t0 + T].rearrange("(c p) t -> p c t", p=128), x_sb[:, :, 0:T], [x_t])
            rmsnorm_chunks(x_sb, x_t, 16, T, gmix_sb, gmix_t, xn, xn_t)
            for j in range(15):
                wb, wt = cast_load(w_ring, win[j])
                pp, ppt = cx.ps()
                for c in range(16):
                    S.op("pe", lambda e, c=c, wb=wb, pp=pp, T=T: e.matmul(pp[:, 0:T], lhsT=wb[:, c, :], rhs=xn[:, c, 0:T], start=(c == 0), stop=(c == 15)),
                         reads=[wt, xn_t], writes=[ppt])
                S.op("act", lambda e, j=j, pp=pp, T=T: e.activation(out=proj[:, j, 0:T], in_=pp[:, 0:T], func=AF.Copy), reads=[ppt], writes=[proj_t])
            cx.store(o_pin[:, t0:t0 + T].rearrange("(c p) t -> p c t", p=128), proj[:, 7:11, 0:T], [proj_t])
            rmsnorm_chunks(proj, proj_t, 4, T, gql_sb, gql_t, cq, cq_t, c0=0)
            rmsnorm_chunks(proj, proj_t, 2, T, gkvl_sb, gkvl_t, ckv_sb, ckv_t, c0=4)
            cx.store(o_ckv[:, t0:t0 + T].rearrange("(c p) t -> p c t", p=128), ckv_sb[:, :, 0:T], [ckv_t])
            sqf, sqft, _ = sqf_ring.next()
            pn, pnt = cx.ps(6, 8)
            S.op("act", lambda e, T=T: e.activation(out=sqf[0:64, 0:T], in_=proj[0:64, 6, 0:T], func=AF.Square), reads=[proj_t], writes=[sqft])
            S.op("pe", lambda e, T=T, pn=pn: e.matmul(pn[0:64, 0:T], lhsT=onesf[0:64, 0:64], rhs=sqf[0:64, 0:T], start=True, stop=True),
                 reads=[sqft, onesf_t], writes=[pnt])
            S.op("act", lambda e, T=T, pn=pn: e.activation(out=rstd[0:64, 0:T], in_=pn[0:64, 0:T], func=AF.Sqrt, bias=EPS, scale=1.0 / 64),
                 reads=[pnt], writes=[rstd_t])
            S.op("dve", lambda e, T=T: e.reciprocal(out=rstd[0:64, 0:T], in_=rstd[0:64, 0:T]), reads=[rstd_t], writes=[rstd_t])
            S.op("dve", lambda e, T=T: stt(e, krn[:, 0:T], proj[0:64, 6, 0:T], gkr_sb[0:64, 0:1], rstd[0:64, 0:T]),
                 reads=[proj_t, gkr_t, rstd_t], writes=[krn_t])
            pr, prt = cx.ps()
            S.op("pe", lambda e, T=T, pr=pr: e.matmul(pr[0:64, 0:T], lhsT=rot_f[0:64, 0:64], rhs=krn[:, 0:T], start=True, stop=True),
                 reads=[krn_t, rot_f_t], writes=[prt])
            S.op("dve", lambda e, T=T, pr=pr: e.tensor_tensor(out=tmpf[0:64, 0:T], in0=pr[0:64, 0:T], in1=sin_sb[0:64, 0:T], op=ALU.mult),
                 reads=[prt, sin_t], writes=[tmpf_t])
            S.op("dve", lambda e, T=T: e.tensor_tensor(out=kro[:, 0:T], in0=krn[:, 0:T], in1=cos_sb[0:64, 0:T], op=ALU.mult),
                 reads=[krn_t, cos_t], writes=[kro_t])
            S.op("dve", lambda e, T=T: e.tensor_tensor(out=kro[:, 0:T], in0=kro[:, 0:T], in1=tmpf[0:64, 0:T], op=ALU.add),
                 reads=[kro_t, tmpf_t], writes=[kro_t])
            cx.store(o_kr[:, t0:t0 + T], kro[:, 0:T], [kro_t])
            for h in range(4):
                rms_stats(cx, [proj[:, 11 + h, 0:T]], [proj_t], T, 128.0, ones[:], ones_t, sq_ring, rstd, rstd_t)
                S.op("dve", lambda e, h=h, T=T: stt(e, qm_sb[:, h, 0:T], proj[:, 11 + h, 0:T], gmq_sb[:, 0:1], rstd[:, 0:T]),
                     reads=[proj_t, gmq_t, rstd_t], writes=[qm_t])
            cx.store(o_qm[:, t0:t0 + T].rearrange("(c p) t -> p c t", p=128), qm_sb[:, :, 0:T], [qm_t])
            for h in range(16):
                wb, wt = cast_load(wq_ring, wuq[h])
                pq, pqt = cx.ps()
                for c in range(4):
                    S.op("pe", lambda e, c=c, wb=wb, pq=pq, T=T: e.matmul(pq[:, 0:T], lhsT=wb[:, c, :], rhs=cq[:, c, 0:T], start=(c == 0), stop=(c == 3)),
                         reads=[wt, cq_t], writes=[pqt])
                rms_stats(cx, [pq[:, 0:T]], [pqt], T, 128.0, ones[:], ones_t, sq_ring, rstd, rstd_t)
                qo, qot, _ = qo_ring.next()
                S.op("dve", lambda e, pq=pq, qo=qo, T=T: stt(e, qo[:, 0:T], pq[:, 0:T], gq_sb[:, 0:1], rstd[:, 0:T]),
                     reads=[pqt, gq_t, rstd_t], writes=[qot])
                cx.store(o_qn[h * 128:(h + 1) * 128, t0:t0 + T], qo[:, 0:T], [qot])
            for hp in range(8):
                wb, wt = cast_load(wq_ring, wuqr[hp])
                pq, pqt = cx.ps()
                for c in range(4):
                    S.op("pe", lambda e, c=c, wb=wb, pq=pq, T=T: e.matmul(pq[:, 0:T], lhsT=wb[:, c, :], rhs=cq[:, c, 0:T], start=(c == 0), stop=(c == 3)),
                         reads=[wt, cq_t], writes=[pqt])
                rms_stats(cx, [pq[:, 0:T]], [pqt], T, 64.0, ones2[:], ones2_t, sq_ring, rstd, rstd_t)
                qrn, qrnt, _ = qrn_ring.next()
                S.op("dve", lambda e, pq=pq, qrn=qrn, T=T: stt(e, qrn[:, 0:T], pq[:, 0:T], gqr_sb[:, 0:1], rstd[:, 0:T]),
                     reads=[pqt, gqr_t, rstd_t], writes=[qrnt])
                pr, prt = cx.ps()
                S.op("pe", lambda e, pr=pr, qrn=qrn, T=T: e.matmul(pr[:, 0:T], lhsT=rot_bf[:], rhs=qrn[:, 0:T], start=True, stop=True),
                     reads=[qrnt, rot_bf_t], writes=[prt])
                S.op("dve", lambda e, pr=pr, T=T: e.tensor_tensor(out=tmpf[:, 0:T], in0=pr[:, 0:T], in1=sin_sb[:, 0:T], op=ALU.mult),
                     reads=[prt, sin_t], writes=[tmpf_t])
                qo, qot, _ = qo_ring.next()
                S.op("dve", lambda e, qrn=qrn, qo=qo, T=T: e.tensor_tensor(out=qo[:, 0:T], in0=qrn[:, 0:T], in1=cos_sb[:, 0:T], op=ALU.mult),
                     reads=[qrnt, cos_t], writes=[qot])
                S.op("dve", lambda e, qo=qo, T=T: e.tensor_tensor(out=qo[:, 0:T], in0=qo[:, 0:T], in1=tmpf[:, 0:T], op=ALU.add),
                     reads=[qot, tmpf_t], writes=[qot])
                cx.store(o_qr[hp * 128:(hp + 1) * 128, t0:t0 + T], qo[:, 0:T], [qot])
        print("A sbuf remaining", nc.sbuf_bytes_remaining)
        cx.finish()
    return nc


def _tok_index(i):
    return np.concatenate([np.arange(512 * (8 * s + i), 512 * (8 * s + i) + 512) for s in range(4)])


def _positions(i):
    return np.concatenate([_tok_index(i), PAST + np.arange(DEC_T), PAST + np.arange(DEC_T)]).astype(np.int64)


def _rope_tables(i):
    half = 32
    inv = np.power(np.float32(10000.0), -np.arange(half, dtype=np.float32) / np.float32(half)).astype(np.float32)
    pos = _positions(i).astype(np.float32)
    ang = (pos[:, None] * inv[None, :]).astype(np.float32)
    cos = np.cos(ang).astype(np.float32).T
    sin = np.sin(ang).astype(np.float32).T
    return np.ascontiguousarray(np.tile(cos, (4, 1))), np.ascontiguousarray(np.tile(sin, (4, 1)))


def _rot_matrix():
    r = np.zeros((128, 128), np.float32)
    for b in range(2):
        for m in range(32):
            r[b * 64 + m + 32, b * 64 + m] = -1.0
        for m in range(32, 64):
            r[b * 64 + m - 32, b * 64 + m] = 1.0
    return r


def _slab(w, kc, nj, f=128):
    return np.ascontiguousarray(w.reshape(kc, 128, nj, f).transpose(2, 1, 0, 3))


def _gain(g, c):
    return np.ascontiguousarray(g.reshape(c, 128).T)


WIN_STARTS = [0, 128, 256, 384, 512, 640, 768, 832, 960, 1088, 1216, 1344, 1472, 1600, 1728]
WIN_WIDTHS = [128] * 6 + [64] + [128] * 8


def _prep_A_weights(inp, l):
    w = {}
    w["g1"] = _gain(inp["ffn1_norm"][l], 16)
    w["wg"] = _slab(inp["ffn1_wg"][l], 16, NFF)
    w["wu"] = _slab(inp["ffn1_wu"][l], 16, NFF)
    w["wd"] = _slab(inp["ffn1_wd"][l], NFF, 16)
    w["gmix"] = _gain(inp["mix_norm"][l], 16)
    win = inp["w_in"][l]
    wp = np.zeros((D, 15 * 128), np.float32)
    for j, (s0, wd_) in enumerate(zip(WIN_STARTS, WIN_WIDTHS)):
        wp[:, j * 128:j * 128 + wd_] = win[:, s0:s0 + wd_]
    w["win"] = _slab(wp, 16, 15)
    w["gql"] = _gain(inp["q_lat_norm"][l], 4)
    w["gkvl"] = _gain(inp["kv_lat_norm"][l], 2)
    w["gq"] = np.ascontiguousarray(inp["q_norm"][l][:, None])
    w["gqr"] = np.ascontiguousarray(np.concatenate([inp["qr_norm"][l]] * 2)[:, None])
    w["gkr"] = np.ascontiguousarray(np.concatenate([inp["kr_norm"][l]] * 2)[:, None])
    w["gmq"] = np.ascontiguousarray(inp["mq_norm"][l][:, None])
    w["wuq"] = _slab(inp["w_uq"][l], 4, 16)
    w["wuqr"] = _slab(inp["w_uqr"][l], 4, 8)
    w["rotT"] = _rot_matrix()
    w["memT"] = np.ascontiguousarray(inp["mem_prompt"][0].T)
    w["gmem"] = _gain(inp["mem_norm"][l], 16)
    w["wmk"] = _slab(inp["w_mk"][l], 16, 4)
    w["gmk"] = np.ascontiguousarray(inp["mk_norm"][l][:, None])
    w["wmv"] = _slab(inp["w_mv"][l], 16, 4, 512)
    return w


def _initial_xT(inp, i):
    xp = inp["x_prompt"][0][_tok_index(i)]
    xs = inp["x_sample"][2 * i:2 * i + 2].reshape(2 * DEC_T, D)
    return np.ascontiguousarray(np.concatenate([xp, xs], 0).T)


SCALE = float((128 + 64) ** -0.5)
S_KEYS = PAST + DEC_T


def rms_stats_ln(cx, srcs, src_tiles, T, Dn, ones_ap, ones_tile, sq_ring, rstd, rstd_tile, npart=128):
    S = cx.S
    pn, pnt = cx.ps(6, 8)
    n = len(srcs)
    for c, (src, st) in enumerate(zip(srcs, src_tiles)):
        sq, sqt, _ = sq_ring.next()
        S.op("act", lambda e, sq=sq, src=src: e.activation(out=sq[0:npart, 0:T], in_=src, func=AF.Square),
             reads=[st], writes=[sqt])
        S.op("pe", lambda e, sq=sq, c=c: e.matmul(pn[0:npart, 0:T], lhsT=ones_ap, rhs=sq[0:npart, 0:T],
                                                  start=(c == 0), stop=(c == n - 1)),
             reads=[sqt, ones_tile], writes=[pnt])
    S.op("act", lambda e: e.activation(out=rstd[0:npart, 0:T], in_=pn[0:npart, 0:T], func=AF.Ln, bias=EPS, scale=1.0 / Dn),
         reads=[pnt], writes=[rstd_tile])
    S.op("act", lambda e: e.activation(out=rstd[0:npart, 0:T], in_=rstd[0:npart, 0:T], func=AF.Exp, scale=-0.5),
         reads=[rstd_tile], writes=[rstd_tile])


def build_B1():
    nc = bass.Bass("TRN2", target_bir_lowering=False)
    with ExitStack() as es:
        cx = Ctx(nc, es)
        S = cx.S
        qnT = cx.inp("qnT", [D, NTOK], BF16)
        qrT = cx.inp("qrT", [1024, NTOK], BF16)
        ckvA = cx.inp("ckvA", [256, SEQ])
        krA = cx.inp("krA", [64, SEQ])
        ckvS = cx.inp("ckvS", [2, 256, S_KEYS])
        krS = cx.inp("krS", [2, 64, S_KEYS])
        maskd = cx.inp("mask", [128, 32, 512])
        wuk = cx.inp("wuk", [16, 128, 2, 128])
        wuv = cx.inp("wuv", [16, 128, 2, 128])
        gk = cx.inp("gk", [128, 1])
        o_at = cx.outp("o_at", [D, NTOK], BF16)

        gk_sb, gk_t = cx.load_const(gk, [128, 1], F32)
        mask_sb, mask_t = cx.load_const(maskd, [128, 32, 512], BF16)
        krA_sb, krA_t = cx.load_const(krA, [64, SEQ], BF16)
        krS_sb = []
        for b in range(2):
            krS_sb.append(cx.load_const(krS[b], [64, S_KEYS], BF16))
        ones = cx.sb([128, 128], BF16); ones_t = Tile()
        S.op("dve", lambda e: e.memset(ones[:], 1.0), writes=[ones_t])
        KT = cx.sb([128, SEQ], BF16); KT_t = Tile()
        V = cx.sb([128, 128, 128], BF16); V_t = Tile()
        rstd = cx.sb([128, 512], F32); rstd_t = Tile()
        rinv = cx.sb([128, 512], F32); rinv_t = Tile()
        sq_ring = cx.ring([128, 512], BF16, 2)
        ck_ring = cx.ring([128, 2, 512], BF16, 3)
        p_ring = cx.ring([128, 512], BF16, 4)
        rstd_ring = cx.ring([128, 512], F32, 3)
        ao_ring = cx.ring([128, 512], BF16, 2)
        qn_ring = cx.ring([128, NTOK], BF16, 2)
        qr_ring = cx.ring([64, NTOK], BF16, 2)
        wk_ring = cx.ring([128, 2, 128], BF16, 2)
        wv_ring = cx.ring([128, 2, 128], BF16, 2)
        po, pot = cx.psb[4], cx.pst[4]
        psm, psmt = cx.psb[5], cx.pst[5]

        def build_kv(h, src_ap, nkeys, wkb, wkt, wvb, wvt):
            k0 = 0
            while k0 < nkeys:
                kl = min(512, nkeys - k0)
                ck, ckt, ds = ck_ring.next()
                S.dma("pool", lambda e, ck=ck, k0=k0, kl=kl: e.dma_start(out=ck[:, :, 0:kl], in_=src_ap[:, k0:k0 + kl].rearrange("(c p) k -> p c k", p=128)),
                      ds, writes=[ckt])
                pk, pkt = cx.ps(0, 4)
                rs, rst, _ = rstd_ring.next()
                for c in range(2):
                    S.op("pe", lambda e, c=c, ck=ck, pk=pk, kl=kl: e.matmul(pk[:, 0:kl], lhsT=wkb[:, c, :], rhs=ck[:, c, 0:kl], start=(c == 0), stop=(c == 1)),
                         reads=[wkt, ckt], writes=[pkt])
                pv, pvt = cx.ps(0, 4)
                nsub = (kl + 127) // 128
                kkmax = min(128, kl)
                for sub in range(nsub):
                    kk = min(128, kl - sub * 128)
                    for c in range(2):
                        S.op("pe", lambda e, c=c, ck=ck, pv=pv, sub=sub, kk=kk: e.matmul(pv[0:kk, sub * 128:(sub + 1) * 128], lhsT=ck[:, c, sub * 128:sub * 128 + kk],
                                                                                        rhs=wvb[:, c, :], start=(c == 0), stop=(c == 1)),
                             reads=[wvt, ckt], writes=[pvt])
                rms_stats_ln(cx, [pk[:, 0:kl]], [pkt], kl, 128.0, ones[:], ones_t, sq_ring, rs, rst)
                kt0 = k0 // 128
                S.op("act", lambda e, pv=pv, kt0=kt0, nsub=nsub, kkmax=kkmax: e.activation(
                    out=V[0:kkmax, kt0:kt0 + nsub, :], in_=pv[0:kkmax, 0:nsub * 128].rearrange("p (a b) -> p a b", b=128), func=AF.Copy),
                     reads=[pvt], writes=[V_t])
                S.op("dve", lambda e, pk=pk, k0=k0, kl=kl, rs=rs: stt(e, KT[:, k0:k0 + kl], pk[:, 0:kl], gk_sb[:, 0:1], rs[:, 0:kl]),
                     reads=[pkt, gk_t, rst], writes=[KT_t])
                k0 += kl

        def attend(h, qn, qnt, qr, qrt, kr_sb, kr_t, q_off, Tq, tiles, mask_lo):
            n = len(tiles)
            LOOK = 3
            staged = []

            def stage1(idx):
                kt, kk = tiles[idx]
                pss, psst = cx.ps(0, 4)
                S.op("pe", lambda e, pss=pss, kt=kt, kk=kk: e.matmul(pss[0:kk, 0:Tq], lhsT=KT[:, kt * 128:kt * 128 + kk], rhs=qn[:, q_off:q_off + Tq], start=True, stop=False),
                     reads=[KT_t, qnt], writes=[psst])
                S.op("pe", lambda e, pss=pss, kt=kt, kk=kk: e.matmul(pss[0:kk, 0:Tq], lhsT=kr_sb[0:64, kt * 128:kt * 128 + kk], rhs=qr[0:64, q_off:q_off + Tq], start=False, stop=True),
                     reads=[kr_t, qrt], writes=[psst])
                pT, pTt, _ = p_ring.next()
                S.op("act", lambda e, pss=pss, pT=pT, kk=kk: e.activation(out=pT[0:kk, 0:Tq], in_=pss[0:kk, 0:Tq], func=AF.Exp, scale=SCALE),
                     reads=[psst], writes=[pTt])
                if mask_lo is not None and kt >= mask_lo:
                    m = kt - mask_lo
                    S.op("dve", lambda e, pT=pT, m=m: e.tensor_tensor(out=pT[:, 0:Tq], in0=pT[:, 0:Tq], in1=mask_sb[:, m, 0:Tq], op=ALU.mult),
                         reads=[pTt, mask_t], writes=[pTt])
                staged.append((pT, pTt))

            def stage2(idx):
                kt, kk = tiles[idx]
                pT, pTt = staged[idx]
                S.op("pe", lambda e, pT=pT, kt=kt, kk=kk, idx=idx: e.matmul(po[:, 0:Tq], lhsT=V[0:kk, kt, :], rhs=pT[0:kk, 0:Tq], start=(idx == 0), stop=(idx == n - 1)),
                     reads=[V_t, pTt], writes=[pot])
                S.op("pe", lambda e, pT=pT, kk=kk, idx=idx: e.matmul(psm[:, 0:Tq], lhsT=ones[0:kk, :], rhs=pT[0:kk, 0:Tq], start=(idx == 0), stop=(idx == n - 1)),
                     reads=[ones_t, pTt], writes=[psmt])

            for idx in range(n):
                stage1(idx)
                if idx >= LOOK:
                    stage2(idx - LOOK)
            for idx in range(max(0, n - LOOK), n):
                stage2(idx)
            S.op("dve", lambda e: e.reciprocal(out=rinv[:, 0:Tq], in_=psm[:, 0:Tq]), reads=[psmt], writes=[rinv_t])
            ao, aot, _ = ao_ring.next()
            S.op("dve", lambda e, ao=ao: e.tensor_tensor(out=ao[:, 0:Tq], in0=po[:, 0:Tq], in1=rinv[:, 0:Tq], op=ALU.mult),
                 reads=[pot, rinv_t], writes=[aot])
            cx.store(o_at[h * 128:(h + 1) * 128, q_off:q_off + Tq], ao[:, 0:Tq], [aot])

        for h in range(16):
            wkb, wkt, ds = wk_ring.next()
            S.dma("pool", lambda e, wkb=wkb, h=h: e.dma_start(out=wkb[:], in_=wuk[h]), ds, writes=[wkt])
            wvb, wvt, ds = wv_ring.next()
            S.dma("pool", lambda e, wvb=wvb, h=h: e.dma_start(out=wvb[:], in_=wuv[h]), ds, writes=[wvt])
            qn, qnt, ds = qn_ring.next()
            S.dma("sp", lambda e, qn=qn, h=h: e.dma_start(out=qn[:], in_=qnT[h * 128:(h + 1) * 128, :]), ds, writes=[qnt])
            qr, qrt, ds = qr_ring.next()
            S.dma("sp", lambda e, qr=qr, h=h: e.dma_start(out=qr[:], in_=qrT[h * 64:(h + 1) * 64, :]), ds, writes=[qrt])
            build_kv(h, ckvA, SEQ, wkb, wkt, wvb, wvt)
            for s in range(4):
                attend(h, qn, qnt, qr, qrt, krA_sb, krA_t, 512 * s, 512, [(kt, 128) for kt in range(32 * (s + 1))], 32 * s)
            for b in range(2):
                build_kv(h, ckvS[b], S_KEYS, wkb, wkt, wvb, wvt)
                tiles = [(kt, 128) for kt in range(32)] + [(32, 64)]
                attend(h, qn, qnt, qr, qrt, krS_sb[b][0], krS_sb[b][1], 2048 + 64 * b, 64, tiles, None)
        print("B1 sbuf remaining", nc.sbuf_bytes_remaining)
        cx.finish()
    return nc


def _mask_for_core(i):
    p = np.arange(128)[:, None, None]
    m = np.arange(32)[None, :, None]
    f = np.arange(512)[None, None, :]
    return ((128 * m + p) // 64 <= (512 * i + f) // 64).astype(np.float32)


TILES_B = [(0, 512, 0, 0), (512, 512, 527, 0), (1024, 512, 1054, 0), (1536, 512, 1581, 0), (2048, 64, 2108, 1), (2112, 64, 2187, 2)]
PINE_W = 4 * 527 + 2 * 79
SCALE_M = float(128 ** -0.5)


def build_B2():
    nc = bass.Bass("TRN2", target_bir_lowering=False)
    with ExitStack() as es:
        cx = Ctx(nc, es)
        S = cx.S
        hT = cx.inp("hT", [D, NTOK])
        atT = cx.inp("atT", [D, NTOK], BF16)
        qmT = cx.inp("qmT", [512, NTOK], BF16)
        pinE = cx.inp("pinE", [512, PINE_W])
        icnt = cx.inp("icnt", [128, 4, NTOK])
        mk3 = cx.inp("mk3", [3, 512, 256])
        mv3 = cx.inp("mv3", [3, 256, D])
        gmix = cx.inp("gmix", [128, 16])
        poolw = cx.inp("poolw", [4, 128, 512])
        pscale = cx.inp("pscale", [128, 16])
        wgate = cx.inp("wgate", [48, 128, 16, 128])
        bgate = cx.inp("bgate", [128, 48])
        wout = cx.inp("wout", [16, 128, 16, 128])
        g2 = cx.inp("g2", [128, 16])
        wg = cx.inp("wg", [NFF, 128, 16, 128])
        wu = cx.inp("wu", [NFF, 128, 16, 128])
        wd = cx.inp("wd", [16, 128, NFF, 128])
        o_x = cx.outp("o_x", [D, NTOK])

        gmix_sb, gmix_t = cx.load_const(gmix, [128, 16], F32)
        g2_sb, g2_t = cx.load_const(g2, [128, 16], F32)
        pscale_sb, pscale_t = cx.load_const(pscale, [128, 16], F32)
        bgate_sb, bgate_t = cx.load_const(bgate, [128, 48], F32)
        poolw_sb, poolw_t = cx.load_const(poolw.rearrange("g p f -> p g f"), [128, 4, 512], BF16)
        ones = cx.sb([128, 128], BF16); ones_t = Tile()
        S.op("dve", lambda e: e.memset(ones[:], 1.0), writes=[ones_t])

        x_sb = cx.sb([128, 16, 512], F32); x_t = Tile()
        xn = cx.sb([128, 16, 512], BF16); xn_t = Tile()
        act = cx.sb([128, NFF, 512], BF16); act_t = Tile()
        at_sb = cx.sb([128, 16, 512], BF16); at_t = Tile()
        pe_sb = cx.sb([128, 4, 527], F32); pe_t = Tile()
        lvA = cx.sb([128, 527], F32); lvA_t = Tile()
        lvB = cx.sb([128, 527], F32); lvB_t = Tile()
        pooled = cx.sb([128, 4, 512], BF16); pooled_t = Tile()
        PmT = cx.sb([128, 4, 2, 512], BF16); PmT_t = Tile()
        mk_sb = cx.sb([128, 4, 256], BF16); mk_t = Tile()
        mv_sb = cx.sb([128, 2, D], BF16); mv_t = Tile()
        qm_sb = cx.sb([128, 4, 512], BF16); qm_t = Tile()
        rstd = cx.sb([128, 512], F32); rstd_t = Tile()
        rinv = cx.sb([128, 512], F32); rinv_t = Tile()
        mrg = cx.sb([128, 512], F32); mrg_t = Tile()
        tmp1 = cx.sb([128, 512], F32); tmp1_t = Tile()
        icnt_ring = cx.ring([128, 512], F32, 2)
        gate_ring = cx.ring([128, 512], F32, 3)
        sq_ring = cx.ring([128, 512], BF16, 2)
        silu_ring = cx.ring([128, 512], BF16, 2)
        w_ring = cx.ring([128, 16, 128], BF16, 3)
        wd_ring = cx.ring([128, NFF, 128], BF16, 2)
        ld_ds = S.dsem()
        at_ds = S.dsem()
        qm_ds = S.dsem()
        pe_ds = S.dsem()
        mem_ds = S.dsem()
        mv_ds = S.dsem()

        def cast_load(ring, src_ap):
            buf, tl, ds = ring.next()
            S.dma("pool", lambda e: e.dma_start(out=buf[:], in_=src_ap), ds, writes=[tl])
            return buf, tl

        def rmsnorm16(T, g_sb, g_t):
            rms_stats(cx, [x_sb[:, c, 0:T] for c in range(16)], [x_t] * 16, T, float(D), ones[:], ones_t, sq_ring, rstd, rstd_t)
            for c in range(16):
                S.op("dve", lambda e, c=c: stt(e, xn[:, c, 0:T], x_sb[:, c, 0:T], g_sb[:, c:c + 1], rstd[:, 0:T]),
                     reads=[x_t, g_t, rstd_t], writes=[xn_t])

        for (t0, T, seg, ms) in TILES_B:
            L = 15 + T
            S.dma("sp", lambda e, t0=t0, T=T: e.dma_start(out=x_sb[:, :, 0:T], in_=hT[:, t0:t0 + T].rearrange("(c p) t -> p c t", p=128)), ld_ds, writes=[x_t])
            S.dma("sp", lambda e, t0=t0, T=T: e.dma_start(out=at_sb[:, :, 0:T], in_=atT[:, t0:t0 + T].rearrange("(c p) t -> p c t", p=128)), at_ds, writes=[at_t])
            S.dma("sp", lambda e, t0=t0, T=T: e.dma_start(out=qm_sb[:, :, 0:T], in_=qmT[:, t0:t0 + T].rearrange("(c p) t -> p c t", p=128)), qm_ds, writes=[qm_t])
            S.dma("sp", lambda e, seg=seg, L=L: e.dma_start(out=pe_sb[:, :, 0:L], in_=pinE[:, seg:seg + L].rearrange("(g p) t -> p g t", p=128)), pe_ds, writes=[pe_t])
            S.dma("pool", lambda e, ms=ms: e.dma_start(out=mk_sb[:], in_=mk3[ms].rearrange("(h p) m -> p h m", p=128)), mem_ds, writes=[mk_t])
            S.dma("pool", lambda e, ms=ms: e.dma_start(out=mv_sb[:], in_=mv3[ms].rearrange("(c p) f -> p c f", p=128)), mv_ds, writes=[mv_t])
            rmsnorm16(T, gmix_sb, gmix_t)
            for g in range(4):
                e_ = lambda a, b, g=g: pe_sb[:, g, a:b]
                S.op("pool", lambda e, g=g, L=L: e.tensor_tensor(out=lvA[:, 1:L], in0=pe_sb[:, g, 1:L], in1=pe_sb[:, g, 0:L - 1], op=ALU.add),
                     reads=[pe_t], writes=[lvA_t])
                cur, cur_t = lvA, lvA_t
                if g >= 1:
                    S.op("pool", lambda e, L=L: e.tensor_tensor(out=lvB[:, 3:L], in0=lvA[:, 3:L], in1=lvA[:, 1:L - 2], op=ALU.add),
                         reads=[lvA_t], writes=[lvB_t])
                    cur, cur_t = lvB, lvB_t
                if g >= 2:
                    S.op("pool", lambda e, L=L: e.tensor_tensor(out=lvA[:, 7:L], in0=lvB[:, 7:L], in1=lvB[:, 3:L - 4], op=ALU.add),
                         reads=[lvB_t, lvA_t], writes=[lvA_t])
                    cur, cur_t = lvA, lvA_t
                if g >= 3:
                    S.op("pool", lambda e, L=L: e.tensor_tensor(out=lvB[:, 15:L], in0=lvA[:, 15:L], in1=lvA[:, 7:L - 8], op=ALU.add),
                         reads=[lvA_t, lvB_t], writes=[lvB_t])
                    cur, cur_t = lvB, lvB_t
                ic, ict, ds = icnt_ring.next()
                S.dma("sp", lambda e, ic=ic, g=g, t0=t0, T=T: e.dma_start(out=ic[:, 0:T], in_=icnt[:, g, t0:t0 + T]), ds, writes=[ict])
                S.op("dve", lambda e, cur=cur, ic=ic, L=L, T=T: e.tensor_tensor(out=tmp1[:, 0:T], in0=cur[:, 15:L], in1=ic[:, 0:T], op=ALU.mult),
                     reads=[cur_t, ict], writes=[tmp1_t])
                S.op("dve", lambda e, g=g, L=L, T=T: e.tensor_tensor(out=pooled[:, g, 0:T], in0=tmp1[:, 0:T], in1=pe_sb[:, g, 15:L], op=ALU.subtract),
                     reads=[tmp1_t, pe_t], writes=[pooled_t])
            for hm in range(4):
                psm, psmt = cx.ps(6, 8)
                for mc in range(2):
                    pss, psst = cx.ps()
                    S.op("pe", lambda e, pss=pss, hm=hm, mc=mc, T=T: e.matmul(pss[:, 0:T], lhsT=mk_sb[:, hm, mc * 128:(mc + 1) * 128], rhs=qm_sb[:, hm, 0:T], start=True, stop=True),
                         reads=[mk_t, qm_t], writes=[psst])
                    S.op("act", lambda e, pss=pss, hm=hm, mc=mc, T=T: e.activation(out=PmT[:, hm, mc, 0:T], in_=pss[:, 0:T], func=AF.Exp, scale=SCALE_M),
                         reads=[psst], writes=[PmT_t])
                for mc in range(2):
                    S.op("pe", lambda e, psm=psm, hm=hm, mc=mc, T=T: e.matmul(psm[:, 0:T], lhsT=ones[:], rhs=PmT[:, hm, mc, 0:T], start=(mc == 0), stop=(mc == 1)),
                         reads=[ones_t, PmT_t], writes=[psmt])
                S.op("dve", lambda e, psm=psm, T=T: e.reciprocal(out=rinv[:, 0:T], in_=psm[:, 0:T]), reads=[psmt], writes=[rinv_t])
                for mc in range(2):
                    S.op("dve", lambda e, hm=hm, mc=mc, T=T: e.tensor_tensor(out=PmT[:, hm, mc, 0:T], in0=PmT[:, hm, mc, 0:T], in1=rinv[:, 0:T], op=ALU.mult),
                         reads=[PmT_t, rinv_t], writes=[PmT_t])
            for dc in range(16):
                g = dc // 4
                cc = dc % 4
                pp, ppt = cx.ps()
                S.op("pe", lambda e, pp=pp, g=g, cc=cc, T=T: e.matmul(pp[:, 0:T], lhsT=poolw_sb[:, g, cc * 128:(cc + 1) * 128], rhs=pooled[:, g, 0:T], start=True, stop=True),
                     reads=[poolw_t, pooled_t], writes=[ppt])
                pm, pmt = cx.ps()
                for mc in range(2):
                    S.op("pe", lambda e, pm=pm, dc=dc, g=g, mc=mc, T=T: e.matmul(pm[:, 0:T], lhsT=mv_sb[:, mc, dc * 128:(dc + 1) * 128], rhs=PmT[:, g, mc, 0:T], start=(mc == 0), stop=(mc == 1)),
                         reads=[mv_t, PmT_t], writes=[pmt])
                gts = []
                for br in range(3):
                    wb, wt = cast_load(w_ring, wgate[br * 16 + dc])
                    pg, pgt = cx.ps()
                    for c in range(16):
                        S.op("pe", lambda e, c=c, wb=wb, pg=pg, T=T: e.matmul(pg[:, 0:T], lhsT=wb[:, c, :], rhs=xn[:, c, 0:T], start=(c == 0), stop=(c == 15)),
                             reads=[wt, xn_t], writes=[pgt])
                    gt, gtt, _ = gate_ring.next()
                    S.op("act", lambda e, pg=pg, gt=gt, br=br, dc=dc, T=T: e.activation(out=gt[:, 0:T], in_=pg[:, 0:T], func=AF.Sigmoid,
                                                                                       bias=bgate_sb[:, br * 16 + dc:br * 16 + dc + 1]),
                         reads=[pgt, bgate_t], writes=[gtt])
                    gts.append((gt, gtt))
                S.op("dve", lambda e, dc=dc, gt=gts[0][0], T=T: e.tensor_tensor(out=mrg[:, 0:T], in0=at_sb[:, dc, 0:T], in1=gt[:, 0:T], op=ALU.mult),
                     reads=[at_t, gts[0][1]], writes=[mrg_t])
                S.op("dve", lambda e, dc=dc, pp=pp, gt=gts[1][0], T=T: stt(e, tmp1[:, 0:T], pp[:, 0:T], pscale_sb[:, dc:dc + 1], gt[:, 0:T]),
                     reads=[ppt, pscale_t, gts[1][1]], writes=[tmp1_t])
                S.op("dve", lambda e, T=T: e.tensor_tensor(out=mrg[:, 0:T], in0=mrg[:, 0:T], in1=tmp1[:, 0:T], op=ALU.add),
                     reads=[mrg_t, tmp1_t], writes=[mrg_t])
                S.op("dve", lambda e, pm=pm, gt=gts[2][0], T=T: e.tensor_tensor(out=tmp1[:, 0:T], in0=pm[:, 0:T], in1=gt[:, 0:T], op=ALU.mult),
                     reads=[pmt, gts[2][1]], writes=[tmp1_t])
                S.op("dve", lambda e, dc=dc, T=T: e.tensor_tensor(out=at_sb[:, dc, 0:T], in0=mrg[:, 0:T], in1=tmp1[:, 0:T], op=ALU.add),
                     reads=[mrg_t, tmp1_t], writes=[at_t])
            for dco in range(16):
                wb, wt = cast_load(w_ring, wout[dco])
                pw, pwt = cx.ps()
                for c in range(16):
                    S.op("pe", lambda e, c=c, wb=wb, pw=pw, T=T: e.matmul(pw[:, 0:T], lhsT=wb[:, c, :], rhs=at_sb[:, c, 0:T], start=(c == 0), stop=(c == 15)),
                         reads=[wt, at_t], writes=[pwt])
                S.op("dve", lambda e, dco=dco, pw=pw, T=T: e.tensor_tensor(out=x_sb[:, dco, 0:T], in0=x_sb[:, dco, 0:T], in1=pw[:, 0:T], op=ALU.add),
                     reads=[pwt, x_t], writes=[x_t])
            rmsnorm16(T, g2_sb, g2_t)
            for j in range(NFF):
                wgb, wgt = cast_load(w_ring, wg[j])
                wub, wut = cast_load(w_ring, wu[j])
                pg, pgt = cx.ps()
                pu, put = cx.ps()
                for c in range(16):
                    S.op("pe", lambda e, c=c, wgb=wgb, pg=pg, T=T: e.matmul(pg[:, 0:T], lhsT=wgb[:, c, :], rhs=xn[:, c, 0:T], start=(c == 0), stop=(c == 15)),
                         reads=[wgt, xn_t], writes=[pgt])
                for c in range(16):
                    S.op("pe", lambda e, c=c, wub=wub, pu=pu, T=T: e.matmul(pu[:, 0:T], lhsT=wub[:, c, :], rhs=xn[:, c, 0:T], start=(c == 0), stop=(c == 15)),
                         reads=[wut, xn_t], writes=[put])
                sl, slt, _ = silu_ring.next()
                S.op("act", lambda e, sl=sl, pg=pg, T=T: e.activation(out=sl[:, 0:T], in_=pg[:, 0:T], func=AF.Silu), reads=[pgt], writes=[slt])
                S.op("dve", lambda e, sl=sl, pu=pu, j=j, T=T: e.tensor_tensor(out=act[:, j, 0:T], in0=sl[:, 0:T], in1=pu[:, 0:T], op=ALU.mult),
                     reads=[slt, put], writes=[act_t])
            for c in range(16):
                wdb, wdt = cast_load(wd_ring, wd[c])
                po, pot = cx.ps()
                for j in range(NFF):
                    S.op("pe", lambda e, j=j, wdb=wdb, po=po, T=T: e.matmul(po[:, 0:T], lhsT=wdb[:, j, :], rhs=act[:, j, 0:T], start=(j == 0), stop=(j == NFF - 1)),
                         reads=[wdt, act_t], writes=[pot])
                S.op("dve", lambda e, c=c, po=po, T=T: stt(e, x_sb[:, c, 0:T], po[:, 0:T], 0.5, x_sb[:, c, 0:T], ALU.mult, ALU.add),
                     reads=[pot, x_t], writes=[x_t])
            cx.store(o_x[:, t0:t0 + T].rearrange("(c p) t -> p c t", p=128), x_sb[:, :, 0:T], [x_t])
        print("B2 sbuf remaining", nc.sbuf_bytes_remaining)
        cx.finish()
    return nc


_PROGS = {}


def _prog(name, fn):
    if name not in _PROGS:
        _PROGS[name] = fn()
    return _PROGS[name]


def _run(nc, in_maps):
    res = run_bass_kernel_spmd(nc, in_maps, core_ids=list(range(NCORES)))
    return [{k: np.asarray(v) for k, v in r.items()} for r in res.results]


def _icnt(i):
    pos = _positions(i)
    out = np.empty((4, NTOK), np.float32)
    for g, w in enumerate((2, 4, 8, 16)):
        out[g] = np.float32(1.0) / np.minimum(pos + 1, w).astype(np.float32)
    return np.ascontiguousarray(np.broadcast_to(out[None], (128, 4, NTOK)))


def kernel(**inp):
    inp = {k: np.asarray(v) for k, v in inp.items()}
    ncA = _prog("A", build_A)
    ncB1 = _prog("B1", build_B1)
    ncB2 = _prog("B2", build_B2)
    xT = [_initial_xT(inp, i) for i in range(NCORES)]
    tabs = [_rope_tables(i) for i in range(NCORES)]
    masks = [_mask_for_core(i) for i in range(NCORES)]
    icnts = [_icnt(i) for i in range(NCORES)]
    L = 2
    o_ckv_p = np.zeros((L, 1, SEQ, 256), np.float32)
    o_kr_p = np.zeros((L, 1, SEQ, 64), np.float32)
    o_pool_p = np.zeros((L, 1, 15, 512), np.float32)
    o_mk_p = np.zeros((L, 1, 256, 4, 128), np.float32)
    o_mv_p = np.zeros((L, 1, 256, 4, 512), np.float32)
    o_ckv_s = np.zeros((L, DEC_B, DEC_T, 256), np.float32)
    o_kr_s = np.zeros((L, DEC_B, DEC_T, 64), np.float32)
    o_pool_s = np.zeros((L, DEC_B, 15, 512), np.float32)
    for l in range(L):
        wA = _prep_A_weights(inp, l)
        maps = []
        for i in range(NCORES):
            m = dict(wA)
            m["xT"] = xT[i]
            m["cosT"], m["sinT"] = tabs[i]
            maps.append(m)
        rA = _run(ncA, maps)
        del maps, wA
        ckv_all = np.zeros((256, SEQ), np.float32)
        kr_all = np.zeros((64, SEQ), np.float32)
        pin_all = np.zeros((512, SEQ), np.float32)
        for i in range(NCORES):
            for s in range(4):
                g0 = 512 * (8 * s + i)
                ckv_all[:, g0:g0 + 512] = rA[i]["o_ckv"][:, 512 * s:512 * s + 512]
                kr_all[:, g0:g0 + 512] = rA[i]["o_kr"][:, 512 * s:512 * s + 512]
                pin_all[:, g0:g0 + 512] = rA[i]["o_pin"][:, 512 * s:512 * s + 512]
        o_ckv_p[l, 0] = ckv_all.T
        o_kr_p[l, 0] = kr_all.T
        o_pool_p[l, 0] = pin_all[:, SEQ - 15:].T
        o_mk_p[l, 0] = rA[0]["o_mk"].T.reshape(256, 4, 128)
        o_mv_p[l, 0] = rA[0]["o_mv"].reshape(256, 4, 512)
        for b in range(DEC_B):
            i, bb = b // 2, b % 2
            c0 = 2048 + 64 * bb
            o_ckv_s[l, b] = rA[i]["o_ckv"][:, c0:c0 + 64].T
            o_kr_s[l, b] = rA[i]["o_kr"][:, c0:c0 + 64].T
            o_pool_s[l, b] = rA[i]["o_pin"][:, c0 + 64 - 15:c0 + 64].T
        wuk = _slab(inp["w_uk"][l], 2, 16)
        wuv = _slab(inp["w_uv"][l], 2, 16)
        gk = np.ascontiguousarray(inp["k_norm"][l][:, None])
        maps = []
        for i in range(NCORES):
            m = {"qnT": rA[i]["o_qn"], "qrT": rA[i]["o_qr"], "ckvA": ckv_all, "krA": kr_all, "mask": masks[i],
                 "wuk": wuk, "wuv": wuv, "gk": gk}
            cs, ks = [], []
            for bb in range(2):
                b = 2 * i + bb
                c0 = 2048 + 64 * bb
                cs.append(np.concatenate([inp["cache_ckv"][l, b].T, rA[i]["o_ckv"][:, c0:c0 + 64]], axis=1))
                ks.append(np.concatenate([inp["cache_krope"][l, b].T, rA[i]["o_kr"][:, c0:c0 + 64]], axis=1))
            m["ckvS"] = np.ascontiguousarray(np.stack(cs))
            m["krS"] = np.ascontiguousarray(np.stack(ks))
            maps.append(m)
        rB1 = _run(ncB1, maps)
        del maps
        wB = {
            "gmix": _gain(inp["mix_norm"][l], 16),
            "poolw": np.ascontiguousarray(inp["pool_w"][l]),
            "pscale": _gain(inp["pool_scale"][l], 16),
            "wgate": _slab(inp["w_gate"][l], 16, 48),
            "bgate": _gain(inp["b_gate"][l], 48),
            "wout": _slab(inp["w_out"][l], 16, 16),
            "g2": _gain(inp["ffn2_norm"][l], 16),
            "wg": _slab(inp["ffn2_wg"][l], 16, NFF),
            "wu": _slab(inp["ffn2_wu"][l], 16, NFF),
            "wd": _slab(inp["ffn2_wd"][l], NFF, 16),
        }
        maps = []
        for i in range(NCORES):
            m = dict(wB)
            m["hT"] = rA[i]["o_h"]
            m["atT"] = rB1[i]["o_at"]
            m["qmT"] = rA[i]["o_qm"]
            segs = []
            for s in range(4):
                g0 = 512 * (8 * s + i)
                halo = pin_all[:, g0 - 15:g0] if g0 > 0 else np.zeros((512, 15), np.float32)
                segs.append(np.concatenate([halo, pin_all[:, g0:g0 + 512]], axis=1))
            mks = [rA[0]["o_mk"]]
            mvs = [rA[0]["o_mv"]]
            for bb in range(2):
                b = 2 * i + bb
                c0 = 2048 + 64 * bb
                segs.append(np.concatenate([inp["state_pool"][l, b].T, rA[i]["o_pin"][:, c0:c0 + 64]], axis=1))
                mks.append(inp["cache_mem_k"][l, b].reshape(256, 512).T)
                mvs.append(inp["cache_mem_v"][l, b].reshape(256, D))
            m["pinE"] = np.ascontiguousarray(np.concatenate(segs, axis=1))
            m["icnt"] = icnts[i]
            m["mk3"] = np.ascontiguousarray(np.stack(mks))
            m["mv3"] = np.ascontiguousarray(np.stack(mvs))
            maps.append(m)
        rB2 = _run(ncB2, maps)
        del maps, wB
        xT = [rB2[i]["o_x"] for i in range(NCORES)]
    y_p = np.zeros((1, SEQ, D), np.float32)
    y_s = np.zeros((DEC_B, DEC_T, D), np.float32)
    for i in range(NCORES):
        y_p[0, _tok_index(i)] = xT[i][:, 0:2048].T
        for bb in range(2):
            y_s[2 * i + bb] = xT[i][:, 2048 + 64 * bb:2048 + 64 * bb + 64].T
    return (y_p, y_s, o_ckv_p, o_kr_p, o_pool_p, o_mk_p, o_mv_p, o_ckv_s, o_kr_s, o_pool_s)
```

```python
import numpy as np
import ml_dtypes
from contextlib import ExitStack
import concourse.bass as bass
import concourse.mybir as mybir
from concourse.bass_utils import run_bass_kernel_spmd

F32 = mybir.dt.float32
BF16 = mybir.dt.bfloat16
AF = mybir.ActivationFunctionType
ALU = mybir.AluOpType

NCORES = 8
D = 2048
DFF = 5632
NFF = DFF // 128
SEQ = 16384
PAST = 4096
DEC_B = 16
DEC_T = 64
NTOK = 2176
TILES = [(0, 512), (512, 512), (1024, 512), (1536, 512), (2048, 128)]
EPS = 1e-6
IN_W = 1856


class Tile:
    __slots__ = ("name", "w", "r")

    def __init__(self, name=""):
        self.name = name
        self.w = None
        self.r = []


class DSem:
    def __init__(self, sem):
        self.sem = sem
        self.count = 0


class Op:
    __slots__ = ("eng", "fn", "deps", "marked", "val", "dsem", "dval")


class Sched:
    ENGS = ("pe", "act", "dve", "pool", "sp")

    def __init__(self, nc, es):
        self.nc = nc
        self.es = es
        self.ops = {e: [] for e in self.ENGS}
        self.sems = {e: es.enter_context(nc.semaphore("s_" + e)) for e in self.ENGS}
        self.n_dsem = 0

    def dsem(self):
        self.n_dsem += 1
        return DSem(self.es.enter_context(self.nc.semaphore("d%d" % self.n_dsem)))

    def _deps(self, reads, writes):
        deps = []
        for t in reads:
            if t.w is not None:
                deps.append(t.w)
        for t in writes:
            if t.w is not None:
                deps.append(t.w)
            deps.extend(t.r)
        return deps

    def _reg(self, o, tok, reads, writes):
        for t in reads:
            t.r.append(tok)
        for t in writes:
            t.w = tok
            t.r = []

    def op(self, eng, fn, reads=(), writes=()):
        o = Op()
        o.eng = eng
        o.fn = fn
        o.deps = self._deps(reads, writes)
        o.marked = False
        o.val = None
        o.dsem = None
        self.ops[eng].append(o)
        self._reg(o, ("e", o), reads, writes)
        return o

    def dma(self, eng, fn, dsem, reads=(), writes=()):
        o = Op()
        o.eng = eng
        o.fn = fn
        o.deps = self._deps(reads, writes)
        o.marked = False
        o.val = None
        o.dsem = dsem
        dsem.count += 16
        o.dval = dsem.count
        self.ops[eng].append(o)
        self._reg(o, ("d", dsem, o.dval), reads, writes)
        return o

    def emit(self, final_dsems=()):
        nc = self.nc
        for e in self.ENGS:
            for o in self.ops[e]:
                for d in o.deps:
                    if d[0] == "e":
                        p = d[1]
                        if p.eng == "pe" and o.eng == "pe" and o.dsem is None:
                            continue
                        p.marked = True
        for e in self.ENGS:
            c = 0
            for o in self.ops[e]:
                if o.marked:
                    assert o.dsem is None
                    c += 1
                    o.val = c
        sems = self.sems
        with nc.Block() as block:
            def run(engname, engine):
                waited = {}
                for o in self.ops[engname]:
                    need = {}
                    for d in o.deps:
                        if d[0] == "e":
                            p = d[1]
                            if p.eng == "pe" and engname == "pe" and o.dsem is None:
                                continue
                            key = ("e", p.eng)
                            v = p.val
                            s = sems[p.eng]
                        else:
                            key = ("d", id(d[1]))
                            v = d[2]
                            s = d[1].sem
                        if waited.get(key, 0) >= v:
                            continue
                        if key not in need or need[key][1] < v:
                            need[key] = (s, v)
                    for key, (s, v) in need.items():
                        engine.wait_ge(s, v)
                        waited[key] = v
                    ins = o.fn(engine)
                    if o.dsem is not None:
                        ins.then_inc(o.dsem.sem, 16)
                    elif o.marked:
                        ins.then_inc(sems[engname], 1)
                if engname == "sp":
                    for ds in final_dsems:
                        engine.wait_ge(ds.sem, ds.count)

            @block.tensor
            def _(e):
                run("pe", e)

            @block.scalar
            def _(e):
                run("act", e)

            @block.vector
            def _(e):
                run("dve", e)

            @block.gpsimd
            def _(e):
                run("pool", e)

            @block.sync
            def _(e):
                run("sp", e)


class Ctx:
    def __init__(self, nc, es):
        self.nc = nc
        self.es = es
        self.S = Sched(nc, es)
        self.n = 0
        self.psb = []
        self.pst = []
        for i in range(8):
            self.psb.append(es.enter_context(nc.psum_tensor("ps%d" % i, [128, 512], F32)))
            self.pst.append(Tile("ps%d" % i))
        self.ps_i = 0
        self.store_ds = {}

    def sb(self, shape, dt, name=None):
        self.n += 1
        return self.es.enter_context(self.nc.sbuf_tensor(name or ("t%d" % self.n), shape, dt))

    def ps(self, lo=0, hi=6):
        k = lo + self.ps_i % (hi - lo)
        self.ps_i += 1
        return self.psb[k], self.pst[k]

    def ring(self, shape, dt, n):
        return Ring(self, shape, dt, n)

    def inp(self, name, shape, dt=F32):
        return self.nc.dram_tensor(name, list(shape), dt, kind="ExternalInput").ap()

    def outp(self, name, shape, dt=F32):
        return self.nc.dram_tensor(name, list(shape), dt, kind="ExternalOutput").ap()

    def store(self, out_ap, in_ap, reads):
        key = id(reads[0])
        if key not in self.store_ds:
            self.store_ds[key] = self.S.dsem()
        self.S.dma("sp", lambda e: e.dma_start(out=out_ap, in_=in_ap), self.store_ds[key], reads=reads)

    def finish(self):
        self.S.emit(final_dsems=list(self.store_ds.values()))

    def load_const(self, dram_ap, shape, dt, eng=None):
        t = self.sb(shape, dt)
        tl = Tile()
        ds = self.S.dsem()
        src_dt_differs = (dt != F32)
        q = "pool" if src_dt_differs else "sp"
        self.S.dma(q, lambda e: e.dma_start(out=t[:], in_=dram_ap), ds, writes=[tl])
        return t, tl


class Ring:
    def __init__(self, cx, shape, dt, n):
        self.bufs = [cx.sb(shape, dt) for _ in range(n)]
        self.tiles = [Tile() for _ in range(n)]
        self.dsems = [cx.S.dsem() for _ in range(n)]
        self.n = n
        self.i = 0

    def next(self):
        k = self.i % self.n
        self.i += 1
        return self.bufs[k], self.tiles[k], self.dsems[k]


def rms_stats(cx, srcs, src_tiles, T, Dn, ones_ap, ones_tile, sq_ring, rstd, rstd_tile, npart=128):
    S = cx.S
    pn, pnt = cx.ps(6, 8)
    n = len(srcs)
    for c, (src, st) in enumerate(zip(srcs, src_tiles)):
        sq, sqt, _ = sq_ring.next()
        S.op("act", lambda e, sq=sq, src=src: e.activation(out=sq[0:npart, 0:T], in_=src, func=AF.Square),
             reads=[st], writes=[sqt])
        S.op("pe", lambda e, sq=sq, c=c: e.matmul(pn[0:npart, 0:T], lhsT=ones_ap, rhs=sq[0:npart, 0:T],
                                                  start=(c == 0), stop=(c == n - 1)),
             reads=[sqt, ones_tile], writes=[pnt])
    S.op("act", lambda e: e.activation(out=rstd[0:npart, 0:T], in_=pn[0:npart, 0:T], func=AF.Ln, bias=EPS, scale=1.0 / Dn),
         reads=[pnt], writes=[rstd_tile])
    S.op("act", lambda e: e.activation(out=rstd[0:npart, 0:T], in_=rstd[0:npart, 0:T], func=AF.Exp, scale=-0.5),
         reads=[rstd_tile], writes=[rstd_tile])


def stt(e, out, in0, scalar, in1, op0=ALU.mult, op1=ALU.mult):
    return e.scalar_tensor_tensor(out=out, in0=in0, scalar=scalar, in1=in1, op0=op0, op1=op1)


def build_A():
    nc = bass.Bass("TRN2", target_bir_lowering=False)
    with ExitStack() as es:
        cx = Ctx(nc, es)
        S = cx.S
        xT = cx.inp("xT", [D, NTOK])
        g1 = cx.inp("g1", [128, 16])
        wg = cx.inp("wg", [NFF, 128, 16, 128])
        wu = cx.inp("wu", [NFF, 128, 16, 128])
        wd = cx.inp("wd", [16, 128, NFF, 128])
        gmix = cx.inp("gmix", [128, 16])
        win = cx.inp("win", [15, 128, 16, 128])
        gql = cx.inp("gql", [128, 4])
        gkvl = cx.inp("gkvl", [128, 2])
        gq = cx.inp("gq", [128, 1])
        gqr = cx.inp("gqr", [128, 1])
        gkr = cx.inp("gkr", [128, 1])
        gmq = cx.inp("gmq", [128, 1])
        wuq = cx.inp("wuq", [16, 128, 4, 128])
        wuqr = cx.inp("wuqr", [8, 128, 4, 128])
        cosT = cx.inp("cosT", [128, NTOK])
        sinT = cx.inp("sinT", [128, NTOK])
        rotT = cx.inp("rotT", [128, 128])
        memT = cx.inp("memT", [D, 256])
        gmem = cx.inp("gmem", [128, 16])
        wmk = cx.inp("wmk", [4, 128, 16, 128])
        gmk = cx.inp("gmk", [128, 1])
        wmv = cx.inp("wmv", [4, 128, 16, 512])
        o_h = cx.outp("o_h", [D, NTOK])
        o_qn = cx.outp("o_qn", [D, NTOK], BF16)
        o_qr = cx.outp("o_qr", [1024, NTOK], BF16)
        o_ckv = cx.outp("o_ckv", [256, NTOK])
        o_kr = cx.outp("o_kr", [64, NTOK])
        o_pin = cx.outp("o_pin", [512, NTOK])
        o_qm = cx.outp("o_qm", [512, NTOK], BF16)
        o_mk = cx.outp("o_mk", [512, 256])
        o_mv = cx.outp("o_mv", [256, D])

        g1_sb, g1_t = cx.load_const(g1, [128, 16], F32)
        gmix_sb, gmix_t = cx.load_const(gmix, [128, 16], F32)
        gql_sb, gql_t = cx.load_const(gql, [128, 4], F32)
        gkvl_sb, gkvl_t = cx.load_const(gkvl, [128, 2], F32)
        gq_sb, gq_t = cx.load_const(gq, [128, 1], F32)
        gqr_sb, gqr_t = cx.load_const(gqr, [128, 1], F32)
        gkr_sb, gkr_t = cx.load_const(gkr, [128, 1], F32)
        gmq_sb, gmq_t = cx.load_const(gmq, [128, 1], F32)
        gmem_sb, gmem_t = cx.load_const(gmem, [128, 16], F32)
        gmk_sb, gmk_t = cx.load_const(gmk, [128, 1], F32)
        rot_bf, rot_bf_t = cx.load_const(rotT, [128, 128], BF16)
        rot_f, rot_f_t = cx.load_const(rotT, [128, 128], F32)
        ones = cx.sb([128, 128], BF16)
        ones_t = Tile()
        S.op("dve", lambda e: e.memset(ones[:], 1.0), writes=[ones_t])
        ones2 = cx.sb([128, 128], BF16)
        ones2_t = Tile()
        S.op("dve", lambda e: e.memset(ones2[:], 0.0), writes=[ones2_t])
        S.op("dve", lambda e: e.memset(ones2[0:64, 0:64], 1.0), reads=[ones2_t], writes=[ones2_t])
        S.op("dve", lambda e: e.memset(ones2[64:128, 64:128], 1.0), reads=[ones2_t], writes=[ones2_t])

        onesf = cx.sb([64, 64], F32)
        onesf_t = Tile()
        S.op("dve", lambda e: e.memset(onesf[:], 1.0), writes=[onesf_t])
        x_sb = cx.sb([128, 16, 512], F32); x_t = Tile()
        xn = cx.sb([128, 16, 512], BF16); xn_t = Tile()
        act = cx.sb([128, NFF, 512], BF16); act_t = Tile()
        proj = cx.sb([128, 15, 512], F32); proj_t = Tile()
        cq = cx.sb([128, 4, 512], BF16); cq_t = Tile()
        ckv_sb = cx.sb([128, 2, 512], F32); ckv_t = Tile()
        qm_sb = cx.sb([128, 4, 512], BF16); qm_t = Tile()
        cos_sb = cx.sb([128, 512], F32); cos_t = Tile()
        sin_sb = cx.sb([128, 512], F32); sin_t = Tile()
        krn = cx.sb([64, 512], F32); krn_t = Tile()
        kro = cx.sb([64, 512], F32); kro_t = Tile()
        tmpf = cx.sb([128, 512], F32); tmpf_t = Tile()
        rstd = cx.sb([128, 512], F32); rstd_t = Tile()
        silu_ring = cx.ring([128, 512], BF16, 2)
        sq_ring = cx.ring([128, 512], BF16, 3)
        sqf_ring = cx.ring([128, 512], F32, 1)
        qo_ring = cx.ring([128, 512], BF16, 2)
        qrn_ring = cx.ring([128, 512], BF16, 2)
        w_ring = cx.ring([128, 16, 128], BF16, 5)
        wd_ring = cx.ring([128, NFF, 128], BF16, 2)
        wq_ring = cx.ring([128, 4, 128], BF16, 4)
        cs_ds = S.dsem()
        cos_ds = S.dsem()
        sin_ds = S.dsem()

        def cast_load(ring, src_ap, shape_slice=None):
            buf, tl, ds = ring.next()
            dst = buf[:] if shape_slice is None else shape_slice(buf)
            S.dma("pool", lambda e: e.dma_start(out=dst, in_=src_ap), ds, writes=[tl])
            return buf, tl

        def rmsnorm_chunks(src_sb, src_t, C, T, g_sb, g_t, out_sb, out_t, c0=0):
            rms_stats(cx, [src_sb[:, c0 + c, 0:T] for c in range(C)], [src_t] * C, T, 128.0 * C,
                      ones[:], ones_t, sq_ring, rstd, rstd_t)
            for c in range(C):
                S.op("dve", lambda e, c=c: stt(e, out_sb[:, c, 0:T], src_sb[:, c0 + c, 0:T], g_sb[:, c:c + 1], rstd[:, 0:T]),
                     reads=[src_t, g_t, rstd_t], writes=[out_t])

        m_sb = x_sb; m_t = x_t
        mn = xn; mn_t = xn_t
        S.dma("sp", lambda e: e.dma_start(out=m_sb[:, :, 0:256], in_=memT.rearrange("(c p) t -> p c t", p=128)), cs_ds, writes=[m_t])
        rmsnorm_chunks(m_sb, m_t, 16, 256, gmem_sb, gmem_t, mn, mn_t)
        mk_sb = proj; mk_t = proj_t
        for h in range(4):
            wb, wt = cast_load(w_ring, wmk[h])
            pk, pkt = cx.ps()
            for c in range(16):
                S.op("pe", lambda e, c=c, wb=wb, pk=pk: e.matmul(pk[:, 0:256], lhsT=wb[:, c, :], rhs=mn[:, c, 0:256], start=(c == 0), stop=(c == 15)),
                     reads=[wt, mn_t], writes=[pkt])
            rms_stats(cx, [pk[:, 0:256]], [pkt], 256, 128.0, ones[:], ones_t, sq_ring, rstd, rstd_t)
            S.op("dve", lambda e, h=h, pk=pk: stt(e, mk_sb[:, h, 0:256], pk[:, 0:256], gmk_sb[:, 0:1], rstd[:, 0:256]),
                 reads=[pkt, gmk_t, rstd_t], writes=[mk_t])
        cx.store(o_mk.rearrange("(c p) t -> p c t", p=128), mk_sb[:, 0:4, 0:256], [mk_t])
        mv_t = proj_t
        wmv_ds = S.dsem()
        for gcol in range(4):
            wb, wt = act, act_t
            S.dma("pool", lambda e, gcol=gcol: e.dma_start(out=act[:, 0:16, :], in_=wmv[gcol]), wmv_ds, writes=[act_t])
            for mc in range(2):
                pv, pvt = cx.ps()
                for c in range(16):
                    S.op("pe", lambda e, c=c, wb=wb, pv=pv, mc=mc: e.matmul(pv[:, :], lhsT=mn[:, c, mc * 128:(mc + 1) * 128], rhs=wb[:, c, :],
                                                                            start=(c == 0), stop=(c == 15)),
                         reads=[wt, mn_t], writes=[pvt])
                S.op("act", lambda e, pv=pv, mc=mc, gcol=gcol: e.activation(out=proj[:, 4 + mc * 4 + gcol, :], in_=pv[:, :], func=AF.Copy),
                     reads=[pvt], writes=[mv_t])
        for mc in range(2):
            cx.store(o_mv[mc * 128:(mc + 1) * 128, :].rearrange("p (g f) -> p g f", f=512), proj[:, 4 + mc * 4:8 + mc * 4, :], [mv_t])

        for (t0, T) in TILES:
            S.dma("sp", lambda e, t0=t0, T=T: e.dma_start(out=x_sb[:, :, 0:T], in_=xT[:, t0:t0 + T].rearrange("(c p) t -> p c t", p=128)),
                  cs_ds, writes=[x_t])
            S.dma("sp", lambda e, t0=t0, T=T: e.dma_start(out=cos_sb[:, 0:T], in_=cosT[:, t0:t0 + T]), cos_ds, writes=[cos_t])
            S.dma("sp", lambda e, t0=t0, T=T: e.dma_start(out=sin_sb[:, 0:T], in_=sinT[:, t0:t0 + T]), sin_ds, writes=[sin_t])
            rmsnorm_chunks(x_sb, x_t, 16, T, g1_sb, g1_t, xn, xn_t)
            for j in range(NFF):
                wgb, wgt = cast_load(w_ring, wg[j])
                wub, wut = cast_load(w_ring, wu[j])
                pg, pgt = cx.ps()
                pu, put = cx.ps()
                for c in range(16):
                    S.op("pe", lambda e, c=c, wgb=wgb, pg=pg, T=T: e.matmul(pg[:, 0:T], lhsT=wgb[:, c, :], rhs=xn[:, c, 0:T], start=(c == 0), stop=(c == 15)),
                         reads=[wgt, xn_t], writes=[pgt])
                for c in range(16):
                    S.op("pe", lambda e, c=c, wub=wub, pu=pu, T=T: e.matmul(pu[:, 0:T], lhsT=wub[:, c, :], rhs=xn[:, c, 0:T], start=(c == 0), stop=(c == 15)),
                         reads=[wut, xn_t], writes=[put])
                sl, slt, _ = silu_ring.next()
                S.op("act", lambda e, sl=sl, pg=pg, T=T: e.activation(out=sl[:, 0:T], in_=pg[:, 0:T], func=AF.Silu), reads=[pgt], writes=[slt])
                S.op("dve", lambda e, sl=sl, pu=pu, j=j, T=T: e.tensor_tensor(out=act[:, j, 0:T], in0=sl[:, 0:T], in1=pu[:, 0:T], op=ALU.mult),
                     reads=[slt, put], writes=[act_t])
            for c in range(16):
                wdb, wdt = cast_load(wd_ring, wd[c])
                po, pot = cx.ps()
                for j in range(NFF):
                    S.op("pe", lambda e, j=j, wdb=wdb, po=po, T=T: e.matmul(po[:, 0:T], lhsT=wdb[:, j, :], rhs=act[:, j, 0:T], start=(j == 0), stop=(j == NFF - 1)),
                         reads=[wdt, act_t], writes=[pot])
                S.op("dve", lambda e, c=c, po=po, T=T: stt(e, x_sb[:, c, 0:T], po[:, 0:T], 0.5, x_sb[:, c, 0:T], ALU.mult, ALU.add),
                     reads=[pot, x_t], writes=[x_t])
            cx.store(o_h[:, t0:t0 + T].rearrange("(c p) t -> p c t", p=128), x_sb[:, :, 0:T], [x_t])
            rmsnorm_chunks(x_sb, x_t, 16, T, gmix_sb, gmix_t, xn, xn_t)
            for j in range(15):
                wb, wt = cast_load(w_ring, win[j])
                pp, ppt = cx.ps()
                for c in range(16):
                    S.op("pe", lambda e, c=c, wb=wb, pp=pp, T=T: e.matmul(pp[:, 0:T], lhsT=wb[:, c, :], rhs=xn[:, c, 0:T], start=(c == 0), stop=(c == 15)),
                         reads=[wt, xn_t], writes=[ppt])
                S.op("act", lambda e, j=j, pp=pp, T=T: e.activation(out=proj[:, j, 0:T], in_=pp[:, 0:T], func=AF.Copy), reads=[ppt], writes=[proj_t])
            cx.store(o_pin[:, t0:t0 + T].rearrange("(c p) t -> p c t", p=128), proj[:, 7:11, 0:T], [proj_t])
            rmsnorm_chunks(proj, proj_t, 4, T, gql_sb, gql_t, cq, cq_t, c0=0)
            rmsnorm_chunks(proj, proj_t, 2, T, gkvl_sb, gkvl_t, ckv_sb, ckv_t, c0=4)
            cx.store(o_ckv[:, t0:t0 + T].rearrange("(c p) t -> p c t", p=128), ckv_sb[:, :, 0:T], [ckv_t])
            sqf, sqft, _ = sqf_ring.next()
            pn, pnt = cx.ps(6, 8)
            S.op("act", lambda e, T=T: e.activation(out=sqf[0:64, 0:T], in_=proj[0:64, 6, 0:T], func=AF.Square), reads=[proj_t], writes=[sqft])
            S.op("pe", lambda e, T=T, pn=pn: e.matmul(pn[0:64, 0:T], lhsT=onesf[0:64, 0:64], rhs=sqf[0:64, 0:T], start=True, stop=True),
                 reads=[sqft, onesf_t], writes=[pnt])
            S.op("act", lambda e, T=T, pn=pn: e.activation(out=rstd[0:64, 0:T], in_=pn[0:64, 0:T], func=AF.Ln, bias=EPS, scale=1.0 / 64),
                 reads=[pnt], writes=[rstd_t])
            S.op("act", lambda e, T=T: e.activation(out=rstd[0:64, 0:T], in_=rstd[0:64, 0:T], func=AF.Exp, scale=-0.5), reads=[rstd_t], writes=[rstd_t])
            S.op("dve", lambda e, T=T: stt(e, krn[:, 0:T], proj[0:64, 6, 0:T], gkr_sb[0:64, 0:1], rstd[0:64, 0:T]),
                 reads=[proj_t, gkr_t, rstd_t], writes=[krn_t])
            pr, prt = cx.ps()
            S.op("pe", lambda e, T=T, pr=pr: e.matmul(pr[0:64, 0:T], lhsT=rot_f[0:64, 0:64], rhs=krn[:, 0:T], start=True, stop=True),
                 reads=[krn_t, rot_f_t], writes=[prt])
            S.op("dve", lambda e, T=T, pr=pr: e.tensor_tensor(out=tmpf[0:64, 0:T], in0=pr[0:64, 0:T], in1=sin_sb[0:64, 0:T], op=ALU.mult),
                 reads=[prt, sin_t], writes=[tmpf_t])
            S.op("dve", lambda e, T=T: e.tensor_tensor(out=kro[:, 0:T], in0=krn[:, 0:T], in1=cos_sb[0:64, 0:T], op=ALU.mult),
                 reads=[krn_t, cos_t], writes=[kro_t])
            S.op("dve", lambda e, T=T: e.tensor_tensor(out=kro[:, 0:T], in0=kro[:, 0:T], in1=tmpf[0:64, 0:T], op=ALU.add),
                 reads=[kro_t, tmpf_t], writes=[kro_t])
            cx.store(o_kr[:, t0:t0 + T], kro[:, 0:T], [kro_t])
            for h in range(4):
                rms_stats(cx, [proj[:, 11 + h, 0:T]], [proj_t], T, 128.0, ones[:], ones_t, sq_ring, rstd, rstd_t)
                S.op("dve", lambda e, h=h, T=T: stt(e, qm_sb[:, h, 0:T], proj[:, 11 + h, 0:T], gmq_sb[:, 0:1], rstd[:, 0:T]),
                     reads=[proj_t, gmq_t, rstd_t], writes=[qm_t])
            cx.store(o_qm[:, t0:t0 + T].rearrange("(c p) t -> p c t", p=128), qm_sb[:, :, 0:T], [qm_t])
            for h in range(16):
                wb, wt = cast_load(wq_ring, wuq[h])
                pq, pqt = cx.ps()
                for c in range(4):
                    S.op("pe", lambda e, c=c, wb=wb, pq=pq, T=T: e.matmul(pq[:, 0:T], lhsT=wb[:, c, :], rhs=cq[:, c, 0:T], start=(c == 0), stop=(c == 3)),
                         reads=[wt, cq_t], writes=[pqt])
                rms_stats(cx, [pq[:, 0:T]], [pqt], T, 128.0, ones[:], ones_t, sq_ring, rstd, rstd_t)
                qo, qot, _ = qo_ring.next()
                S.op("dve", lambda e, pq=pq, qo=qo, T=T: stt(e, qo[:, 0:T], pq[:, 0:T], gq_sb[:, 0:1], rstd[:, 0:T]),
                     reads=[pqt, gq_t, rstd_t], writes=[qot])
                cx.store(o_qn[h * 128:(h + 1) * 128, t0:t0 + T], qo[:, 0:T], [qot])
            for hp in range(8):
                wb, wt = cast_load(wq_ring, wuqr[hp])
                pq, pqt = cx.ps()
                for c in range(4):
                    S.op("pe", lambda e, c=c, wb=wb, pq=pq, T=T: e.matmul(pq[:, 0:T], lhsT=wb[:, c, :], rhs=cq[:, c, 0:T], start=(c == 0), stop=(c == 3)),
                         reads=[wt, cq_t], writes=[pqt])
                rms_stats(cx, [pq[:, 0:T]], [pqt], T, 64.0, ones2[:], ones2_t, sq_ring, rstd, rstd_t)
                qrn, qrnt, _ = qrn_ring.next()
                S.op("dve", lambda e, pq=pq, qrn=qrn, T=T: stt(e, qrn[:, 0:T], pq[:, 0:T], gqr_sb[:, 0:1], rstd[:, 0:T]),
                     reads=[pqt, gqr_t, rstd_t], writes=[qrnt])
                pr, prt = cx.ps()
                S.op("pe", lambda e, pr=pr, qrn=qrn, T=T: e.matmul(pr[:, 0:T], lhsT=rot_bf[:], rhs=qrn[:, 0:T], start=True, stop=True),
                     reads=[qrnt, rot_bf_t], writes=[prt])
                S.op("dve", lambda e, pr=pr, T=T: e.tensor_tensor(out=tmpf[:, 0:T], in0=pr[:, 0:T], in1=sin_sb[:, 0:T], op=ALU.mult),
                     reads=[prt, sin_t], writes=[tmpf_t])
                qo, qot, _ = qo_ring.next()
                S.op("dve", lambda e, qrn=qrn, qo=qo, T=T: e.tensor_tensor(out=qo[:, 0:T], in0=qrn[:, 0:T], in1=cos_sb[:, 0:T], op=ALU.mult),
                     reads=[qrnt, cos_t], writes=[qot])
                S.op("dve", lambda e, qo=qo, T=T: e.tensor_tensor(out=qo[:, 0:T], in0=qo[:, 0:T], in1=tmpf[:, 0:T], op=ALU.add),
                     reads=[qot, tmpf_t], writes=[qot])
                cx.store(o_qr[hp * 128:(hp + 1) * 128, t0:t0 + T], qo[:, 0:T], [qot])
        print("A sbuf remaining", nc.sbuf_bytes_remaining)
        cx.finish()
    return nc


def _tok_index(i):
    return np.concatenate([np.arange(512 * (8 * s + i), 512 * (8 * s + i) + 512) for s in range(4)])


def _positions(i):
    return np.concatenate([_tok_index(i), PAST + np.arange(DEC_T), PAST + np.arange(DEC_T)]).astype(np.int64)


def _rope_tables(i):
    half = 32
    inv = np.power(np.float32(10000.0), -np.arange(half, dtype=np.float32) / np.float32(half)).astype(np.float32)
    pos = _positions(i).astype(np.float32)
    ang = (pos[:, None] * inv[None, :]).astype(np.float32)
    cos = np.cos(ang).astype(np.float32).T
    sin = np.sin(ang).astype(np.float32).T
    return np.ascontiguousarray(np.tile(cos, (4, 1))), np.ascontiguousarray(np.tile(sin, (4, 1)))


def _rot_matrix():
    r = np.zeros((128, 128), np.float32)
    for b in range(2):
        for m in range(32):
            r[b * 64 + m + 32, b * 64 + m] = -1.0
        for m in range(32, 64):
            r[b * 64 + m - 32, b * 64 + m] = 1.0
    return r


def _slab(w, kc, nj, f=128):
    return np.ascontiguousarray(w.reshape(kc, 128, nj, f).transpose(2, 1, 0, 3))


def _gain(g, c):
    return np.ascontiguousarray(g.reshape(c, 128).T)


WIN_STARTS = [0, 128, 256, 384, 512, 640, 768, 832, 960, 1088, 1216, 1344, 1472, 1600, 1728]
WIN_WIDTHS = [128] * 6 + [64] + [128] * 8


def _prep_A_weights(inp, l):
    w = {}
    w["g1"] = _gain(inp["ffn1_norm"][l], 16)
    w["wg"] = _slab(inp["ffn1_wg"][l], 16, NFF)
    w["wu"] = _slab(inp["ffn1_wu"][l], 16, NFF)
    w["wd"] = _slab(inp["ffn1_wd"][l], NFF, 16)
    w["gmix"] = _gain(inp["mix_norm"][l], 16)
    win = inp["w_in"][l]
    wp = np.zeros((D, 15 * 128), np.float32)
    for j, (s0, wd_) in enumerate(zip(WIN_STARTS, WIN_WIDTHS)):
        wp[:, j * 128:j * 128 + wd_] = win[:, s0:s0 + wd_]
    w["win"] = _slab(wp, 16, 15)
    w["gql"] = _gain(inp["q_lat_norm"][l], 4)
    w["gkvl"] = _gain(inp["kv_lat_norm"][l], 2)
    w["gq"] = np.ascontiguousarray(inp["q_norm"][l][:, None])
    w["gqr"] = np.ascontiguousarray(np.concatenate([inp["qr_norm"][l]] * 2)[:, None])
    w["gkr"] = np.ascontiguousarray(np.concatenate([inp["kr_norm"][l]] * 2)[:, None])
    w["gmq"] = np.ascontiguousarray(inp["mq_norm"][l][:, None])
    w["wuq"] = _slab(inp["w_uq"][l], 4, 16)
    w["wuqr"] = _slab(inp["w_uqr"][l], 4, 8)
    w["rotT"] = _rot_matrix()
    w["memT"] = np.ascontiguousarray(inp["mem_prompt"][0].T)
    w["gmem"] = _gain(inp["mem_norm"][l], 16)
    w["wmk"] = _slab(inp["w_mk"][l], 16, 4)
    w["gmk"] = np.ascontiguousarray(inp["mk_norm"][l][:, None])
    w["wmv"] = _slab(inp["w_mv"][l], 16, 4, 512)
    return w


def _initial_xT(inp, i):
    xp = inp["x_prompt"][0][_tok_index(i)]
    xs = inp["x_sample"][2 * i:2 * i + 2].reshape(2 * DEC_T, D)
    return np.ascontiguousarray(np.concatenate([xp, xs], 0).T)


SCALE = float((128 + 64) ** -0.5)
S_KEYS = PAST + DEC_T


def rms_stats_ln(cx, srcs, src_tiles, T, Dn, ones_ap, ones_tile, sq_ring, rstd, rstd_tile, npart=128):
    S = cx.S
    pn, pnt = cx.ps(6, 8)
    n = len(srcs)
    for c, (src, st) in enumerate(zip(srcs, src_tiles)):
        sq, sqt, _ = sq_ring.next()
        S.op("act", lambda e, sq=sq, src=src: e.activation(out=sq[0:npart, 0:T], in_=src, func=AF.Square),
             reads=[st], writes=[sqt])
        S.op("pe", lambda e, sq=sq, c=c: e.matmul(pn[0:npart, 0:T], lhsT=ones_ap, rhs=sq[0:npart, 0:T],
                                                  start=(c == 0), stop=(c == n - 1)),
             reads=[sqt, ones_tile], writes=[pnt])
    S.op("act", lambda e: e.activation(out=rstd[0:npart, 0:T], in_=pn[0:npart, 0:T], func=AF.Ln, bias=EPS, scale=1.0 / Dn),
         reads=[pnt], writes=[rstd_tile])
    S.op("act", lambda e: e.activation(out=rstd[0:npart, 0:T], in_=rstd[0:npart, 0:T], func=AF.Exp, scale=-0.5),
         reads=[rstd_tile], writes=[rstd_tile])


def build_B1():
    nc = bass.Bass("TRN2", target_bir_lowering=False)
    with ExitStack() as es:
        cx = Ctx(nc, es)
        S = cx.S
        qnT = cx.inp("qnT", [D, NTOK], BF16)
        qrT = cx.inp("qrT", [1024, NTOK], BF16)
        ckvA = cx.inp("ckvA", [256, SEQ])
        krA = cx.inp("krA", [64, SEQ])
        ckvS = cx.inp("ckvS", [2, 256, S_KEYS])
        krS = cx.inp("krS", [2, 64, S_KEYS])
        maskd = cx.inp("mask", [128, 32, 512])
        wuk = cx.inp("wuk", [16, 128, 2, 128])
        wuv = cx.inp("wuv", [16, 128, 2, 128])
        gk = cx.inp("gk", [128, 1])
        o_at = cx.outp("o_at", [D, NTOK], BF16)

        gk_sb, gk_t = cx.load_const(gk, [128, 1], F32)
        mask_sb, mask_t = cx.load_const(maskd, [128, 32, 512], BF16)
        krA_sb, krA_t = cx.load_const(krA, [64, SEQ], BF16)
        krS_sb = []
        for b in range(2):
            krS_sb.append(cx.load_const(krS[b], [64, S_KEYS], BF16))
        ones = cx.sb([128, 128], BF16); ones_t = Tile()
        S.op("dve", lambda e: e.memset(ones[:], 1.0), writes=[ones_t])
        KT = cx.sb([128, SEQ], BF16); KT_t = Tile()
        V = cx.sb([128, 128, 128], BF16); V_t = Tile()
        rstd = cx.sb([128, 512], F32); rstd_t = Tile()
        rinv = cx.sb([128, 512], F32); rinv_t = Tile()
        sq_ring = cx.ring([128, 512], BF16, 2)
        ck_ring = cx.ring([128, 2, 512], BF16, 3)
        p_ring = cx.ring([128, 512], BF16, 4)
        rstd_ring = cx.ring([128, 512], F32, 3)
        ao_ring = cx.ring([128, 512], BF16, 2)
        qn_ring = cx.ring([128, NTOK], BF16, 2)
        qr_ring = cx.ring([64, NTOK], BF16, 2)
        wk_ring = cx.ring([128, 2, 128], BF16, 2)
        wv_ring = cx.ring([128, 2, 128], BF16, 2)
        po, pot = cx.psb[4], cx.pst[4]
        psm, psmt = cx.psb[5], cx.pst[5]

        def build_kv(h, src_ap, nkeys, wkb, wkt, wvb, wvt):
            k0 = 0
            while k0 < nkeys:
                kl = min(512, nkeys - k0)
                ck, ckt, ds = ck_ring.next()
                S.dma("pool", lambda e, ck=ck, k0=k0, kl=kl: e.dma_start(out=ck[:, :, 0:kl], in_=src_ap[:, k0:k0 + kl].rearrange("(c p) k -> p c k", p=128)),
                      ds, writes=[ckt])
                pk, pkt = cx.ps(0, 4)
                rs, rst, _ = rstd_ring.next()
                for c in range(2):
                    S.op("pe", lambda e, c=c, ck=ck, pk=pk, kl=kl: e.matmul(pk[:, 0:kl], lhsT=wkb[:, c, :], rhs=ck[:, c, 0:kl], start=(c == 0), stop=(c == 1)),
                         reads=[wkt, ckt], writes=[pkt])
                pv, pvt = cx.ps(0, 4)
                nsub = (kl + 127) // 128
                kkmax = min(128, kl)
                for sub in range(nsub):
                    kk = min(128, kl - sub * 128)
                    for c in range(2):
                        S.op("pe", lambda e, c=c, ck=ck, pv=pv, sub=sub, kk=kk: e.matmul(pv[0:kk, sub * 128:(sub + 1) * 128], lhsT=ck[:, c, sub * 128:sub * 128 + kk],
                                                                                        rhs=wvb[:, c, :], start=(c == 0), stop=(c == 1)),
                             reads=[wvt, ckt], writes=[pvt])
                rms_stats_ln(cx, [pk[:, 0:kl]], [pkt], kl, 128.0, ones[:], ones_t, sq_ring, rs, rst)
                kt0 = k0 // 128
                S.op("act", lambda e, pv=pv, kt0=kt0, nsub=nsub, kkmax=kkmax: e.activation(
                    out=V[0:kkmax, kt0:kt0 + nsub, :], in_=pv[0:kkmax, 0:nsub * 128].rearrange("p (a b) -> p a b", b=128), func=AF.Copy),
                     reads=[pvt], writes=[V_t])
                S.op("dve", lambda e, pk=pk, k0=k0, kl=kl, rs=rs: stt(e, KT[:, k0:k0 + kl], pk[:, 0:kl], gk_sb[:, 0:1], rs[:, 0:kl]),
                     reads=[pkt, gk_t, rst], writes=[KT_t])
                k0 += kl

        def attend(h, qn, qnt, qr, qrt, kr_sb, kr_t, q_off, Tq, tiles, mask_lo):
            n = len(tiles)
            LOOK = 3
            staged = []

            def stage1(idx):
                kt, kk = tiles[idx]
                pss, psst = cx.ps(0, 4)
                S.op("pe", lambda e, pss=pss, kt=kt, kk=kk: e.matmul(pss[0:kk, 0:Tq], lhsT=KT[:, kt * 128:kt * 128 + kk], rhs=qn[:, q_off:q_off + Tq], start=True, stop=False),
                     reads=[KT_t, qnt], writes=[psst])
                S.op("pe", lambda e, pss=pss, kt=kt, kk=kk: e.matmul(pss[0:kk, 0:Tq], lhsT=kr_sb[0:64, kt * 128:kt * 128 + kk], rhs=qr[0:64, q_off:q_off + Tq], start=False, stop=True),
                     reads=[kr_t, qrt], writes=[psst])
                pT, pTt, _ = p_ring.next()
                S.op("act", lambda e, pss=pss, pT=pT, kk=kk: e.activation(out=pT[0:kk, 0:Tq], in_=pss[0:kk, 0:Tq], func=AF.Exp, scale=SCALE),
                     reads=[psst], writes=[pTt])
                if mask_lo is not None and kt >= mask_lo:
                    m = kt - mask_lo
                    S.op("dve", lambda e, pT=pT, m=m: e.tensor_tensor(out=pT[:, 0:Tq], in0=pT[:, 0:Tq], in1=mask_sb[:, m, 0:Tq], op=ALU.mult),
                         reads=[pTt, mask_t], writes=[pTt])
                staged.append((pT, pTt))

            def stage2(idx):
                kt, kk = tiles[idx]
                pT, pTt = staged[idx]
                S.op("pe", lambda e, pT=pT, kt=kt, kk=kk, idx=idx: e.matmul(po[:, 0:Tq], lhsT=V[0:kk, kt, :], rhs=pT[0:kk, 0:Tq], start=(idx == 0), stop=(idx == n - 1)),
                     reads=[V_t, pTt], writes=[pot])
                S.op("pe", lambda e, pT=pT, kk=kk, idx=idx: e.matmul(psm[:, 0:Tq], lhsT=ones[0:kk, :], rhs=pT[0:kk, 0:Tq], start=(idx == 0), stop=(idx == n - 1)),
                     reads=[ones_t, pTt], writes=[psmt])

            for idx in range(n):
                stage1(idx)
                if idx >= LOOK:
                    stage2(idx - LOOK)
            for idx in range(max(0, n - LOOK), n):
                stage2(idx)
            S.op("dve", lambda e: e.reciprocal(out=rinv[:, 0:Tq], in_=psm[:, 0:Tq]), reads=[psmt], writes=[rinv_t])
            ao, aot, _ = ao_ring.next()
            S.op("dve", lambda e, ao=ao: e.tensor_tensor(out=ao[:, 0:Tq], in0=po[:, 0:Tq], in1=rinv[:, 0:Tq], op=ALU.mult),
                 reads=[pot, rinv_t], writes=[aot])
            cx.store(o_at[h * 128:(h + 1) * 128, q_off:q_off + Tq], ao[:, 0:Tq], [aot])

        for h in range(16):
            wkb, wkt, ds = wk_ring.next()
            S.dma("pool", lambda e, wkb=wkb, h=h: e.dma_start(out=wkb[:], in_=wuk[h]), ds, writes=[wkt])
            wvb, wvt, ds = wv_ring.next()
            S.dma("pool", lambda e, wvb=wvb, h=h: e.dma_start(out=wvb[:], in_=wuv[h]), ds, writes=[wvt])
            qn, qnt, ds = qn_ring.next()
            S.dma("sp", lambda e, qn=qn, h=h: e.dma_start(out=qn[:], in_=qnT[h * 128:(h + 1) * 128, :]), ds, writes=[qnt])
            qr, qrt, ds = qr_ring.next()
            S.dma("sp", lambda e, qr=qr, h=h: e.dma_start(out=qr[:], in_=qrT[h * 64:(h + 1) * 64, :]), ds, writes=[qrt])
            build_kv(h, ckvA, SEQ, wkb, wkt, wvb, wvt)
            for s in range(4):
                attend(h, qn, qnt, qr, qrt, krA_sb, krA_t, 512 * s, 512, [(kt, 128) for kt in range(32 * (s + 1))], 32 * s)
            for b in range(2):
                build_kv(h, ckvS[b], S_KEYS, wkb, wkt, wvb, wvt)
                tiles = [(kt, 128) for kt in range(32)] + [(32, 64)]
                attend(h, qn, qnt, qr, qrt, krS_sb[b][0], krS_sb[b][1], 2048 + 64 * b, 64, tiles, None)
        print("B1 sbuf remaining", nc.sbuf_bytes_remaining)
        cx.finish()
    return nc


def _mask_for_core(i):
    p = np.arange(128)[:, None, None]
    m = np.arange(32)[None, :, None]
    f = np.arange(512)[None, None, :]
    return ((128 * m + p) // 64 <= (512 * i + f) // 64).astype(np.float32)


TILES_B = [(0, 512, 0, 0), (512, 512, 527, 0), (1024, 512, 1054, 0), (1536, 512, 1581, 0), (2048, 64, 2108, 1), (2112, 64, 2187, 2)]
PINE_W = 4 * 527 + 2 * 79
SCALE_M = float(128 ** -0.5)


def build_B2():
    nc = bass.Bass("TRN2", target_bir_lowering=False)
    with ExitStack() as es:
        cx = Ctx(nc, es)
        S = cx.S
        hT = cx.inp("hT", [D, NTOK])
        atT = cx.inp("atT", [D, NTOK], BF16)
        qmT = cx.inp("qmT", [512, NTOK], BF16)
        pinE = cx.inp("pinE", [512, PINE_W])
        icnt = cx.inp("icnt", [128, 4, NTOK])
        mk3 = cx.inp("mk3", [3, 512, 256])
        mv3 = cx.inp("mv3", [3, 256, D])
        gmix = cx.inp("gmix", [128, 16])
        poolw = cx.inp("poolw", [4, 128, 512])
        pscale = cx.inp("pscale", [128, 16])
        wgate = cx.inp("wgate", [48, 128, 16, 128])
        bgate = cx.inp("bgate", [128, 48])
        wout = cx.inp("wout", [16, 128, 16, 128])
        g2 = cx.inp("g2", [128, 16])
        wg = cx.inp("wg", [NFF, 128, 16, 128])
        wu = cx.inp("wu", [NFF, 128, 16, 128])
        wd = cx.inp("wd", [16, 128, NFF, 128])
        o_x = cx.outp("o_x", [D, NTOK])

        gmix_sb, gmix_t = cx.load_const(gmix, [128, 16], F32)
        g2_sb, g2_t = cx.load_const(g2, [128, 16], F32)
        pscale_sb, pscale_t = cx.load_const(pscale, [128, 16], F32)
        bgate_sb, bgate_t = cx.load_const(bgate, [128, 48], F32)
        poolw_sb, poolw_t = cx.load_const(poolw.rearrange("g p f -> p g f"), [128, 4, 512], BF16)
        ones = cx.sb([128, 128], BF16); ones_t = Tile()
        S.op("dve", lambda e: e.memset(ones[:], 1.0), writes=[ones_t])

        x_sb = cx.sb([128, 16, 512], F32); x_t = Tile()
        xn = cx.sb([128, 16, 512], BF16); xn_t = Tile()
        act = cx.sb([128, NFF, 512], BF16); act_t = Tile()
        at_sb = cx.sb([128, 16, 512], BF16); at_t = Tile()
        pe_sb = cx.sb([128, 4, 527], F32); pe_t = Tile()
        lvA = cx.sb([128, 527], F32); lvA_t = Tile()
        lvB = cx.sb([128, 527], F32); lvB_t = Tile()
        pooled = cx.sb([128, 4, 512], BF16); pooled_t = Tile()
        PmT = cx.sb([128, 4, 2, 512], BF16); PmT_t = Tile()
        mk_sb = cx.sb([128, 4, 256], BF16); mk_t = Tile()
        mv_sb = cx.sb([128, 2, D], BF16); mv_t = Tile()
        qm_sb = cx.sb([128, 4, 512], BF16); qm_t = Tile()
        rstd = cx.sb([128, 512], F32); rstd_t = Tile()
        rinv = cx.sb([128, 512], F32); rinv_t = Tile()
        mrg = cx.sb([128, 512], F32); mrg_t = Tile()
        tmp1 = cx.sb([128, 512], F32); tmp1_t = Tile()
        icnt_ring = cx.ring([128, 512], F32, 2)
        gate_ring = cx.ring([128, 512], F32, 3)
        sq_ring = cx.ring([128, 512], BF16, 2)
        silu_ring = cx.ring([128, 512], BF16, 2)
        w_ring = cx.ring([128, 16, 128], BF16, 3)
        wd_ring = cx.ring([128, NFF, 128], BF16, 2)
        ld_ds = S.dsem()
        at_ds = S.dsem()
        qm_ds = S.dsem()
        pe_ds = S.dsem()
        mem_ds = S.dsem()
        mv_ds = S.dsem()

        def cast_load(ring, src_ap):
            buf, tl, ds = ring.next()
            S.dma("pool", lambda e: e.dma_start(out=buf[:], in_=src_ap), ds, writes=[tl])
            return buf, tl

        def rmsnorm16(T, g_sb, g_t):
            rms_stats(cx, [x_sb[:, c, 0:T] for c in range(16)], [x_t] * 16, T, float(D), ones[:], ones_t, sq_ring, rstd, rstd_t)
            for c in range(16):
                S.op("dve", lambda e, c=c: stt(e, xn[:, c, 0:T], x_sb[:, c, 0:T], g_sb[:, c:c + 1], rstd[:, 0:T]),
                     reads=[x_t, g_t, rstd_t], writes=[xn_t])

        for (t0, T, seg, ms) in TILES_B:
            L = 15 + T
            S.dma("sp", lambda e, t0=t0, T=T: e.dma_start(out=x_sb[:, :, 0:T], in_=hT[:, t0:t0 + T].rearrange("(c p) t -> p c t", p=128)), ld_ds, writes=[x_t])
            S.dma("sp", lambda e, t0=t0, T=T: e.dma_start(out=at_sb[:, :, 0:T], in_=atT[:, t0:t0 + T].rearrange("(c p) t -> p c t", p=128)), at_ds, writes=[at_t])
            S.dma("sp", lambda e, t0=t0, T=T: e.dma_start(out=qm_sb[:, :, 0:T], in_=qmT[:, t0:t0 + T].rearrange("(c p) t -> p c t", p=128)), qm_ds, writes=[qm_t])
            S.dma("sp", lambda e, seg=seg, L=L: e.dma_start(out=pe_sb[:, :, 0:L], in_=pinE[:, seg:seg + L].rearrange("(g p) t -> p g t", p=128)), pe_ds, writes=[pe_t])
            S.dma("pool", lambda e, ms=ms: e.dma_start(out=mk_sb[:], in_=mk3[ms].rearrange("(h p) m -> p h m", p=128)), mem_ds, writes=[mk_t])
            S.dma("pool", lambda e, ms=ms: e.dma_start(out=mv_sb[:], in_=mv3[ms].rearrange("(c p) f -> p c f", p=128)), mv_ds, writes=[mv_t])
            rmsnorm16(T, gmix_sb, gmix_t)
            for g in range(4):
                e_ = lambda a, b, g=g: pe_sb[:, g, a:b]
                S.op("pool", lambda e, g=g, L=L: e.tensor_tensor(out=lvA[:, 1:L], in0=pe_sb[:, g, 1:L], in1=pe_sb[:, g, 0:L - 1], op=ALU.add),
                     reads=[pe_t], writes=[lvA_t])
                cur, cur_t = lvA, lvA_t
                if g >= 1:
                    S.op("pool", lambda e, L=L: e.tensor_tensor(out=lvB[:, 3:L], in0=lvA[:, 3:L], in1=lvA[:, 1:L - 2], op=ALU.add),
                         reads=[lvA_t], writes=[lvB_t])
                    cur, cur_t = lvB, lvB_t
                if g >= 2:
                    S.op("pool", lambda e, L=L: e.tensor_tensor(out=lvA[:, 7:L], in0=lvB[:, 7:L], in1=lvB[:, 3:L - 4], op=ALU.add),
                         reads=[lvB_t, lvA_t], writes=[lvA_t])
                    cur, cur_t = lvA, lvA_t
                if g >= 3:
                    S.op("pool", lambda e, L=L: e.tensor_tensor(out=lvB[:, 15:L], in0=lvA[:, 15:L], in1=lvA[:, 7:L - 8], op=ALU.add),
                         reads=[lvA_t, lvB_t], writes=[lvB_t])
                    cur, cur_t = lvB, lvB_t
                ic, ict, ds = icnt_ring.next()
                S.dma("sp", lambda e, ic=ic, g=g, t0=t0, T=T: e.dma_start(out=ic[:, 0:T], in_=icnt[:, g, t0:t0 + T]), ds, writes=[ict])
                S.op("dve", lambda e, cur=cur, ic=ic, L=L, T=T: e.tensor_tensor(out=tmp1[:, 0:T], in0=cur[:, 15:L], in1=ic[:, 0:T], op=ALU.mult),
                     reads=[cur_t, ict], writes=[tmp1_t])
                S.op("dve", lambda e, g=g, L=L, T=T: e.tensor_tensor(out=pooled[:, g, 0:T], in0=tmp1[:, 0:T], in1=pe_sb[:, g, 15:L], op=ALU.subtract),
                     reads=[tmp1_t, pe_t], writes=[pooled_t])
            for hm in range(4):
                psm, psmt = cx.ps(6, 8)
                for mc in range(2):
                    pss, psst = cx.ps()
                    S.op("pe", lambda e, pss=pss, hm=hm, mc=mc, T=T: e.matmul(pss[:, 0:T], lhsT=mk_sb[:, hm, mc * 128:(mc + 1) * 128], rhs=qm_sb[:, hm, 0:T], start=True, stop=True),
                         reads=[mk_t, qm_t], writes=[psst])
                    S.op("act", lambda e, pss=pss, hm=hm, mc=mc, T=T: e.activation(out=PmT[:, hm, mc, 0:T], in_=pss[:, 0:T], func=AF.Exp, scale=SCALE_M),
                         reads=[psst], writes=[PmT_t])
                for mc in range(2):
                    S.op("pe", lambda e, psm=psm, hm=hm, mc=mc, T=T: e.matmul(psm[:, 0:T], lhsT=ones[:], rhs=PmT[:, hm, mc, 0:T], start=(mc == 0), stop=(mc == 1)),
                         reads=[ones_t, PmT_t], writes=[psmt])
                S.op("dve", lambda e, psm=psm, T=T: e.reciprocal(out=rinv[:, 0:T], in_=psm[:, 0:T]), reads=[psmt], writes=[rinv_t])
                for mc in range(2):
                    S.op("dve", lambda e, hm=hm, mc=mc, T=T: e.tensor_tensor(out=PmT[:, hm, mc, 0:T], in0=PmT[:, hm, mc, 0:T], in1=rinv[:, 0:T], op=ALU.mult),
                         reads=[PmT_t, rinv_t], writes=[PmT_t])
            for dc in range(16):
                g = dc // 4
                cc = dc % 4
                pp, ppt = cx.ps()
                S.op("pe", lambda e, pp=pp, g=g, cc=cc, T=T: e.matmul(pp[:, 0:T], lhsT=poolw_sb[:, g, cc * 128:(cc + 1) * 128], rhs=pooled[:, g, 0:T], start=True, stop=True),
                     reads=[poolw_t, pooled_t], writes=[ppt])
                pm, pmt = cx.ps()
                for mc in range(2):
                    S.op("pe", lambda e, pm=pm, dc=dc, g=g, mc=mc, T=T: e.matmul(pm[:, 0:T], lhsT=mv_sb[:, mc, dc * 128:(dc + 1) * 128], rhs=PmT[:, g, mc, 0:T], start=(mc == 0), stop=(mc == 1)),
                         reads=[mv_t, PmT_t], writes=[pmt])
                gts = []
                for br in range(3):
                    wb, wt = cast_load(w_ring, wgate[br * 16 + dc])
                    pg, pgt = cx.ps()
                    for c in range(16):
                        S.op("pe", lambda e, c=c, wb=wb, pg=pg, T=T: e.matmul(pg[:, 0:T], lhsT=wb[:, c, :], rhs=xn[:, c, 0:T], start=(c == 0), stop=(c == 15)),
                             reads=[wt, xn_t], writes=[pgt])
                    gt, gtt, _ = gate_ring.next()
                    S.op("act", lambda e, pg=pg, gt=gt, br=br, dc=dc, T=T: e.activation(out=gt[:, 0:T], in_=pg[:, 0:T], func=AF.Sigmoid,
                                                                                       bias=bgate_sb[:, br * 16 + dc:br * 16 + dc + 1]),
                         reads=[pgt, bgate_t], writes=[gtt])
                    gts.append((gt, gtt))
                S.op("dve", lambda e, dc=dc, gt=gts[0][0], T=T: e.tensor_tensor(out=mrg[:, 0:T], in0=at_sb[:, dc, 0:T], in1=gt[:, 0:T], op=ALU.mult),
                     reads=[at_t, gts[0][1]], writes=[mrg_t])
                S.op("dve", lambda e, dc=dc, pp=pp, gt=gts[1][0], T=T: stt(e, tmp1[:, 0:T], pp[:, 0:T], pscale_sb[:, dc:dc + 1], gt[:, 0:T]),
                     reads=[ppt, pscale_t, gts[1][1]], writes=[tmp1_t])
                S.op("dve", lambda e, T=T: e.tensor_tensor(out=mrg[:, 0:T], in0=mrg[:, 0:T], in1=tmp1[:, 0:T], op=ALU.add),
                     reads=[mrg_t, tmp1_t], writes=[mrg_t])
                S.op("dve", lambda e, pm=pm, gt=gts[2][0], T=T: e.tensor_tensor(out=tmp1[:, 0:T], in0=pm[:, 0:T], in1=gt[:, 0:T], op=ALU.mult),
                     reads=[pmt, gts[2][1]], writes=[tmp1_t])
                S.op("dve", lambda e, dc=dc, T=T: e.tensor_tensor(out=at_sb[:, dc, 0:T], in0=mrg[:, 0:T], in1=tmp1[:, 0:T], op=ALU.add),
                     reads=[mrg_t, tmp1_t], writes=[at_t])
            for dco in range(16):
                wb, wt = cast_load(w_ring, wout[dco])
                pw, pwt = cx.ps()
                for c in range(16):
                    S.op("pe", lambda e, c=c, wb=wb, pw=pw, T=T: e.matmul(pw[:, 0:T], lhsT=wb[:, c, :], rhs=at_sb[:, c, 0:T], start=(c == 0), stop=(c == 15)),
                         reads=[wt, at_t], writes=[pwt])
                S.op("dve", lambda e, dco=dco, pw=pw, T=T: e.tensor_tensor(out=x_sb[:, dco, 0:T], in0=x_sb[:, dco, 0:T], in1=pw[:, 0:T], op=ALU.add),
                     reads=[pwt, x_t], writes=[x_t])
            rmsnorm16(T, g2_sb, g2_t)
            for j in range(NFF):
                wgb, wgt = cast_load(w_ring, wg[j])
                wub, wut = cast_load(w_ring, wu[j])
                pg, pgt = cx.ps()
                pu, put = cx.ps()
                for c in range(16):
                    S.op("pe", lambda e, c=c, wgb=wgb, pg=pg, T=T: e.matmul(pg[:, 0:T], lhsT=wgb[:, c, :], rhs=xn[:, c, 0:T], start=(c == 0), stop=(c == 15)),
                         reads=[wgt, xn_t], writes=[pgt])
                for c in range(16):
                    S.op("pe", lambda e, c=c, wub=wub, pu=pu, T=T: e.matmul(pu[:, 0:T], lhsT=wub[:, c, :], rhs=xn[:, c, 0:T], start=(c == 0), stop=(c == 15)),
                         reads=[wut, xn_t], writes=[put])
                sl, slt, _ = silu_ring.next()
                S.op("act", lambda e, sl=sl, pg=pg, T=T: e.activation(out=sl[:, 0:T], in_=pg[:, 0:T], func=AF.Silu), reads=[pgt], writes=[slt])
                S.op("dve", lambda e, sl=sl, pu=pu, j=j, T=T: e.tensor_tensor(out=act[:, j, 0:T], in0=sl[:, 0:T], in1=pu[:, 0:T], op=ALU.mult),
                     reads=[slt, put], writes=[act_t])
            for c in range(16):
                wdb, wdt = cast_load(wd_ring, wd[c])
                po, pot = cx.ps()
                for j in range(NFF):
                    S.op("pe", lambda e, j=j, wdb=wdb, po=po, T=T: e.matmul(po[:, 0:T], lhsT=wdb[:, j, :], rhs=act[:, j, 0:T], start=(j == 0), stop=(j == NFF - 1)),
                         reads=[wdt, act_t], writes=[pot])
                S.op("dve", lambda e, c=c, po=po, T=T: stt(e, x_sb[:, c, 0:T], po[:, 0:T], 0.5, x_sb[:, c, 0:T], ALU.mult, ALU.add),
                     reads=[pot, x_t], writes=[x_t])
            cx.store(o_x[:, t0:t0 + T].rearrange("(c p) t -> p c t", p=128), x_sb[:, :, 0:T], [x_t])
        print("B2 sbuf remaining", nc.sbuf_bytes_remaining)
        cx.finish()
    return nc


_PROGS = {}


def _prog(name, fn):
    if name not in _PROGS:
        _PROGS[name] = fn()
    return _PROGS[name]


def _run(nc, in_maps):
    res = run_bass_kernel_spmd(nc, in_maps, core_ids=list(range(NCORES)))
    return [{k: np.asarray(v) for k, v in r.items()} for r in res.results]


def _icnt(i):
    pos = _positions(i)
    out = np.empty((4, NTOK), np.float32)
    for g, w in enumerate((2, 4, 8, 16)):
        out[g] = np.float32(1.0) / np.minimum(pos + 1, w).astype(np.float32)
    return np.ascontiguousarray(np.broadcast_to(out[None], (128, 4, NTOK)))


def kernel(**inp):
    inp = {k: np.asarray(v) for k, v in inp.items()}
    ncA = _prog("A", build_A)
    ncB1 = _prog("B1", build_B1)
    ncB2 = _prog("B2", build_B2)
    xT = [_initial_xT(inp, i) for i in range(NCORES)]
    tabs = [_rope_tables(i) for i in range(NCORES)]
    masks = [_mask_for_core(i) for i in range(NCORES)]
    icnts = [_icnt(i) for i in range(NCORES)]
    L = 2
    o_ckv_p = np.zeros((L, 1, SEQ, 256), np.float32)
    o_kr_p = np.zeros((L, 1, SEQ, 64), np.float32)
    o_pool_p = np.zeros((L, 1, 15, 512), np.float32)
    o_mk_p = np.zeros((L, 1, 256, 4, 128), np.float32)
    o_mv_p = np.zeros((L, 1, 256, 4, 512), np.float32)
    o_ckv_s = np.zeros((L, DEC_B, DEC_T, 256), np.float32)
    o_kr_s = np.zeros((L, DEC_B, DEC_T, 64), np.float32)
    o_pool_s = np.zeros((L, DEC_B, 15, 512), np.float32)
    for l in range(L):
        wA = _prep_A_weights(inp, l)
        maps = []
        for i in range(NCORES):
            m = dict(wA)
            m["xT"] = xT[i]
            m["cosT"], m["sinT"] = tabs[i]
            maps.append(m)
        rA = _run(ncA, maps)
        del maps, wA
        ckv_all = np.zeros((256, SEQ), np.float32)
        kr_all = np.zeros((64, SEQ), np.float32)
        pin_all = np.zeros((512, SEQ), np.float32)
        for i in range(NCORES):
            for s in range(4):
                g0 = 512 * (8 * s + i)
                ckv_all[:, g0:g0 + 512] = rA[i]["o_ckv"][:, 512 * s:512 * s + 512]
                kr_all[:, g0:g0 + 512] = rA[i]["o_kr"][:, 512 * s:512 * s + 512]
                pin_all[:, g0:g0 + 512] = rA[i]["o_pin"][:, 512 * s:512 * s + 512]
        o_ckv_p[l, 0] = ckv_all.T
        o_kr_p[l, 0] = kr_all.T
        o_pool_p[l, 0] = pin_all[:, SEQ - 15:].T
        o_mk_p[l, 0] = rA[0]["o_mk"].T.reshape(256, 4, 128)
        o_mv_p[l, 0] = rA[0]["o_mv"].reshape(256, 4, 512)
        for b in range(DEC_B):
            i, bb = b // 2, b % 2
            c0 = 2048 + 64 * bb
            o_ckv_s[l, b] = rA[i]["o_ckv"][:, c0:c0 + 64].T
            o_kr_s[l, b] = rA[i]["o_kr"][:, c0:c0 + 64].T
            o_pool_s[l, b] = rA[i]["o_pin"][:, c0 + 64 - 15:c0 + 64].T
        wuk = _slab(inp["w_uk"][l], 2, 16)
        wuv = _slab(inp["w_uv"][l], 2, 16)
        gk = np.ascontiguousarray(inp["k_norm"][l][:, None])
        maps = []
        for i in range(NCORES):
            m = {"qnT": rA[i]["o_qn"], "qrT": rA[i]["o_qr"], "ckvA": ckv_all, "krA": kr_all, "mask": masks[i],
                 "wuk": wuk, "wuv": wuv, "gk": gk}
            cs, ks = [], []
            for bb in range(2):
                b = 2 * i + bb
                c0 = 2048 + 64 * bb
                cs.append(np.concatenate([inp["cache_ckv"][l, b].T, rA[i]["o_ckv"][:, c0:c0 + 64]], axis=1))
                ks.append(np.concatenate([inp["cache_krope"][l, b].T, rA[i]["o_kr"][:, c0:c0 + 64]], axis=1))
            m["ckvS"] = np.ascontiguousarray(np.stack(cs))
            m["krS"] = np.ascontiguousarray(np.stack(ks))
            maps.append(m)
        rB1 = _run(ncB1, maps)
        del maps
        wB = {
            "gmix": _gain(inp["mix_norm"][l], 16),
            "poolw": np.ascontiguousarray(inp["pool_w"][l]),
            "pscale": _gain(inp["pool_scale"][l], 16),
            "wgate": _slab(inp["w_gate"][l], 16, 48),
            "bgate": _gain(inp["b_gate"][l], 48),
            "wout": _slab(inp["w_out"][l], 16, 16),
            "g2": _gain(inp["ffn2_norm"][l], 16),
            "wg": _slab(inp["ffn2_wg"][l], 16, NFF),
            "wu": _slab(inp["ffn2_wu"][l], 16, NFF),
            "wd": _slab(inp["ffn2_wd"][l], NFF, 16),
        }
        maps = []
        for i in range(NCORES):
            m = dict(wB)
            m["hT"] = rA[i]["o_h"]
            m["atT"] = rB1[i]["o_at"]
            m["qmT"] = rA[i]["o_qm"]
            segs = []
            for s in range(4):
                g0 = 512 * (8 * s + i)
                halo = pin_all[:, g0 - 15:g0] if g0 > 0 else np.zeros((512, 15), np.float32)
                segs.append(np.concatenate([halo, pin_all[:, g0:g0 + 512]], axis=1))
            mks = [rA[0]["o_mk"]]
            mvs = [rA[0]["o_mv"]]
            for bb in range(2):
                b = 2 * i + bb
                c0 = 2048 + 64 * bb
                segs.append(np.concatenate([inp["state_pool"][l, b].T, rA[i]["o_pin"][:, c0:c0 + 64]], axis=1))
                mks.append(inp["cache_mem_k"][l, b].reshape(256, 512).T)
                mvs.append(inp["cache_mem_v"][l, b].reshape(256, D))
            m["pinE"] = np.ascontiguousarray(np.concatenate(segs, axis=1))
            m["icnt"] = icnts[i]
            m["mk3"] = np.ascontiguousarray(np.stack(mks))
            m["mv3"] = np.ascontiguousarray(np.stack(mvs))
            maps.append(m)
        rB2 = _run(ncB2, maps)
        del maps, wB
        xT = [rB2[i]["o_x"] for i in range(NCORES)]
    y_p = np.zeros((1, SEQ, D), np.float32)
    y_s = np.zeros((DEC_B, DEC_T, D), np.float32)
    for i in range(NCORES):
        y_p[0, _tok_index(i)] = xT[i][:, 0:2048].T
        for bb in range(2):
            y_s[2 * i + bb] = xT[i][:, 2048 + 64 * bb:2048 + 64 * bb + 64].T
    return (y_p, y_s, o_ckv_p, o_kr_p, o_pool_p, o_mk_p, o_mv_p, o_ckv_s, o_kr_s, o_pool_s)
```
